# Optimizing a Trainium2 kernel written in Bass

```python
import jax, jax.numpy as jnp
from jax import lax
import numpy as np

D_MODEL = 1024
BATCH = 4
SEQ = 4096
DEPTH = 4

A_GROUPS = 8
A_GROUP_DIM = 64
A_WIDTH = A_GROUPS * A_GROUP_DIM
A_CHUNK = 128
B_HEADS = 8
B_HEAD_DIM = 64
B_WIDTH = B_HEADS * B_HEAD_DIM
B_BRANCHES = ((128, 1), (512, 4), (2048, 16))
B_BLOCK = 128
C_HEADS = 4
C_DK = 128
C_DV = 256
C_GATE_RANK = 16
C_TAU = 16.0
C_CHUNK = 64
D_FF = 2816
CONV_W = 3

EPS = 1e-6
NEG = -1e30
N_EVEN = (DEPTH + 1) // 2
N_ODD = DEPTH // 2
EVEN_IN = 2 * A_WIDTH + 3 * B_WIDTH
ODD_IN = 2 * C_HEADS * C_DK + 2 * C_HEADS * C_DV + C_GATE_RANK

kernel_name = "hybrid_gmlp_dilated_gla_convffn"


def rms_norm(x, g):
    xf = x.astype(jnp.float32)
    return xf * lax.rsqrt(jnp.mean(xf * xf, -1, keepdims=True) + EPS) * g.astype(jnp.float32)


def layer_norm(x, g, b):
    xf = x.astype(jnp.float32)
    mu = jnp.mean(xf, -1, keepdims=True)
    xc = xf - mu
    return xc * lax.rsqrt(jnp.mean(xc * xc, -1, keepdims=True) + EPS) * g + b


def causal_dwconv(h, w, b):
    S = h.shape[1]
    hp = jnp.pad(h, ((0, 0), (CONV_W - 1, 0), (0, 0)))
    out = b
    for k in range(CONV_W):
        out = out + hp[:, k:k + S] * w[k]
    return out


def chunked_gmlp(u, v, ln_g, ln_b, w_s, b_s):
    Bn, S, _ = u.shape
    nc = S // A_CHUNK
    vn = layer_norm(v, ln_g, ln_b).reshape(Bn, nc, A_CHUNK, A_GROUPS, A_GROUP_DIM)
    causal = jnp.tril(jnp.ones((A_CHUNK, A_CHUNK), dtype=bool))
    ws = jnp.where(causal[None], w_s.astype(jnp.float32), 0.0)
    mixed = jnp.einsum('gts,bnsgc->bntgc', ws, vn) + b_s.astype(jnp.float32).T[:, :, None]
    return u.astype(jnp.float32) * mixed.reshape(Bn, S, A_WIDTH)


def dilated_branch(q, k, v, window, dilation):
    Bn, H, S, hd = q.shape
    span = dilation * B_BLOCK
    Sp = -(-S // span) * span
    M = Sp // dilation
    nb = M // B_BLOCK
    n_back = window // dilation

    def to_blocks(t):
        t = jnp.pad(t, ((0, 0), (0, 0), (0, Sp - S), (0, 0)))
        t = t.reshape(Bn, H, M, dilation, hd)
        t = jnp.moveaxis(t, 3, 2)
        return t.reshape(Bn, H, dilation, nb, B_BLOCK, hd)

    def band(t):
        tp = jnp.pad(t, ((0, 0), (0, 0), (0, 0), (1, 0), (0, 0), (0, 0)))
        return jnp.concatenate([tp[:, :, :, :-1], tp[:, :, :, 1:]], axis=4)

    qb = to_blocks(q)
    kk = band(to_blocks(k))
    vv = band(to_blocks(v))

    i = jnp.arange(B_BLOCK)[:, None]
    j = jnp.arange(2 * B_BLOCK)[None, :]
    dist = B_BLOCK + i - j
    in_band = (dist >= 0) & (dist <= n_back)
    valid = in_band[None] & ((jnp.arange(nb) > 0)[:, None, None] | (j >= B_BLOCK)[None])

    s = jnp.einsum('bhrnic,bhrnjc->bhrnij', qb, kk) * (hd ** -0.5)
    s = jnp.where(valid, s, NEG)
    m = jnp.max(s, -1, keepdims=True)
    p = jnp.exp(s - m)
    den = jnp.sum(p, -1)
    o = jnp.einsum('bhrnij,bhrnjc->bhrnic', p, vv) / den[..., None]
    lse = m[..., 0] + jnp.log(den)

    def from_blocks(t):
        tail = t.shape[5:]
        t = t.reshape((Bn, H, dilation, M) + tail)
        t = jnp.moveaxis(t, 2, 3).reshape((Bn, H, Sp) + tail)
        return t[:, :, :S]

    return from_blocks(o), from_blocks(lse)


def even_mixer(h, w_in, a_ln_g, a_ln_b, a_ws, a_bs, q_g, k_g, w_out):
    Bn, S, _ = h.shape
    z = h @ w_in
    u, v, q, k, vb = jnp.split(
        z, [A_WIDTH, 2 * A_WIDTH, 2 * A_WIDTH + B_WIDTH, 2 * A_WIDTH + 2 * B_WIDTH], axis=-1)
    a_out = chunked_gmlp(jax.nn.gelu(u, approximate=False), jax.nn.gelu(v, approximate=False),
                         a_ln_g, a_ln_b, a_ws, a_bs)

    def heads(t):
        return t.reshape(Bn, S, B_HEADS, B_HEAD_DIM).transpose(0, 2, 1, 3)

    qh = rms_norm(heads(q), q_g)
    kh = rms_norm(heads(k), k_g)
    vh = heads(vb).astype(jnp.float32)
    outs, lses = [], []
    for window, dilation in B_BRANCHES:
        o_r, l_r = dilated_branch(qh, kh, vh, window, dilation)
        outs.append(o_r)
        lses.append(l_r)
    wts = jax.nn.softmax(jnp.stack(lses), axis=0)
    o = jnp.einsum('rbhs,rbhsc->bhsc', wts, jnp.stack(outs))
    b_out = o.transpose(0, 2, 1, 3).reshape(Bn, S, B_WIDTH)
    return jnp.concatenate([a_out, b_out], axis=-1).astype(h.dtype) @ w_out


def gla_chunked(q, k, v, log_a):
    Bn, H, S, dk = q.shape
    dv = v.shape[-1]
    nc = S // C_CHUNK

    def rs(t):
        return t.reshape(Bn, H, nc, C_CHUNK, t.shape[-1])

    q, k, v, log_a = rs(q), rs(k), rs(v), rs(log_a)
    b = jnp.cumsum(log_a, axis=3)
    b_last = b[:, :, :, -1:]
    q_t = q * jnp.exp(b)
    k_t = k * jnp.exp(-b)
    k_s = k * jnp.exp(b_last - b)
    causal = jnp.tril(jnp.ones((C_CHUNK, C_CHUNK), dtype=bool))
    attn = jnp.where(causal, jnp.einsum('bhnik,bhnjk->bhnij', q_t, k_t), 0.0)
    o_intra = jnp.einsum('bhnij,bhnjv->bhniv', attn, v)
    kv = jnp.einsum('bhnjk,bhnjv->bhnkv', k_s, v)
    decay = jnp.exp(b_last[:, :, :, 0])

    def step(state, inp):
        kv_n, d_n = inp
        return d_n[..., None] * state + kv_n, state

    init = jnp.zeros((Bn, H, dk, dv), jnp.float32)
    _, states = lax.scan(step, init, (jnp.moveaxis(kv, 2, 0), jnp.moveaxis(decay, 2, 0)))
    states = jnp.moveaxis(states, 0, 2)
    o_inter = jnp.einsum('bhnik,bhnkv->bhniv', q_t, states)
    return (o_intra + o_inter).reshape(Bn, H, S, dv)


def gla_mixer(h, w_in, w_a2, b_a, head_g, w_out):
    Bn, S, _ = h.shape
    hk, hv = C_HEADS * C_DK, C_HEADS * C_DV
    z = h @ w_in
    q, k, v, r, ga = jnp.split(z, [hk, 2 * hk, 2 * hk + hv, 2 * hk + 2 * hv], axis=-1)

    def heads(t, dh):
        return t.reshape(Bn, S, C_HEADS, dh).transpose(0, 2, 1, 3).astype(jnp.float32)

    log_a = jax.nn.log_sigmoid((ga @ w_a2 + b_a).astype(jnp.float32)) / C_TAU
    o = gla_chunked(heads(q, C_DK) * (C_DK ** -0.5), heads(k, C_DK), heads(v, C_DV),
                    heads(log_a, C_DK))
    o = rms_norm(o, head_g)
    o = o.transpose(0, 2, 1, 3).reshape(Bn, S, hv) * jax.nn.silu(r.astype(jnp.float32))
    return o.astype(h.dtype) @ w_out


def conv_ffn(h, w_gate, w_up, conv_w, conv_b, w_down):
    g = causal_dwconv(h @ w_gate, conv_w, conv_b)
    return (jax.nn.silu(g) * (h @ w_up)) @ w_down


def setup_inputs(seed: int = 0) -> dict:
    key = jax.random.key(seed)
    ks = iter(jax.random.split(key, 32))

    def nrm(shape, scale):
        return jax.random.normal(next(ks), shape, jnp.float32) * scale

    res = (2 * DEPTH) ** -0.5
    return {
        "x": nrm((BATCH, SEQ, D_MODEL), 1.0),
        "norm_mix_g": 1.0 + nrm((DEPTH, D_MODEL), 0.1),
        "norm_ffn_g": 1.0 + nrm((DEPTH, D_MODEL), 0.1),
        "ev_w_in": nrm((N_EVEN, D_MODEL, EVEN_IN), D_MODEL ** -0.5),
        "ev_a_ln_g": 1.0 + nrm((N_EVEN, A_WIDTH), 0.1),
        "ev_a_ln_b": nrm((N_EVEN, A_WIDTH), 0.02),
        "ev_a_ws": nrm((N_EVEN, A_GROUPS, A_CHUNK, A_CHUNK), A_CHUNK ** -0.5),
        "ev_a_bs": 1.0 + nrm((N_EVEN, A_GROUPS, A_CHUNK), 0.1),
        "ev_q_g": 1.0 + nrm((N_EVEN, B_HEAD_DIM), 0.1),
        "ev_k_g": 1.0 + nrm((N_EVEN, B_HEAD_DIM), 0.1),
        "ev_w_out": nrm((N_EVEN, A_WIDTH + B_WIDTH, D_MODEL), (A_WIDTH + B_WIDTH) ** -0.5 * res),
        "od_w_in": nrm((N_ODD, D_MODEL, ODD_IN), D_MODEL ** -0.5),
        "od_w_a2": nrm((N_ODD, C_GATE_RANK, C_HEADS * C_DK), C_GATE_RANK ** -0.5),
        "od_b_a": nrm((N_ODD, C_HEADS * C_DK), 0.5),
        "od_head_g": 1.0 + nrm((N_ODD, C_DV), 0.1),
        "od_w_out": nrm((N_ODD, C_HEADS * C_DV, D_MODEL), (C_HEADS * C_DV) ** -0.5 * res),
        "ffn_w_gate": nrm((DEPTH, D_MODEL, D_FF), D_MODEL ** -0.5),
        "ffn_w_up": nrm((DEPTH, D_MODEL, D_FF), D_MODEL ** -0.5),
        "ffn_conv_w": nrm((DEPTH, CONV_W, D_FF), CONV_W ** -0.5),
        "ffn_conv_b": nrm((DEPTH, D_FF), 0.02),
        "ffn_w_down": nrm((DEPTH, D_FF, D_MODEL), D_FF ** -0.5 * res),
    }


def reference(x, norm_mix_g, norm_ffn_g,
              ev_w_in, ev_a_ln_g, ev_a_ln_b, ev_a_ws, ev_a_bs, ev_q_g, ev_k_g, ev_w_out,
              od_w_in, od_w_a2, od_b_a, od_head_g, od_w_out,
              ffn_w_gate, ffn_w_up, ffn_conv_w, ffn_conv_b, ffn_w_down):
    for layer in range(DEPTH):
        h = rms_norm(x, norm_mix_g[layer]).astype(x.dtype)
        if layer % 2 == 0:
            e = layer // 2
            mix = even_mixer(h, ev_w_in[e], ev_a_ln_g[e], ev_a_ln_b[e], ev_a_ws[e], ev_a_bs[e],
                             ev_q_g[e], ev_k_g[e], ev_w_out[e])
        else:
            o = layer // 2
            mix = gla_mixer(h, od_w_in[o], od_w_a2[o], od_b_a[o], od_head_g[o], od_w_out[o])
        x = x + mix.astype(x.dtype)
        h = rms_norm(x, norm_ffn_g[layer]).astype(x.dtype)
        x = x + conv_ffn(h, ffn_w_gate[layer], ffn_w_up[layer], ffn_conv_w[layer],
                         ffn_conv_b[layer], ffn_w_down[layer]).astype(x.dtype)
    return x
```

```python
import contextlib
import numpy as np
import concourse.bass as bass
import concourse.mybir as mybir
from concourse.bass_utils import run_bass_kernel_spmd

F32 = mybir.dt.float32
BF16 = mybir.dt.bfloat16
AF = mybir.ActivationFunctionType
ALU = mybir.AluOpType

ENGS = ("pe", "act", "dve", "pool", "sp")
SEM_CHUNK = 12000
DMA_RING = 6
EPS = 1e-6
T = 2048
NT = 4
DFF = 2816
NFC = 22
NEGM = -30000.0


class Op:
    __slots__ = ("eng", "fn", "reads", "writes", "is_dma", "deps", "signal",
                 "sem", "val", "ring_prev", "idx")


class Prog:
    def __init__(self, nc):
        self.nc = nc
        self.ops = []
        self.final_waits = []

    def op(self, eng, fn, reads=(), writes=(), is_dma=False, barrier=False):
        o = Op()
        o.eng, o.fn, o.is_dma = eng, fn, is_dma
        psr_ = tuple(r for r in reads if r.startswith("ps") and r[2:].isdigit() or r == "pst")
        o.reads = tuple(reads) + (() if barrier else ("PHASE",))
        o.writes = tuple(writes) + psr_ + (("PHASE",) if barrier else ())
        o.deps = []
        o.signal = is_dma
        o.sem = None
        o.val = 0
        o.ring_prev = None
        o.idx = len(self.ops)
        self.ops.append(o)
        return o

    def dma(self, eng, out, in_, reads=(), writes=(), final=False):
        o = self.op(eng, lambda e, out=out, in_=in_: e.dma_start(out=out, in_=in_),
                    reads, writes, is_dma=True)
        if final:
            self.final_waits.append(o)
        return o

    def _analyze(self):
        last_w = {}
        readers = {}
        ops = self.ops
        for o in ops:
            deps = {}
            for r in o.reads:
                lw = last_w.get(r)
                if lw is not None:
                    deps[lw] = "raw"
            for w in o.writes:
                lw = last_w.get(w)
                if lw is not None and lw not in deps:
                    deps[lw] = "waw"
                for rd in readers.get(w, ()):
                    if rd not in deps:
                        deps[rd] = "war"
            deps.pop(o.idx, None)
            for r in o.reads:
                readers.setdefault(r, []).append(o.idx)
            for w in o.writes:
                last_w[w] = o.idx
                readers[w] = []
            best = {}
            dma_deps = []
            for d, kind in deps.items():
                od = ops[d]
                if od.is_dma:
                    dma_deps.append(d)
                    continue
                if od.eng == o.eng and not o.is_dma:
                    if od.eng in ("pe", "sp"):
                        continue
                if od.eng not in best or best[od.eng] < d:
                    best[od.eng] = d
            o.deps = sorted(list(best.values()) + dma_deps)
        waited = {e: {} for e in ENGS}
        waited_dma = {e: set() for e in ENGS}
        for o in ops:
            nd = []
            for d in o.deps:
                od = ops[d]
                if od.is_dma:
                    if d in waited_dma[o.eng]:
                        continue
                    waited_dma[o.eng].add(d)
                    nd.append(d)
                else:
                    if waited[o.eng].get(od.eng, -1) >= d:
                        continue
                    waited[o.eng][od.eng] = d
                    nd.append(d)
            o.deps = nd
            for d in nd:
                ops[d].signal = True

    def emit(self):
        nc = self.nc
        self._analyze()
        with contextlib.ExitStack() as stack:
            cnt = {e: 0 for e in ENGS}
            sems = {e: [] for e in ENGS}
            rings = {e: [] for e in ENGS}
            ring_cnt = {e: 0 for e in ENGS}
            ring_ord = {}
            ring_last = {}
            for o in self.ops:
                if o.is_dma:
                    j = ring_cnt[o.eng] % DMA_RING
                    ring_cnt[o.eng] += 1
                    if len(rings[o.eng]) <= j:
                        rings[o.eng].append(stack.enter_context(nc.semaphore(f"dr_{o.eng}_{j}")))
                    key = (o.eng, j)
                    ring_ord[key] = ring_ord.get(key, 0) + 1
                    o.sem = rings[o.eng][j]
                    o.val = 16 * ring_ord[key]
                    o.ring_prev = ring_last.get(key)
                    ring_last[key] = o
                elif o.signal:
                    c = cnt[o.eng]
                    k = c // SEM_CHUNK
                    if len(sems[o.eng]) <= k:
                        sems[o.eng].append(stack.enter_context(nc.semaphore(f"cs_{o.eng}_{k}")))
                    o.sem = sems[o.eng][k]
                    o.val = (c % SEM_CHUNK) + 1
                    cnt[o.eng] += 1
            block = stack.enter_context(nc.Block())
            ops = self.ops
            finals = self.final_waits

            def run(eng_name):
                def body(e):
                    for o in ops:
                        if o.eng != eng_name:
                            continue
                        if o.ring_prev is not None:
                            e.wait_ge(o.ring_prev.sem, o.ring_prev.val)
                        for d in o.deps:
                            od = ops[d]
                            e.wait_ge(od.sem, od.val)
                        ins = o.fn(e)
                        if o.signal:
                            ins.then_inc(o.sem, 16 if o.is_dma else 1)
                    for o in finals:
                        if o.eng == eng_name:
                            e.wait_ge(o.sem, o.val)
                return body

            block.tensor(run("pe"))
            block.scalar(run("act"))
            block.vector(run("dve"))
            block.gpsimd(run("pool"))
            block.sync(run("sp"))


C_ID, C_BONES, C_MPREV, C_MCUR, C_TRIL, C_M2, C_RESET, C_N = 0, 128, 256, 384, 512, 640, 768, 1280
L_GMIX, L_GFFN, L_CW0, L_CW1, L_CW2, L_CB, L_X = 0, 8, 16, 38, 60, 82, 104
L_QG, L_KG, L_BIAS, L_LNG, L_LNB = 104, 105, 106, 106 + 512, 106 + 1024
L_BA, L_HG = 104, 108
L_LNGF, L_LNBF = 106 + 1536, 106 + 1536 + 4
L_N = 106 + 1536 + 8


def build(layers=(0, 1, 2, 3), halves=(0, 1), phases=("mix", "ffn")):
    nc = bass.Bass("TRN2", target_bir_lowering=False)

    def D(name, shape, kind="ExternalInput", dt=F32):
        return nc.dram_tensor(name, shape, dt, kind=kind).ap()

    xT = D("xT", [2, 1024, T])
    outT = D("outT", [2, 1024, T], "ExternalOutput")
    xs = D("xs", [2, 1024, T], "Internal")
    consts = D("consts", [128, C_N])
    lp = D("lp", [4, 128, L_N])
    has_even = any(l % 2 == 0 for l in layers) and "mix" in phases
    has_odd = any(l % 2 == 1 for l in layers) and "mix" in phases
    has_ffn = "ffn" in phases
    ev_w_in = D("ev_w_in", [2, 1024, 2560]) if has_even else None
    ev_wsT = D("ev_wsT", [2, 128, 8, 128]) if has_even else None
    ev_w_out = D("ev_w_out", [2, 1024, 1024]) if has_even else None
    od_w_in = D("od_w_in", [2, 1024, 3088]) if has_odd else None
    od_w_a2 = D("od_w_a2", [2, 16, 512]) if has_odd else None
    od_w_out = D("od_w_out", [2, 1024, 1024]) if has_odd else None
    w_gate = D("ffn_w_gate", [4, 1024, DFF]) if has_ffn else None
    w_up = D("ffn_w_up", [4, 1024, DFF]) if has_ffn else None
    w_down = D("ffn_w_down", [4, DFF, 1024]) if has_ffn else None
    kcar = D("kcar", [4, 128, T], "Internal", BF16)
    vcar = D("vcar", [3, 4, 128, 16 * 192], "Internal", BF16)

    P = Prog(nc)
    fuse_ffn_norm = ("mix" in phases) and ("ffn" in phases)
    top = contextlib.ExitStack()
    uid = [0]

    def sb(stack, name, shape, dt):
        uid[0] += 1
        return stack.enter_context(nc.sbuf_tensor(f"{name}_{uid[0]}", shape, dt))

    with top:
        x = sb(top, "x", [128, 8, T], F32)
        h = sb(top, "h", [128, 8, T], BF16)
        cst = sb(top, "cst", [128, C_N - C_TRIL], F32)
        lpt = sb(top, "lpt", [128, 128], F32)
        ident = sb(top, "ident", [128, 128], BF16)
        ones = sb(top, "ones", [128, 128], BF16)
        bones = sb(top, "bones", [128, 128], BF16)
        maskb4 = sb(top, "maskb4", [128, 512], BF16)
        state = sb(top, "state", [128, 4, 256], F32)
        gtail = sb(top, "gtail", [128, NFC, 2], F32)
        dummy = sb(top, "dummy", [128, 8], F32)
        psb = [top.enter_context(nc.psum_tensor(f"ps{i}", [128, 512], F32)) for i in range(7)]
        pst = top.enter_context(nc.psum_tensor("pst", [128, 1024], BF16))

        def barrier():
            P.op("pool", lambda e: e.memset(dummy[:], 0.0), barrier=True)

        def mm(out, pairs, reads, writes, start=True):
            def fn(e):
                n = len(pairs)
                ins = None
                for i, (l, r) in enumerate(pairs):
                    ins = e.matmul(out, l, r, start=(start and i == 0), stop=(i == n - 1),
                                   skip_group_check=True)
                return ins
            P.op("pe", fn, reads, writes)

        psr = [0]

        def next_ps(n=1):
            i = psr[0] % 6
            psr[0] += 1
            return i

        P.dma("sp", cst[:], consts[:, C_TRIL:C_N], writes=["cst"])
        P.dma("pool", ident[:], consts[:, C_ID:C_ID + 128], writes=["ident"])
        P.dma("pool", bones[:], consts[:, C_BONES:C_BONES + 128], writes=["bones"])
        for q_, c_ in enumerate((C_MPREV, C_MPREV, C_MCUR, C_MCUR)):
            P.dma("pool", maskb4[:, q_ * 128:(q_ + 1) * 128], consts[:, c_:c_ + 128], writes=["maskb4"])
        P.op("dve", lambda e: e.memset(ones[:], 1.0), writes=["ones"])

        def xr(c, tt):
            return f"x{c}_{tt}"

        def norm_tile(tt, gcol, sq, rt, rs):
            b = tt % 2
            ts = slice(tt * 512, (tt + 1) * 512)
            pi = next_ps()
            P.op("act", lambda e: e.activation(out=sq[b][:], in_=x[:, :, ts], func=AF.Square),
                 reads=[xr(k, tt) for k in range(8)], writes=[f"sq{b}"])
            mm(psb[pi][:], [(ones[:], sq[b][:, kc, :]) for kc in range(8)],
               reads=[f"sq{b}", "ones"], writes=[f"ps{pi}"])
            P.op("act", lambda e: e.activation(out=rt[b][:], in_=psb[pi][:], func=AF.Ln, bias=EPS, scale=1.0 / 1024),
                 reads=[f"ps{pi}"], writes=[f"rt{b}"])
            P.op("act", lambda e: e.activation(out=rs[b][:], in_=rt[b][:], func=AF.Exp, scale=-0.5),
                 reads=[f"rt{b}"], writes=[f"rs{b}"])
            for kc in range(8):
                P.op("dve", lambda e, kc=kc: e.scalar_tensor_tensor(
                    out=h[:, kc, ts], in0=x[:, kc, ts], scalar=lpt[:, gcol + kc:gcol + kc + 1],
                    in1=rs[b][:], op0=ALU.mult, op1=ALU.mult),
                    reads=[xr(kc, tt), f"rs{b}", "lpt"], writes=["h"])

        def rms_norm(gcol):
            with contextlib.ExitStack() as st:
                sq = [sb(st, "sq", [128, 8, 512], BF16) for _ in range(2)]
                rt = [sb(st, "rt", [128, 512], F32) for _ in range(2)]
                rs = [sb(st, "rs", [128, 512], F32) for _ in range(2)]
                for tt in range(NT):
                    norm_tile(tt, gcol, sq, rt, rs)
                barrier()

        def out_proj(w_dram, src, nkc=8, norm_gcol=None):
            with contextlib.ExitStack() as st:
                wb = [sb(st, "wo", [128, nkc, 256], BF16) for _ in range(4)]
                wv = w_dram.rearrange("(kc p) n -> p kc n", p=128)
                if norm_gcol is not None:
                    sq = [sb(st, "sq", [128, 8, 512], BF16) for _ in range(2)]
                    rt = [sb(st, "rt", [128, 512], F32) for _ in range(2)]
                    rs = [sb(st, "rs", [128, 512], F32) for _ in range(2)]
                for blk in range(4):
                    P.dma("pool", wb[blk][:], wv[:, :, blk * 256:(blk + 1) * 256], writes=[f"wo{blk}"])

                def op_tile(tt):
                    ts = slice(tt * 512, (tt + 1) * 512)
                    for blk in range(4):
                        for dc in range(2):
                            dch = blk * 2 + dc
                            pi = next_ps()
                            mm(psb[pi][:], [(wb[blk][:, kc, dc * 128:(dc + 1) * 128], src[:, kc, ts]) for kc in range(nkc)],
                               reads=[f"wo{blk}", "src"], writes=[f"ps{pi}"])
                            P.op("dve", lambda e, pi=pi, dch=dch: e.tensor_tensor(
                                out=x[:, dch, ts], in0=x[:, dch, ts], in1=psb[pi][:], op=ALU.add),
                                reads=[f"ps{pi}", xr(dch, tt)], writes=[xr(dch, tt)])
                for i in range(NT + 1):
                    if i < NT:
                        op_tile(i)
                    if i >= 1 and norm_gcol is not None:
                        norm_tile(i - 1, norm_gcol, sq, rt, rs)
                barrier()

        def ffn(layer, half, fin=None, normed=False):
            if not normed:
                rms_norm(L_GFFN)
            wgv = w_gate[layer].rearrange("(kc p) n -> p kc n", p=128)
            wuv = w_up[layer].rearrange("(kc p) n -> p kc n", p=128)
            wdv = w_down[layer].rearrange("(kc p) n -> p kc n", p=128)
            if half == 0:
                P.op("dve", lambda e: e.memset(gtail[:], 0.0), writes=["gtail"])
            with contextlib.ExitStack() as st:
                hid = sb(st, "hid", [128, NFC, 1024], BF16)
                wg = [sb(st, "wg", [128, 8, 256], BF16) for _ in range(2)]
                wu = [sb(st, "wu", [128, 8, 256], BF16) for _ in range(2)]
                gs = [sb(st, "gs", [128, 514], F32) for _ in range(2)]
                ac = [sb(st, "ac", [128, 512], F32) for _ in range(2)]
                wd = [sb(st, "wd", [128, NFC, 256], BF16) for _ in range(2)]
                it = 0
                for sh in range(2):
                    for blk in range(11):
                        b = blk % 2
                        P.dma("pool", wg[b][:], wgv[:, :, blk * 256:(blk + 1) * 256], writes=[f"wg{b}"])
                        P.dma("pool", wu[b][:], wuv[:, :, blk * 256:(blk + 1) * 256], writes=[f"wu{b}"])
                        for fc2 in range(2):
                            fc = blk * 2 + fc2
                            cs = slice(fc2 * 128, (fc2 + 1) * 128)
                            for t2 in range(2):
                                tt = sh * 2 + t2
                                ts = slice(tt * 512, (tt + 1) * 512)
                                hs = slice(t2 * 512, (t2 + 1) * 512)
                                s = it % 2
                                it += 1
                                pg = next_ps()
                                pu = next_ps()
                                mm(psb[pg][:], [(wg[b][:, kc, cs], h[:, kc, ts]) for kc in range(8)],
                                   reads=[f"wg{b}", "h"], writes=[f"ps{pg}"])
                                mm(psb[pu][:], [(wu[b][:, kc, cs], h[:, kc, ts]) for kc in range(8)],
                                   reads=[f"wu{b}", "h"], writes=[f"ps{pu}"])
                                P.op("dve", lambda e, s=s, fc=fc: e.tensor_copy(out=gs[s][:, 0:2], in_=gtail[:, fc, :]),
                                     reads=[f"gtail{fc}", "gtail"], writes=[f"gs{s}"])
                                P.op("act", lambda e, s=s, pg=pg: e.activation(out=gs[s][:, 2:514], in_=psb[pg][:], func=AF.Copy),
                                     reads=[f"ps{pg}"], writes=[f"gs{s}b"])
                                P.op("dve", lambda e, s=s, fc=fc: e.tensor_copy(out=gtail[:, fc, :], in_=gs[s][:, 512:514]),
                                     reads=[f"gs{s}b"], writes=[f"gtail{fc}"])
                                P.op("dve", lambda e, s=s, fc=fc: e.tensor_scalar(
                                    out=ac[s][:], in0=gs[s][:, 2:514], scalar1=lpt[:, L_CW2 + fc:L_CW2 + fc + 1],
                                    scalar2=lpt[:, L_CB + fc:L_CB + fc + 1], op0=ALU.mult, op1=ALU.add),
                                    reads=[f"gs{s}b", "lpt"], writes=[f"ac{s}"])
                                P.op("dve", lambda e, s=s, fc=fc: e.scalar_tensor_tensor(
                                    out=ac[s][:], in0=gs[s][:, 1:513], scalar=lpt[:, L_CW1 + fc:L_CW1 + fc + 1],
                                    in1=ac[s][:], op0=ALU.mult, op1=ALU.add),
                                    reads=[f"gs{s}b", f"gs{s}", f"ac{s}", "lpt"], writes=[f"ac{s}"])
                                P.op("dve", lambda e, s=s, fc=fc: e.scalar_tensor_tensor(
                                    out=ac[s][:], in0=gs[s][:, 0:512], scalar=lpt[:, L_CW0 + fc:L_CW0 + fc + 1],
                                    in1=ac[s][:], op0=ALU.mult, op1=ALU.add),
                                    reads=[f"gs{s}b", f"gs{s}", f"ac{s}", "lpt"], writes=[f"ac{s}"])
                                P.op("act", lambda e, s=s: e.activation(out=ac[s][:], in_=ac[s][:], func=AF.Silu),
                                     reads=[f"ac{s}"], writes=[f"ac{s}"])
                                P.op("dve", lambda e, s=s, pu=pu, fc=fc, hs=hs: e.tensor_tensor(
                                    out=hid[:, fc, hs], in0=ac[s][:], in1=psb[pu][:], op=ALU.mult),
                                    reads=[f"ac{s}", f"ps{pu}"], writes=["hid"])
                    for blk in range(4):
                        b = blk % 2
                        P.dma("pool", wd[b][:], wdv[:, :, blk * 256:(blk + 1) * 256], writes=[f"wd{b}"])
                        for dc in range(2):
                            dch = blk * 2 + dc
                            for t2 in range(2):
                                tt = sh * 2 + t2
                                ts = slice(tt * 512, (tt + 1) * 512)
                                hs = slice(t2 * 512, (t2 + 1) * 512)
                                pi = next_ps()
                                mm(psb[pi][:], [(wd[b][:, kc, dc * 128:(dc + 1) * 128], hid[:, kc, hs]) for kc in range(NFC)],
                                   reads=[f"wd{b}", "hid"], writes=[f"ps{pi}"])
                                P.op("dve", lambda e, pi=pi, dch=dch, ts=ts: e.tensor_tensor(
                                    out=x[:, dch, ts], in0=x[:, dch, ts], in1=psb[pi][:], op=ALU.add),
                                    reads=[f"ps{pi}", xr(dch, tt)], writes=[xr(dch, tt)])
                            if sh == 1 and fin is not None:
                                fin(dch)
                barrier()

        def gla(o, half):
            wv = od_w_in[o].rearrange("(kc p) n -> p kc n", p=128)
            with contextlib.ExitStack() as st:
                cat = sb(st, "cat", [128, 8, T], BF16)
                with contextlib.ExitStack() as st2:
                    wq = sb(st2, "wq", [128, 8, 128], BF16)
                    wk = sb(st2, "wk", [128, 8, 128], BF16)
                    wvv = sb(st2, "wvv", [128, 8, 256], BF16)
                    wr = sb(st2, "wr", [128, 8, 256], BF16)
                    wga = sb(st2, "wga", [128, 8, 16], BF16)
                    wa2 = sb(st2, "wa2", [16, 512], F32)
                    nba = sb(st2, "nba", [128, 4], F32)
                    gaT = sb(st2, "gaT", [16, T], F32)
                    e1 = sb(st2, "e1", [128, 512], F32)
                    b16 = sb(st2, "b16", [128, 512], F32)
                    eb = sb(st2, "eb", [128, 512], F32)
                    enb = sb(st2, "enb", [128, 512], F32)
                    ed = sb(st2, "ed", [128, 512], F32)
                    ks = sb(st2, "ks", [128, 512], BF16)
                    dec = [sb(st2, "dec", [128, 8], F32) for _ in range(2)]
                    qt = [sb(st2, "qt", [128, 512], BF16) for _ in range(2)]
                    kt = [sb(st2, "kt", [128, 512], BF16) for _ in range(2)]
                    ksT = [sb(st2, "ksT", [128, 4, 128], BF16) for _ in range(2)]
                    vtm = [sb(st2, "vtm", [128, 4, 256], BF16) for _ in range(2)]
                    sr = [sb(st2, "sr", [128, 2, 512], BF16) for _ in range(2)]
                    aT = [sb(st2, "aT", [128, 128], BF16) for _ in range(2)]
                    ot = sb(st2, "ot", [128, 2, 512], F32)
                    osq = sb(st2, "osq", [128, 2, 512], BF16)
                    ort = sb(st2, "ort", [128, 512], F32)
                    stb = [sb(st2, "stb", [128, 256], BF16) for _ in range(2)]
                    P.dma("pool", wga[:], wv[:, :, 3072:3088], writes=["wga"])
                    P.dma("sp", wa2[:], od_w_a2[o], writes=["wa2"])
                    P.op("dve", lambda e: e.tensor_scalar(out=nba[:], in0=lpt[:, L_BA:L_BA + 4], scalar1=-1.0, scalar2=None,
                                                          op0=ALU.mult), reads=["lpt"], writes=["nba"])
                    if half == 0:
                        P.op("dve", lambda e: e.memset(state[:], 0.0), writes=["state"])
                    for tt in range(NT):
                        ts = slice(tt * 512, (tt + 1) * 512)
                        pi = next_ps()
                        mm(psb[pi][0:16, :], [(wga[:, kc, :], h[:, kc, ts]) for kc in range(8)],
                           reads=["wga", "h"], writes=[f"ps{pi}"])
                        P.op("act", lambda e, pi=pi, ts=ts: e.activation(out=gaT[:, ts], in_=psb[pi][0:16, :], func=AF.Copy),
                             reads=[f"ps{pi}"], writes=["gaT"])
                    seq = [(hd, tt) for hd in range(4) for tt in range(NT)]
                    pctr = [0]
                    cctr = [0]

                    def ps_p():
                        pctr[0] += 1
                        return pctr[0] % 3

                    def ps_o():
                        cctr[0] += 1
                        return 3 + cctr[0] % 2

                    def mm_split(out, pairs, reads, writes, n=4):
                        for k0 in range(0, len(pairs), n):
                            mm(out, pairs[k0:k0 + n], reads=reads, writes=writes, start=(k0 == 0))
                            yield

                    def pgen(i):
                        hd, tt = seq[i]
                        z = i % 2
                        ts = slice(tt * 512, (tt + 1) * 512)
                        if tt == 0:
                            P.dma("pool", wq[:], wv[:, :, hd * 128:(hd + 1) * 128], writes=["wq"])
                            P.dma("pool", wk[:], wv[:, :, 512 + hd * 128:512 + (hd + 1) * 128], writes=["wk"])
                            P.dma("pool", wvv[:], wv[:, :, 1024 + hd * 256:1024 + (hd + 1) * 256], writes=["wvv"])
                            P.dma("pool", wr[:], wv[:, :, 2048 + hd * 256:2048 + (hd + 1) * 256], writes=["wr"])
                        pi = ps_p()
                        mm(psb[pi][:], [(wa2[:, hd * 128:(hd + 1) * 128], gaT[:, ts])], reads=["wa2", "gaT"], writes=[f"ps{pi}"])
                        P.op("act", lambda e: e.activation(out=e1[:], in_=psb[pi][:], func=AF.Exp, bias=nba[:, hd:hd + 1], scale=-1.0),
                             reads=[f"ps{pi}", "nba"], writes=["e1"])
                        yield
                        P.op("act", lambda e: e.activation(out=e1[:], in_=e1[:], func=AF.Ln, bias=1.0, scale=1.0),
                             reads=["e1"], writes=["e1"])
                        yield
                        P.op("dve", lambda e: e.tensor_tensor_scan(out=b16[:], data0=cst[:, 256:768],
                                                                 data1=e1[:], initial=0.0, op0=ALU.mult, op1=ALU.add),
                             reads=["e1", "cst"], writes=["b16"])
                        yield
                        P.op("dve", lambda e: e.tensor_tensor(
                            out=e1[:].rearrange("p (c j) -> p c j", j=64),
                            in0=b16[:].rearrange("p (c j) -> p c j", j=64)[:, :, 63:64].to_broadcast([128, 8, 64]),
                            in1=b16[:].rearrange("p (c j) -> p c j", j=64), op=ALU.subtract),
                            reads=["b16", "e1"], writes=["e1"])
                        P.op("act", lambda e: e.activation(out=eb[:], in_=b16[:], func=AF.Exp, scale=-1.0 / 16), reads=["b16"], writes=["eb"])
                        yield
                        P.op("act", lambda e: e.activation(out=enb[:], in_=b16[:], func=AF.Exp, scale=1.0 / 16), reads=["b16"], writes=["enb"])
                        yield
                        P.op("act", lambda e: e.activation(out=ed[:], in_=e1[:], func=AF.Exp, scale=-1.0 / 16), reads=["e1"], writes=["ed"])
                        P.op("act", lambda e: e.activation(out=dec[z][:], in_=b16[:, 63:512:64], func=AF.Exp, scale=-1.0 / 16),
                             reads=["b16"], writes=[f"dec{z}"])
                        yield
                        pq = ps_p()
                        yield from mm_split(psb[pq][:], [(wq[:, kc, :], h[:, kc, ts]) for kc in range(8)], ["wq", "h"], [f"ps{pq}"])
                        P.op("dve", lambda e: e.scalar_tensor_tensor(out=qt[z][:], in0=psb[pq][:], scalar=128.0 ** -0.5,
                                                                   in1=eb[:], op0=ALU.mult, op1=ALU.mult),
                             reads=[f"ps{pq}", "eb"], writes=[f"qt{z}"])
                        yield
                        pk = ps_p()
                        yield from mm_split(psb[pk][:], [(wk[:, kc, :], h[:, kc, ts]) for kc in range(8)], ["wk", "h"], [f"ps{pk}"])
                        P.op("dve", lambda e: e.tensor_tensor(out=kt[z][:], in0=psb[pk][:], in1=enb[:], op=ALU.mult),
                             reads=[f"ps{pk}", "enb"], writes=[f"kt{z}"])
                        yield
                        P.op("dve", lambda e: e.tensor_tensor(out=ks[:], in0=psb[pk][:], in1=ed[:], op=ALU.mult),
                             reads=[f"ps{pk}", "ed"], writes=["ks"])
                        yield

                        def tr(e):
                            ins = None
                            for bl in range(4):
                                ins = e.transpose(pst[:, bl * 128:(bl + 1) * 128], ks[:, bl * 128:(bl + 1) * 128], ident[:])
                            return ins
                        P.op("pe", tr, reads=["ks", "ident"], writes=["pst"])
                        P.op("act", lambda e: e.activation(out=ksT[z][:].rearrange("p a b -> p (a b)"), in_=pst[:, 0:512], func=AF.Copy),
                             reads=["pst"], writes=[f"ksT{z}"])
                        yield
                        for bp in range(2):
                            pv = ps_p()
                            for b2 in range(2):
                                bl = bp * 2 + b2
                                t0 = tt * 512 + bl * 128
                                pairs = [(h[:, kc, t0:t0 + 128], wvv[:, kc, :]) for kc in range(8)]
                                for k0 in (0, 4):
                                    mm(psb[pv][:, b2 * 256:(b2 + 1) * 256], pairs[k0:k0 + 4], reads=["wvv", "h"], writes=[f"ps{pv}"],
                                       start=(k0 == 0 and b2 == 0))
                                    yield
                            P.op("act", lambda e, pv=pv, bp=bp: e.activation(
                                out=vtm[z][:, bp * 2:bp * 2 + 2, :].rearrange("p a b -> p (a b)"), in_=psb[pv][:], func=AF.Copy),
                                reads=[f"ps{pv}"], writes=[f"vtm{z}"])
                            yield
                        for dvc in range(2):
                            pr = ps_p()
                            yield from mm_split(psb[pr][:], [(wr[:, kc, dvc * 128:(dvc + 1) * 128], h[:, kc, ts]) for kc in range(8)],
                                                ["wr", "h"], [f"ps{pr}"])
                            P.op("act", lambda e, pr=pr, dvc=dvc: e.activation(out=sr[z][:, dvc, :], in_=psb[pr][:], func=AF.Silu),
                                 reads=[f"ps{pr}"], writes=[f"sr{z}"])
                            yield

                    def adv(g, n=1):
                        if g is None:
                            return
                        for _ in range(n):
                            try:
                                next(g)
                            except StopIteration:
                                return

                    def cstage(i, g):
                        hd, tt = seq[i]
                        z = i % 2
                        ts = slice(tt * 512, (tt + 1) * 512)
                        for bl in range(4):
                            bs_ = slice(bl * 128, (bl + 1) * 128)
                            az = bl % 2
                            pa = 5
                            mm(psb[pa][:, 0:128], [(kt[z][:, bs_], qt[z][:, bs_])], reads=[f"kt{z}", f"qt{z}"], writes=[f"ps{pa}"])
                            P.op("dve", lambda e, pa=pa, az=az: e.tensor_tensor(out=aT[az][:], in0=psb[pa][:, 0:128],
                                                                             in1=cst[:, 128:256], op=ALU.mult),
                                 reads=[f"ps{pa}", "cst"], writes=[f"aT{az}"])
                            adv(g)
                            po = ps_o()
                            for dvc in range(2):
                                mm(psb[po][:, dvc * 128:(dvc + 1) * 128], [(vtm[z][:, bl, dvc * 128:(dvc + 1) * 128], aT[az][:])],
                                   reads=[f"vtm{z}", f"aT{az}"], writes=[f"ps{po}"], start=(dvc == 0))
                            for c2 in range(2):
                                ci = bl * 2 + c2
                                cs = slice(bl * 128 + c2 * 64, bl * 128 + (c2 + 1) * 64)
                                pp = slice(c2 * 64, (c2 + 1) * 64)
                                P.op("dve", lambda e, c2=c2: e.tensor_copy(out=stb[c2][:], in_=state[:, hd, :]),
                                     reads=["state"], writes=[f"stb{c2}"])
                                mm(psb[6][:, 0:256], [(ksT[z][pp, bl, :], vtm[z][pp, bl, :])], reads=[f"ksT{z}", f"vtm{z}"], writes=["ps6"])
                                adv(g)
                                P.op("dve", lambda e, ci=ci: e.scalar_tensor_tensor(
                                    out=state[:, hd, :], in0=state[:, hd, :], scalar=dec[z][:, ci:ci + 1], in1=psb[6][:, 0:256],
                                    op0=ALU.mult, op1=ALU.add), reads=["state", f"dec{z}", "ps6", f"stb{c2}"], writes=["state"])
                                for dvc in range(2):
                                    mm(psb[po][:, dvc * 128 + c2 * 64:dvc * 128 + (c2 + 1) * 64],
                                       [(stb[c2][:, dvc * 128:(dvc + 1) * 128], qt[z][:, cs])],
                                       reads=[f"stb{c2}", f"qt{z}"], writes=[f"ps{po}"], start=False)
                                adv(g, 2)
                            P.op("act", lambda e, po=po, bs_=bs_: e.activation(
                                out=ot[:, :, bs_], in_=psb[po][:, 0:256].rearrange("p (a b) -> p a b", a=2), func=AF.Copy),
                                reads=[f"ps{po}"], writes=["ot"])
                        P.op("act", lambda e: e.activation(out=osq[:], in_=ot[:], func=AF.Square), reads=["ot"], writes=["osq"])
                        pn = 5
                        mm(psb[pn][:], [(ones[:], osq[:, dvc, :]) for dvc in range(2)], reads=["ones", "osq"], writes=[f"ps{pn}"])
                        P.op("act", lambda e: e.activation(out=ort[:], in_=psb[pn][:], func=AF.Ln, bias=EPS, scale=1.0 / 256),
                             reads=[f"ps{pn}"], writes=["ort"])
                        P.op("act", lambda e: e.activation(out=ort[:], in_=ort[:], func=AF.Exp, scale=-0.5), reads=["ort"], writes=["ort"])
                        adv(g, 2)
                        for dvc in range(2):
                            P.op("dve", lambda e, dvc=dvc: e.scalar_tensor_tensor(
                                out=ot[:, dvc, :], in0=ot[:, dvc, :], scalar=lpt[:, L_HG + dvc:L_HG + dvc + 1], in1=ort[:],
                                op0=ALU.mult, op1=ALU.mult), reads=["ot", "ort", "lpt"], writes=["ot"])
                            P.op("dve", lambda e, dvc=dvc: e.tensor_tensor(
                                out=cat[:, hd * 2 + dvc, ts], in0=ot[:, dvc, :], in1=sr[z][:, dvc, :], op=ALU.mult),
                                reads=["ot", f"sr{z}"], writes=["src"])
                        adv(g, 100)

                    g0 = pgen(0)
                    adv(g0, 100)
                    for i in range(len(seq)):
                        g = pgen(i + 1) if i + 1 < len(seq) else None
                        cstage(i, g)
                    barrier()
                out_proj(od_w_out[o], cat, norm_gcol=(L_GFFN if fuse_ffn_norm else None))


        def attn_branch(hp, bi, d, half, vT, qn, kn, knp, vp, vpp, acc, pT, vctr):
            nbl = 16 // d
            vb = vctr[0] % 2
            vctr[0] += 1
            vpc = vp[vb]
            vpn = f"vp{vb}"

            def cols(r, n):
                a = r + d * 128 * n
                return slice(a, a + d * 127 + 1, d)
            if half == 1:
                P.dma("sp", vpp[:].rearrange("p a b -> p (a b)"), vcar[bi, hp], reads=[f"vcar{bi}_{hp}"], writes=["vpp"])
            for q4 in range(4):
                ptile, pname = ((pst[:, 0:512], "pst") if q4 % 2 == 0 else (psb[6][:].bitcast(BF16)[:, 0:512], "ps6"))

                def trs(e, q4=q4, ptile=ptile):
                    ins = None
                    for b4 in range(4):
                        blk = q4 * 4 + b4
                        r, n = blk // nbl, blk % nbl
                        ins = e.transpose(ptile[:, b4 * 128:(b4 + 1) * 128], vT[:, cols(r, n)], ident[:])
                    return ins
                P.op("pe", trs, reads=["vT", "ident"], writes=[pname])
                pv3 = ptile.rearrange("p (a b) -> p a b", a=4)
                P.op("act", lambda e, pv3=pv3, q4=q4: e.activation(out=vpc[:, q4 * 4:q4 * 4 + 4, 0:64], in_=pv3[:, :, 0:64], func=AF.Copy),
                     reads=[pname], writes=[vpn])
                P.op("act", lambda e, pv3=pv3, q4=q4: e.activation(out=vpc[:, q4 * 4:q4 * 4 + 4, 128:192], in_=pv3[:, :, 64:128], func=AF.Copy),
                     reads=[pname], writes=[vpn])
            if half == 0:
                P.dma("sp", vcar[bi, hp], vpc[:].rearrange("p a b -> p (a b)"), reads=[vpn], writes=[f"vcar{bi}_{hp}"])
            info = {}

            def s1(blk):
                r, n = blk // nbl, blk % nbl
                cq = cols(r, n)
                has_prev = (n > 0) or (half == 1)
                s = blk % 3
                pi = next_ps()
                vprev = None
                if has_prev:
                    if n > 0:
                        kpt, kc_, vprev, kpn = kn, cols(r, n - 1), (vpc, blk - 1, vpn), "kn"
                    else:
                        kpt, kc_, vprev, kpn = knp, cols(r, nbl - 1), (vpp, r * nbl + nbl - 1, "vpp"), "knp"
                    mm(psb[pi][:, 0:256], [(kpt[:, kc_], qn[:, :, cq])], reads=[kpn, "qn"], writes=[f"ps{pi}"])
                    mm(psb[pi][:, 256:512], [(kn[:, cq], qn[:, :, cq])], reads=["kn", "qn"], writes=[f"ps{pi}"], start=False)
                    mm(psb[pi][:], [(ident[:], maskb4[:])], reads=["ident", "maskb4"], writes=[f"ps{pi}"], start=False)
                    P.op("act", lambda e: e.activation(out=pT[s][:], in_=psb[pi][:], func=AF.Exp, scale=0.125),
                         reads=[f"ps{pi}"], writes=[f"pT{s}"])
                else:
                    mm(psb[pi][:, 256:512], [(kn[:, cq], qn[:, :, cq]), (ident[:], maskb4[:, 256:512])],
                       reads=["kn", "qn", "ident", "maskb4"], writes=[f"ps{pi}"])
                    P.op("act", lambda e: e.activation(out=pT[s][:, 256:512], in_=psb[pi][:, 256:512], func=AF.Exp, scale=0.125),
                         reads=[f"ps{pi}"], writes=[f"pT{s}"])
                info[blk] = (cq, has_prev, s, vprev)

            def s2(blk):
                cq, has_prev, s, vprev = info[blk]
                pn = next_ps()
                first = True
                for hh in range(2):
                    vs = slice(hh * 64, hh * 64 + 128)
                    prs, rd = [], [vpn, f"pT{s}"]
                    if has_prev:
                        prs.append((vprev[0][:, vprev[1], vs], pT[s][:, hh * 128:hh * 128 + 128]))
                        rd.append(vprev[2])
                    prs.append((vpc[:, blk, vs], pT[s][:, 256 + hh * 128:256 + hh * 128 + 128]))
                    mm(psb[pn][:, hh * 128:(hh + 1) * 128], prs, reads=rd, writes=[f"ps{pn}"], start=first)
                    first = False
                av = acc[:, :, cq]
                sv = psb[pn][:, 0:256].rearrange("p (a b) -> p a b", a=2)
                if bi == 0:
                    P.op("act", lambda e: e.activation(out=av, in_=sv, func=AF.Copy), reads=[f"ps{pn}"], writes=["acc"])
                else:
                    P.op("dve", lambda e: e.tensor_tensor(out=av, in0=av, in1=sv, op=ALU.add), reads=[f"ps{pn}", "acc"], writes=["acc"])
            SK = 2
            for i in range(16 + SK):
                if i < 16:
                    s1(i)
                if i >= SK:
                    s2(i - SK)

        def even(ei, half):
            wv = ev_w_in[ei].rearrange("(kc p) n -> p kc n", p=128)
            with contextlib.ExitStack() as st:
                cat = sb(st, "cat", [128, 8, T], BF16)
                with contextlib.ExitStack() as st2:
                    wva = sb(st2, "wva", [128, 8, 512], BF16)
                    wu = [sb(st2, "wua", [128, 8, 128], BF16) for _ in range(2)]
                    wsT = sb(st2, "wsT", [128, 8, 128], BF16)
                    wsf = sb(st2, "wsf", [128, 8, 128], F32)
                    gv = [sb(st2, "gv", [128, 512], F32) for _ in range(2)]
                    vn = [sb(st2, "vn", [128, 512], BF16) for _ in range(2)]
                    s1 = [sb(st2, "s1", [128, 6], F32) for _ in range(2)]
                    s2 = [sb(st2, "s2", [128, 2], F32) for _ in range(2)]
                    s3 = [sb(st2, "s3", [128, 1], F32) for _ in range(2)]
                    tmp = [sb(st2, "tmp", [128, 128], F32) for _ in range(4)]
                    mhalf = sb(st2, "mhalf", [128, 1], F32)
                    lpa = sb(st2, "lpa", [128, 520], F32)
                    P.dma("sp", lpa[:, 0:512], lp[2 * ei][:, L_BIAS:L_BIAS + 512], writes=["lpa"])
                    P.dma("sp", lpa[:, 512:520], lp[2 * ei][:, L_LNGF:L_LNGF + 8], writes=["lpa"])
                    b2t = sb(st2, "b2t", [128, 4, 128], F32)
                    s4 = [sb(st2, "s4", [128, 1], F32) for _ in range(2)]
                    P.op("pool", lambda e: e.memset(mhalf[:], -0.5), writes=["mhalf"])
                    P.dma("sp", wsf[:], ev_wsT[ei], writes=["wsf"])
                    P.op("dve", lambda e: e.tensor_tensor(out=wsT[:], in0=wsf[:],
                                                        in1=cst[:, 0:128].unsqueeze(1).to_broadcast([128, 8, 128]),
                                                        op=ALU.mult), reads=["wsf", "cst"], writes=["wsT"])
                    for g in range(8):
                        cp_, gg_ = g // 2, g % 2
                        pp_ = slice(gg_ * 64, (gg_ + 1) * 64)
                        pi = next_ps()
                        mm(psb[pi][:, 0:128], [(ones[:], wsT[:, g, :])], reads=["ones", "wsT"], writes=[f"ps{pi}"])
                        P.op("dve", lambda e, pi=pi, pp_=pp_, cp_=cp_: e.scalar_tensor_tensor(
                            out=b2t[pp_, cp_, :], in0=psb[pi][pp_, 0:128], scalar=lpa[pp_, 516 + cp_:517 + cp_],
                            in1=lpa[pp_, cp_ * 128:(cp_ + 1) * 128], op0=ALU.mult, op1=ALU.add),
                            reads=[f"ps{pi}", "lpa"], writes=["b2t"])
                    P.dma("pool", wva[:], wv[:, :, 512:1024], writes=["wva"])
                    for uc in range(4):
                        b = uc % 2
                        P.dma("pool", wu[b][:], wv[:, :, uc * 128:(uc + 1) * 128], writes=[f"wua{b}"])
                        for tt in range(NT):
                            ts = slice(tt * 512, (tt + 1) * 512)
                            pi = next_ps()
                            mm(psb[pi][:], [(wu[b][:, kc, :], h[:, kc, ts]) for kc in range(8)], reads=[f"wua{b}", "h"], writes=[f"ps{pi}"])
                            P.op("act", lambda e, pi=pi, uc=uc, ts=ts: e.activation(out=cat[:, uc, ts], in_=psb[pi][:], func=AF.Gelu),
                                 reads=[f"ps{pi}"], writes=["src"])

                    def a_s1(tc_):
                        b = tc_ % 2
                        tsl = slice(tc_ * 128, (tc_ + 1) * 128)
                        pi = next_ps()
                        mm(psb[pi][:], [(h[:, kc, tsl], wva[:, kc, :]) for kc in range(8)], reads=["wva", "h"], writes=[f"ps{pi}"])
                        P.op("act", lambda e: e.activation(out=gv[b][:], in_=psb[pi][:], func=AF.Gelu), reads=[f"ps{pi}"], writes=[f"gv{b}"])
                        yield
                        P.op("dve", lambda e: e.bn_stats(out=s1[b][:], in_=gv[b][:]), reads=[f"gv{b}"], writes=[f"s1{b}"])
                        yield
                        P.op("dve", lambda e: e.bn_aggr(out=s2[b][:], in_=s1[b][:]), reads=[f"s1{b}"], writes=[f"s2{b}"])
                        yield
                        P.op("pool", lambda e: e.tensor_scalar(out=s3[b][:], in0=s2[b][:, 1:2], scalar1=EPS, scalar2=None, op0=ALU.add),
                             reads=[f"s2{b}"], writes=[f"s3{b}"])
                        P.op("pool", lambda e: e.tensor_tensor(out=s3[b][:], in0=s3[b][:], in1=mhalf[:], op=ALU.pow),
                             reads=[f"s3{b}", "mhalf"], writes=[f"s3{b}"])
                        yield
                        yield
                        P.op("dve", lambda e: e.scalar_tensor_tensor(out=s4[b][:], in0=s2[b][:, 0:1], scalar=-1.0, in1=s3[b][:],
                                                                   op0=ALU.mult, op1=ALU.mult), reads=[f"s2{b}", f"s3{b}"], writes=[f"s4{b}"])
                        yield
                        P.op("act", lambda e: e.activation(out=vn[b][:], in_=gv[b][:], func=AF.Identity, scale=s3[b][:, 0:1], bias=s4[b][:, 0:1]),
                             reads=[f"gv{b}", f"s3{b}", f"s4{b}"], writes=[f"vn{b}"])
                        yield

                    def a_adv(g):
                        if g is not None:
                            try:
                                next(g)
                            except StopIteration:
                                pass

                    def a_s2(tc_, g):
                        b = tc_ % 2
                        tsl = slice(tc_ * 128, (tc_ + 1) * 128)
                        k = 0
                        for cp in range(4):
                            for gg in range(2):
                                gi = cp * 2 + gg
                                pp = slice(gg * 64, (gg + 1) * 64)
                                pi = next_ps()
                                tb = k % 4
                                k += 1
                                mm(psb[pi][:, 0:128], [(vn[b][:, cp * 128:(cp + 1) * 128], wsT[:, gi, :])], reads=[f"vn{b}", "wsT"], writes=[f"ps{pi}"])
                                P.op("dve", lambda e, pi=pi, pp=pp, cp=cp, tb=tb: e.scalar_tensor_tensor(
                                    out=tmp[tb][pp, :], in0=psb[pi][pp, 0:128], scalar=lpa[pp, 512 + cp:513 + cp], in1=b2t[pp, cp, :],
                                    op0=ALU.mult, op1=ALU.add), reads=[f"ps{pi}", "lpa", "b2t"], writes=[f"tmp{tb}"])
                                P.op("pool", lambda e, pp=pp, cp=cp, tsl=tsl, tb=tb: e.tensor_tensor(
                                    out=cat[pp, cp, tsl], in0=tmp[tb][pp, :], in1=cat[pp, cp, tsl], op=ALU.mult),
                                    reads=[f"tmp{tb}", "src"], writes=["src"])
                                a_adv(g)
                        for _ in range(8):
                            a_adv(g)
                    g0 = a_s1(0)
                    for _ in range(8):
                        a_adv(g0)
                    for i in range(16):
                        gnext = a_s1(i + 1) if i + 1 < 16 else None
                        a_s2(i, gnext)
                    barrier()
                with contextlib.ExitStack() as st2:
                    wq = sb(st2, "wq", [128, 8, 128], BF16)
                    wk = sb(st2, "wk", [128, 8, 128], BF16)
                    wvb = sb(st2, "wvb", [128, 8, 128], BF16)
                    qn = sb(st2, "qz", [128, 2, T], BF16)
                    kn = sb(st2, "kn", [128, T], BF16)
                    P.op("pool", lambda e: e.memset(qn[:], 0.0), writes=["qn"])
                    knp = sb(st2, "knp", [128, T], BF16)
                    vT = sb(st2, "vT", [128, T], BF16)
                    vp = [sb(st2, "vp", [128, 16, 192], BF16) for _ in range(2)]
                    vpp = sb(st2, "vpp", [128, 16, 192], BF16)
                    acc = sb(st2, "acc", [128, 2, T], F32)
                    for vv in vp:
                        P.op("pool", lambda e, vv=vv: e.memset(vv[:, :, 64:128], 1.0), writes=["vp0", "vp1"])
                    sq = [sb(st2, "sqb", [128, 512], BF16) for _ in range(2)]
                    rt = [sb(st2, "rtb", [128, 512], F32) for _ in range(2)]
                    pT = [sb(st2, "pT", [128, 512], BF16) for _ in range(3)]
                    vctr = [0]
                    for hp in range(4):
                        P.dma("pool", wq[:], wv[:, :, 1024 + hp * 128:1024 + (hp + 1) * 128], writes=["wq"])
                        P.dma("pool", wk[:], wv[:, :, 1536 + hp * 128:1536 + (hp + 1) * 128], writes=["wk"])
                        P.dma("pool", wvb[:], wv[:, :, 2048 + hp * 128:2048 + (hp + 1) * 128], writes=["wvb"])
                        if half == 1:
                            P.dma("sp", knp[:], kcar[hp], reads=[f"kcar{hp}"], writes=["knp"])
                        jobs = [(wt_, wn, gcol, dst, dn, tt) for (wt_, wn, gcol, dst, dn) in
                                ((wq, "wq", L_QG, qn, "qn"), (wk, "wk", L_KG, kn, "kn")) for tt in range(NT)]
                        jps = {}

                        def n_s1(j):
                            wt_, wn, gcol, dst, dn, tt = jobs[j]
                            b = j % 2
                            ts = slice(tt * 512, (tt + 1) * 512)
                            pi = next_ps()
                            jps[j] = pi
                            mm(psb[pi][:], [(wt_[:, kc, :], h[:, kc, ts]) for kc in range(8)], reads=[wn, "h"], writes=[f"ps{pi}"])
                            P.op("act", lambda e: e.activation(out=sq[b][:], in_=psb[pi][:], func=AF.Square), reads=[f"ps{pi}"], writes=[f"sqb{b}"])

                        def n_s2(j):
                            wt_, wn, gcol, dst, dn, tt = jobs[j]
                            b = j % 2
                            ts = slice(tt * 512, (tt + 1) * 512)
                            pi = jps[j]
                            p2 = 6
                            mm(psb[p2][:], [(bones[:], sq[b][:])], reads=["bones", f"sqb{b}"], writes=[f"ps{p2}"])
                            P.op("act", lambda e: e.activation(out=rt[b][:], in_=psb[p2][:], func=AF.Ln, bias=EPS, scale=1.0 / 64),
                                 reads=[f"ps{p2}"], writes=[f"rtb{b}"])
                            P.op("act", lambda e: e.activation(out=rt[b][:], in_=rt[b][:], func=AF.Exp, scale=-0.5), reads=[f"rtb{b}"], writes=[f"rtb{b}"])
                            if dn == "qn":
                                for hh in range(2):
                                    hs_ = slice(hh * 64, (hh + 1) * 64)
                                    P.op("dve", lambda e, hh=hh, hs_=hs_: e.scalar_tensor_tensor(
                                        out=dst[hs_, hh, ts], in0=psb[pi][hs_, :], scalar=lpt[hs_, gcol:gcol + 1], in1=rt[b][hs_, :],
                                        op0=ALU.mult, op1=ALU.mult), reads=[f"ps{pi}", f"rtb{b}", "lpt"], writes=[dn])
                            else:
                                P.op("dve", lambda e: e.scalar_tensor_tensor(
                                    out=dst[:, ts], in0=psb[pi][:], scalar=lpt[:, gcol:gcol + 1], in1=rt[b][:], op0=ALU.mult, op1=ALU.mult),
                                    reads=[f"ps{pi}", f"rtb{b}", "lpt"], writes=[dn])
                        for j in range(len(jobs) + 1):
                            if j < len(jobs):
                                n_s1(j)
                            if j >= 1:
                                n_s2(j - 1)
                        if half == 0:
                            P.dma("sp", kcar[hp], kn[:], reads=["kn"], writes=[f"kcar{hp}"])
                        for tt in range(NT):
                            ts = slice(tt * 512, (tt + 1) * 512)
                            pi = next_ps()
                            mm(psb[pi][:], [(wvb[:, kc, :], h[:, kc, ts]) for kc in range(8)], reads=["wvb", "h"], writes=[f"ps{pi}"])
                            P.op("act", lambda e, pi=pi, ts=ts: e.activation(out=vT[:, ts], in_=psb[pi][:], func=AF.Copy),
                                 reads=[f"ps{pi}"], writes=["vT"])
                        for bi, d in enumerate((1, 4, 16)):
                            attn_branch(hp, bi, d, half, vT, qn, kn, knp, vp, vpp, acc, pT, vctr)
                        k_ = 0
                        for hh in range(2):
                            nr = slice(hh * 64, (hh + 1) * 64)
                            dr = slice((1 - hh) * 64, (2 - hh) * 64)
                            for tt in range(NT):
                                ts = slice(tt * 512, (tt + 1) * 512)
                                b = k_ % 2
                                k_ += 1
                                P.op("act", lambda e, b=b, hh=hh, nr=nr, dr=dr, ts=ts: e.activation(out=rt[b][nr, :], in_=acc[dr, hh, ts], func=AF.Ln),
                                     reads=["acc"], writes=[f"rtb{b}"])
                                P.op("act", lambda e, b=b, nr=nr: e.activation(out=rt[b][nr, :], in_=rt[b][nr, :], func=AF.Exp, scale=-1.0),
                                     reads=[f"rtb{b}"], writes=[f"rtb{b}"])
                                P.op("dve", lambda e, b=b, hh=hh, nr=nr, ts=ts, hp=hp: e.tensor_tensor(
                                    out=cat[nr, 4 + hp, ts], in0=acc[nr, hh, ts], in1=rt[b][nr, :], op=ALU.mult),
                                    reads=["acc", f"rtb{b}"], writes=["src"])
                    barrier()

                out_proj(ev_w_out[ei], cat, norm_gcol=(L_GFFN if fuse_ffn_norm else None))

        steps = [(li, layer, half) for li, layer in enumerate(layers) for half in halves]
        xall = lambda c: [xr(c, t_) for t_ in range(NT)]

        def xsrc(si):
            li, layer, half = steps[si]
            return (xT if li == 0 else xs)[half], half

        src0, h0 = xsrc(0)
        for c in range(8):
            P.dma("sp", x[:, c, :], src0[c * 128:(c + 1) * 128, :], reads=[f"xs{h0}_{c}"], writes=xall(c))
        for si, (li, layer, half) in enumerate(steps):
            if half == halves[0]:
                P.dma("sp", lpt[:], lp[layer][:, 0:128], writes=["lpt"])
            last = li == len(layers) - 1
            dst = (outT if last else xs)[half]

            def fin(c, si=si, dst=dst, last=last, half=half):
                P.dma("sp", dst[c * 128:(c + 1) * 128, :], x[:, c, :], reads=xall(c), writes=[f"xs{half}_{c}"], final=last)
                if si + 1 < len(steps):
                    nsrc, nh = xsrc(si + 1)
                    P.dma("sp", x[:, c, :], nsrc[c * 128:(c + 1) * 128, :], reads=[f"xs{nh}_{c}"], writes=xall(c))
            if "mix" in phases:
                rms_norm(L_GMIX)
                if layer % 2 == 0:
                    even(layer // 2, half)
                else:
                    gla(layer // 2, half)
            if "ffn" in phases:
                ffn(layer, half, fin, normed=fuse_ffn_norm)
            else:
                for c in range(8):
                    fin(c)
                barrier()
        P.emit()
    return nc


def _consts():
    c = np.zeros((128, C_N), np.float32)
    p = np.arange(128)[:, None]
    i = np.arange(128)[None, :]
    c[:, C_ID:C_ID + 128] = (p == i)
    c[:, C_BONES:C_BONES + 128] = (p // 64 == i // 64)
    c[:, C_MPREV:C_MPREV + 128] = np.where(i <= p, 0.0, NEGM)
    c[:, C_MCUR:C_MCUR + 128] = np.where(p <= i, 0.0, NEGM)
    c[:, C_TRIL:C_TRIL + 128] = (p <= i)
    c[:, C_M2:C_M2 + 128] = (p <= i) & (p // 64 == i // 64)
    c[:, C_RESET:C_RESET + 512] = (np.arange(512)[None, :] % 64 != 0)
    return c


def _fm(v, n):
    return np.ascontiguousarray(v.reshape(n, 128).T)


def _layer_pack(inp):
    lp = np.zeros((4, 128, L_N), np.float32)
    for l in range(4):
        lp[l, :, L_GMIX:L_GMIX + 8] = _fm(inp["norm_mix_g"][l], 8)
        lp[l, :, L_GFFN:L_GFFN + 8] = _fm(inp["norm_ffn_g"][l], 8)
        for k, off in enumerate((L_CW0, L_CW1, L_CW2)):
            lp[l, :, off:off + NFC] = _fm(inp["ffn_conv_w"][l, k], NFC)
        lp[l, :, L_CB:L_CB + NFC] = _fm(inp["ffn_conv_b"][l], NFC)
        if l % 2 == 0:
            e = l // 2
            lp[l, :, L_QG] = np.tile(inp["ev_q_g"][e], 2)
            lp[l, :, L_KG] = np.tile(inp["ev_k_g"][e], 2)
            bs = inp["ev_a_bs"][e]
            for cp in range(4):
                lp[l, 0:64, L_BIAS + cp * 128:L_BIAS + (cp + 1) * 128] = bs[2 * cp][None, :]
                lp[l, 64:128, L_BIAS + cp * 128:L_BIAS + (cp + 1) * 128] = bs[2 * cp + 1][None, :]
            lp[l, :, L_LNG:L_LNG + 512] = inp["ev_a_ln_g"][e][None, :]
            lp[l, :, L_LNB:L_LNB + 512] = inp["ev_a_ln_b"][e][None, :]
            lp[l, :, L_LNGF:L_LNGF + 4] = _fm(inp["ev_a_ln_g"][e], 4)
            lp[l, :, L_LNBF:L_LNBF + 4] = _fm(inp["ev_a_ln_b"][e], 4)
        else:
            o = l // 2
            lp[l, :, L_BA:L_BA + 4] = _fm(inp["od_b_a"][o], 4)
            lp[l, :, L_HG:L_HG + 2] = _fm(inp["od_head_g"][o], 2)
    return lp


def make_in_maps(inp, seqs):
    consts = _consts()
    lp = _layer_pack(inp)
    wsT = np.ascontiguousarray(np.transpose(inp["ev_a_ws"], (0, 3, 1, 2)))
    shared = {
        "consts": consts, "lp": lp,
        "ev_w_in": np.ascontiguousarray(inp["ev_w_in"]), "ev_wsT": wsT,
        "ev_w_out": np.ascontiguousarray(inp["ev_w_out"]),
        "od_w_in": np.ascontiguousarray(inp["od_w_in"]), "od_w_a2": np.ascontiguousarray(inp["od_w_a2"]),
        "od_w_out": np.ascontiguousarray(inp["od_w_out"]),
        "ffn_w_gate": np.ascontiguousarray(inp["ffn_w_gate"]), "ffn_w_up": np.ascontiguousarray(inp["ffn_w_up"]),
        "ffn_w_down": np.ascontiguousarray(inp["ffn_w_down"]),
    }
    maps = []
    for b in seqs:
        xb = np.asarray(inp["x"][b], np.float32)
        xTt = np.ascontiguousarray(xb.reshape(2, T, 1024).transpose(0, 2, 1))
        m = dict(shared)
        m["xT"] = xTt
        maps.append(m)
    return maps


_NC = {}


def kernel(**inputs):
    inp = {k: np.asarray(v) for k, v in inputs.items()}
    if "full" not in _NC:
        _NC["full"] = build()
    nc = _NC["full"]
    maps = make_in_maps(inp, range(4))
    res = run_bass_kernel_spmd(nc, maps, core_ids=list(range(4)))
    out = np.empty((4, 2 * T, 1024), np.float32)
    for b in range(4):
        o = res.results[b]["outT"]
        out[b] = o.transpose(0, 2, 1).reshape(2 * T, 1024)
    return out
```

```python
import contextlib
import numpy as np
import concourse.bass as bass
import concourse.mybir as mybir
from concourse.bass_utils import run_bass_kernel_spmd

F32 = mybir.dt.float32
BF16 = mybir.dt.bfloat16
AF = mybir.ActivationFunctionType
ALU = mybir.AluOpType

ENGS = ("pe", "act", "dve", "pool", "sp")
SEM_CHUNK = 12000
DMA_RING = 6
EPS = 1e-6
T = 2048
NT = 4
DFF = 2816
NFC = 22
NEGM = -30000.0


class Op:
    __slots__ = ("eng", "fn", "reads", "writes", "is_dma", "deps", "signal",
                 "sem", "val", "ring_prev", "idx")


class Prog:
    def __init__(self, nc):
        self.nc = nc
        self.ops = []
        self.final_waits = []

    def op(self, eng, fn, reads=(), writes=(), is_dma=False, barrier=False):
        o = Op()
        o.eng, o.fn, o.is_dma = eng, fn, is_dma
        psr_ = tuple(r for r in reads if r.startswith("ps") and r[2:].isdigit() or r == "pst")
        o.reads = tuple(reads) + (() if barrier else ("PHASE",))
        o.writes = tuple(writes) + psr_ + (("PHASE",) if barrier else ())
        o.deps = []
        o.signal = is_dma
        o.sem = None
        o.val = 0
        o.ring_prev = None
        o.idx = len(self.ops)
        self.ops.append(o)
        return o

    def dma(self, eng, out, in_, reads=(), writes=(), final=False):
        o = self.op(eng, lambda e, out=out, in_=in_: e.dma_start(out=out, in_=in_),
                    reads, writes, is_dma=True)
        if final:
            self.final_waits.append(o)
        return o

    def _analyze(self):
        last_w = {}
        readers = {}
        ops = self.ops
        for o in ops:
            deps = {}
            for r in o.reads:
                lw = last_w.get(r)
                if lw is not None:
                    deps[lw] = "raw"
            for w in o.writes:
                lw = last_w.get(w)
                if lw is not None and lw not in deps:
                    deps[lw] = "waw"
                for rd in readers.get(w, ()):
                    if rd not in deps:
                        deps[rd] = "war"
            deps.pop(o.idx, None)
            for r in o.reads:
                readers.setdefault(r, []).append(o.idx)
            for w in o.writes:
                last_w[w] = o.idx
                readers[w] = []
            best = {}
            dma_deps = []
            for d, kind in deps.items():
                od = ops[d]
                if od.is_dma:
                    dma_deps.append(d)
                    continue
                if od.eng == o.eng and not o.is_dma:
                    if od.eng in ("pe", "sp"):
                        continue
                if od.eng not in best or best[od.eng] < d:
                    best[od.eng] = d
            o.deps = sorted(list(best.values()) + dma_deps)
        waited = {e: {} for e in ENGS}
        waited_dma = {e: set() for e in ENGS}
        for o in ops:
            nd = []
            for d in o.deps:
                od = ops[d]
                if od.is_dma:
                    if d in waited_dma[o.eng]:
                        continue
                    waited_dma[o.eng].add(d)
                    nd.append(d)
                else:
                    if waited[o.eng].get(od.eng, -1) >= d:
                        continue
                    waited[o.eng][od.eng] = d
                    nd.append(d)
            o.deps = nd
            for d in nd:
                ops[d].signal = True

    def emit(self):
        nc = self.nc
        self._analyze()
        with contextlib.ExitStack() as stack:
            cnt = {e: 0 for e in ENGS}
            sems = {e: [] for e in ENGS}
            rings = {e: [] for e in ENGS}
            ring_cnt = {e: 0 for e in ENGS}
            ring_ord = {}
            ring_last = {}
            for o in self.ops:
                if o.is_dma:
                    j = ring_cnt[o.eng] % DMA_RING
                    ring_cnt[o.eng] += 1
                    if len(rings[o.eng]) <= j:
                        rings[o.eng].append(stack.enter_context(nc.semaphore(f"dr_{o.eng}_{j}")))
                    key = (o.eng, j)
                    ring_ord[key] = ring_ord.get(key, 0) + 1
                    o.sem = rings[o.eng][j]
                    o.val = 16 * ring_ord[key]
                    o.ring_prev = ring_last.get(key)
                    ring_last[key] = o
                elif o.signal:
                    c = cnt[o.eng]
                    k = c // SEM_CHUNK
                    if len(sems[o.eng]) <= k:
                        sems[o.eng].append(stack.enter_context(nc.semaphore(f"cs_{o.eng}_{k}")))
                    o.sem = sems[o.eng][k]
                    o.val = (c % SEM_CHUNK) + 1
                    cnt[o.eng] += 1
            block = stack.enter_context(nc.Block())
            ops = self.ops
            finals = self.final_waits

            def run(eng_name):
                def body(e):
                    for o in ops:
                        if o.eng != eng_name:
                            continue
                        if o.ring_prev is not None:
                            e.wait_ge(o.ring_prev.sem, o.ring_prev.val)
                        for d in o.deps:
                            od = ops[d]
                            e.wait_ge(od.sem, od.val)
                        ins = o.fn(e)
                        if o.signal:
                            ins.then_inc(o.sem, 16 if o.is_dma else 1)
                    for o in finals:
                        if o.eng == eng_name:
                            e.wait_ge(o.sem, o.val)
                return body

            block.tensor(run("pe"))
            block.scalar(run("act"))
            block.vector(run("dve"))
            block.gpsimd(run("pool"))
            block.sync(run("sp"))


C_ID, C_BONES, C_MPREV, C_MCUR, C_TRIL, C_M2, C_RESET, C_N = 0, 128, 256, 384, 512, 640, 768, 1280
L_GMIX, L_GFFN, L_CW0, L_CW1, L_CW2, L_CB, L_X = 0, 8, 16, 38, 60, 82, 104
L_QG, L_KG, L_BIAS, L_LNG, L_LNB = 104, 105, 106, 106 + 512, 106 + 1024
L_BA, L_HG = 104, 108
L_LNGF, L_LNBF = 106 + 1536, 106 + 1536 + 4
L_N = 106 + 1536 + 8


def build(layers=(0, 1, 2, 3), halves=(0, 1), phases=("mix", "ffn")):
    nc = bass.Bass("TRN2", target_bir_lowering=False)

    def D(name, shape, kind="ExternalInput", dt=F32):
        return nc.dram_tensor(name, shape, dt, kind=kind).ap()

    xT = D("xT", [2, 1024, T])
    outT = D("outT", [2, 1024, T], "ExternalOutput")
    xs = D("xs", [2, 1024, T], "Internal")
    consts = D("consts", [128, C_N])
    lp = D("lp", [4, 128, L_N])
    has_even = any(l % 2 == 0 for l in layers) and "mix" in phases
    has_odd = any(l % 2 == 1 for l in layers) and "mix" in phases
    has_ffn = "ffn" in phases
    ev_w_in = D("ev_w_in", [2, 1024, 2560]) if has_even else None
    ev_wsT = D("ev_wsT", [2, 128, 8, 128]) if has_even else None
    ev_w_out = D("ev_w_out", [2, 1024, 1024]) if has_even else None
    od_w_in = D("od_w_in", [2, 1024, 3088]) if has_odd else None
    od_w_a2 = D("od_w_a2", [2, 16, 512]) if has_odd else None
    od_w_out = D("od_w_out", [2, 1024, 1024]) if has_odd else None
    w_gate = D("ffn_w_gate", [4, 1024, DFF]) if has_ffn else None
    w_up = D("ffn_w_up", [4, 1024, DFF]) if has_ffn else None
    w_down = D("ffn_w_down", [4, DFF, 1024]) if has_ffn else None
    kcar = D("kcar", [4, 128, T], "Internal", BF16)
    vcar = D("vcar", [3, 4, 128, 16 * 192], "Internal", BF16)

    P = Prog(nc)
    fuse_ffn_norm = ("mix" in phases) and ("ffn" in phases)
    top = contextlib.ExitStack()
    uid = [0]

    def sb(stack, name, shape, dt):
        uid[0] += 1
        return stack.enter_context(nc.sbuf_tensor(f"{name}_{uid[0]}", shape, dt))

    with top:
        x = sb(top, "x", [128, 8, T], F32)
        h = sb(top, "h", [128, 8, T], BF16)
        cst = sb(top, "cst", [128, C_N - C_TRIL], F32)
        lpt = sb(top, "lpt", [128, 128], F32)
        ident = sb(top, "ident", [128, 128], BF16)
        ones = sb(top, "ones", [128, 128], BF16)
        bones = sb(top, "bones", [128, 128], BF16)
        maskb4 = sb(top, "maskb4", [128, 512], BF16)
        state = sb(top, "state", [128, 4, 256], F32)
        gtail = sb(top, "gtail", [128, NFC, 2], F32)
        dummy = sb(top, "dummy", [128, 8], F32)
        psb = [top.enter_context(nc.psum_tensor(f"ps{i}", [128, 512], F32)) for i in range(7)]
        pst = top.enter_context(nc.psum_tensor("pst", [128, 1024], BF16))

        def barrier():
            P.op("pool", lambda e: e.memset(dummy[:], 0.0), barrier=True)

        def mm(out, pairs, reads, writes, start=True):
            def fn(e):
                n = len(pairs)
                ins = None
                for i, (l, r) in enumerate(pairs):
                    ins = e.matmul(out, l, r, start=(start and i == 0), stop=(i == n - 1),
                                   skip_group_check=True)
                return ins
            P.op("pe", fn, reads, writes)

        psr = [0]

        def next_ps(n=1):
            i = psr[0] % 6
            psr[0] += 1
            return i

        P.dma("sp", cst[:], consts[:, C_TRIL:C_N], writes=["cst"])
        P.dma("pool", ident[:], consts[:, C_ID:C_ID + 128], writes=["ident"])
        P.dma("pool", bones[:], consts[:, C_BONES:C_BONES + 128], writes=["bones"])
        for q_, c_ in enumerate((C_MPREV, C_MPREV, C_MCUR, C_MCUR)):
            P.dma("pool", maskb4[:, q_ * 128:(q_ + 1) * 128], consts[:, c_:c_ + 128], writes=["maskb4"])
        P.op("dve", lambda e: e.memset(ones[:], 1.0), writes=["ones"])

        def xr(c, tt):
            return f"x{c}_{tt}"

        def norm_tile(tt, gcol, sq, rt, rs):
            b = tt % 2
            ts = slice(tt * 512, (tt + 1) * 512)
            pi = next_ps()
            P.op("act", lambda e: e.activation(out=sq[b][:], in_=x[:, :, ts], func=AF.Square),
                 reads=[xr(k, tt) for k in range(8)], writes=[f"sq{b}"])
            mm(psb[pi][:], [(ones[:], sq[b][:, kc, :]) for kc in range(8)],
               reads=[f"sq{b}", "ones"], writes=[f"ps{pi}"])
            P.op("act", lambda e: e.activation(out=rt[b][:], in_=psb[pi][:], func=AF.Ln, bias=EPS, scale=1.0 / 1024),
                 reads=[f"ps{pi}"], writes=[f"rt{b}"])
            P.op("act", lambda e: e.activation(out=rs[b][:], in_=rt[b][:], func=AF.Exp, scale=-0.5),
                 reads=[f"rt{b}"], writes=[f"rs{b}"])
            for kc in range(8):
                P.op("dve", lambda e, kc=kc: e.scalar_tensor_tensor(
                    out=h[:, kc, ts], in0=x[:, kc, ts], scalar=lpt[:, gcol + kc:gcol + kc + 1],
                    in1=rs[b][:], op0=ALU.mult, op1=ALU.mult),
                    reads=[xr(kc, tt), f"rs{b}", "lpt"], writes=[f"h{kc}_{tt}"])

        def rms_norm(gcol):
            with contextlib.ExitStack() as st:
                sq = [sb(st, "sq", [128, 8, 512], BF16) for _ in range(2)]
                rt = [sb(st, "rt", [128, 512], F32) for _ in range(2)]
                rs = [sb(st, "rs", [128, 512], F32) for _ in range(2)]
                for tt in range(NT):
                    norm_tile(tt, gcol, sq, rt, rs)
                barrier()

        def out_proj(w_dram, src, nkc=8, norm_gcol=None):
            with contextlib.ExitStack() as st:
                wb = [sb(st, "wo", [128, nkc, 256], BF16) for _ in range(4)]
                wv = w_dram.rearrange("(kc p) n -> p kc n", p=128)
                if norm_gcol is not None:
                    sq = [sb(st, "sq", [128, 8, 512], BF16) for _ in range(2)]
                    rt = [sb(st, "rt", [128, 512], F32) for _ in range(2)]
                    rs = [sb(st, "rs", [128, 512], F32) for _ in range(2)]
                for blk in range(4):
                    P.dma("pool", wb[blk][:], wv[:, :, blk * 256:(blk + 1) * 256], writes=[f"wo{blk}"])

                def op_tile(tt):
                    ts = slice(tt * 512, (tt + 1) * 512)
                    for blk in range(4):
                        for dc in range(2):
                            dch = blk * 2 + dc
                            pi = next_ps()
                            mm(psb[pi][:], [(wb[blk][:, kc, dc * 128:(dc + 1) * 128], src[:, kc, ts]) for kc in range(nkc)],
                               reads=[f"wo{blk}", "src"], writes=[f"ps{pi}"])
                            P.op("dve", lambda e, pi=pi, dch=dch: e.tensor_tensor(
                                out=x[:, dch, ts], in0=x[:, dch, ts], in1=psb[pi][:], op=ALU.add),
                                reads=[f"ps{pi}", xr(dch, tt)], writes=[xr(dch, tt)])
                for i in range(NT + 1):
                    if i < NT:
                        op_tile(i)
                    if i >= 1 and norm_gcol is not None:
                        norm_tile(i - 1, norm_gcol, sq, rt, rs)
                barrier()

        def ffn(layer, half, fin=None, normed=False):
            if not normed:
                rms_norm(L_GFFN)
            wgv = w_gate[layer].rearrange("(kc p) n -> p kc n", p=128)
            wuv = w_up[layer].rearrange("(kc p) n -> p kc n", p=128)
            wdv = w_down[layer].rearrange("(kc p) n -> p kc n", p=128)
            if half == 0:
                P.op("dve", lambda e: e.memset(gtail[:], 0.0), writes=["gtail"])
            with contextlib.ExitStack() as st:
                hid = sb(st, "hid", [128, NFC, 1024], BF16)
                wg = [sb(st, "wg", [128, 8, 256], BF16) for _ in range(2)]
                wu = [sb(st, "wu", [128, 8, 256], BF16) for _ in range(2)]
                gs = [sb(st, "gs", [128, 514], F32) for _ in range(2)]
                ac = [sb(st, "ac", [128, 512], F32) for _ in range(2)]
                wd = [sb(st, "wd", [128, NFC, 256], BF16) for _ in range(2)]
                it = 0
                for sh in range(2):
                    for blk in range(11):
                        b = blk % 2
                        P.dma("pool", wg[b][:], wgv[:, :, blk * 256:(blk + 1) * 256], writes=[f"wg{b}"])
                        P.dma("pool", wu[b][:], wuv[:, :, blk * 256:(blk + 1) * 256], writes=[f"wu{b}"])
                        for fc2 in range(2):
                            fc = blk * 2 + fc2
                            cs = slice(fc2 * 128, (fc2 + 1) * 128)
                            for t2 in range(2):
                                tt = sh * 2 + t2
                                ts = slice(tt * 512, (tt + 1) * 512)
                                hs = slice(t2 * 512, (t2 + 1) * 512)
                                s = it % 2
                                it += 1
                                pg = next_ps()
                                pu = next_ps()
                                mm(psb[pg][:], [(wg[b][:, kc, cs], h[:, kc, ts]) for kc in range(8)],
                                   reads=[f"wg{b}", "h"], writes=[f"ps{pg}"])
                                mm(psb[pu][:], [(wu[b][:, kc, cs], h[:, kc, ts]) for kc in range(8)],
                                   reads=[f"wu{b}", "h"], writes=[f"ps{pu}"])
                                P.op("dve", lambda e, s=s, fc=fc: e.tensor_copy(out=gs[s][:, 0:2], in_=gtail[:, fc, :]),
                                     reads=[f"gtail{fc}", "gtail"], writes=[f"gs{s}"])
                                P.op("act", lambda e, s=s, pg=pg: e.activation(out=gs[s][:, 2:514], in_=psb[pg][:], func=AF.Copy),
                                     reads=[f"ps{pg}"], writes=[f"gs{s}b"])
                                P.op("dve", lambda e, s=s, fc=fc: e.tensor_copy(out=gtail[:, fc, :], in_=gs[s][:, 512:514]),
                                     reads=[f"gs{s}b"], writes=[f"gtail{fc}"])
                                P.op("dve", lambda e, s=s, fc=fc: e.tensor_scalar(
                                    out=ac[s][:], in0=gs[s][:, 2:514], scalar1=lpt[:, L_CW2 + fc:L_CW2 + fc + 1],
                                    scalar2=lpt[:, L_CB + fc:L_CB + fc + 1], op0=ALU.mult, op1=ALU.add),
                                    reads=[f"gs{s}b", "lpt"], writes=[f"ac{s}"])
                                P.op("dve", lambda e, s=s, fc=fc: e.scalar_tensor_tensor(
                                    out=ac[s][:], in0=gs[s][:, 1:513], scalar=lpt[:, L_CW1 + fc:L_CW1 + fc + 1],
                                    in1=ac[s][:], op0=ALU.mult, op1=ALU.add),
                                    reads=[f"gs{s}b", f"gs{s}", f"ac{s}", "lpt"], writes=[f"ac{s}"])
                                P.op("dve", lambda e, s=s, fc=fc: e.scalar_tensor_tensor(
                                    out=ac[s][:], in0=gs[s][:, 0:512], scalar=lpt[:, L_CW0 + fc:L_CW0 + fc + 1],
                                    in1=ac[s][:], op0=ALU.mult, op1=ALU.add),
                                    reads=[f"gs{s}b", f"gs{s}", f"ac{s}", "lpt"], writes=[f"ac{s}"])
                                P.op("act", lambda e, s=s: e.activation(out=ac[s][:], in_=ac[s][:], func=AF.Silu),
                                     reads=[f"ac{s}"], writes=[f"ac{s}"])
                                P.op("dve", lambda e, s=s, pu=pu, fc=fc, hs=hs: e.tensor_tensor(
                                    out=hid[:, fc, hs], in0=ac[s][:], in1=psb[pu][:], op=ALU.mult),
                                    reads=[f"ac{s}", f"ps{pu}"], writes=["hid"])
                    for blk in range(4):
                        b = blk % 2
                        P.dma("pool", wd[b][:], wdv[:, :, blk * 256:(blk + 1) * 256], writes=[f"wd{b}"])
                        for dc in range(2):
                            dch = blk * 2 + dc
                            for t2 in range(2):
                                tt = sh * 2 + t2
                                ts = slice(tt * 512, (tt + 1) * 512)
                                hs = slice(t2 * 512, (t2 + 1) * 512)
                                pi = next_ps()
                                mm(psb[pi][:], [(wd[b][:, kc, dc * 128:(dc + 1) * 128], hid[:, kc, hs]) for kc in range(NFC)],
                                   reads=[f"wd{b}", "hid"], writes=[f"ps{pi}"])
                                P.op("dve", lambda e, pi=pi, dch=dch, ts=ts: e.tensor_tensor(
                                    out=x[:, dch, ts], in0=x[:, dch, ts], in1=psb[pi][:], op=ALU.add),
                                    reads=[f"ps{pi}", xr(dch, tt)], writes=[xr(dch, tt)])
                            if sh == 1 and fin is not None:
                                fin(dch)
                barrier()

        def gla(o, half):
            wv = od_w_in[o].rearrange("(kc p) n -> p kc n", p=128)
            with contextlib.ExitStack() as st:
                cat = sb(st, "cat", [128, 8, T], BF16)
                with contextlib.ExitStack() as st2:
                    wq = sb(st2, "wq", [128, 8, 128], BF16)
                    wk = sb(st2, "wk", [128, 8, 128], BF16)
                    wvv = sb(st2, "wvv", [128, 8, 256], BF16)
                    wr = sb(st2, "wr", [128, 8, 256], BF16)
                    wga = sb(st2, "wga", [128, 8, 16], BF16)
                    wa2 = sb(st2, "wa2", [16, 512], F32)
                    nba = sb(st2, "nba", [128, 4], F32)
                    gaT = sb(st2, "gaT", [16, T], F32)
                    e1 = sb(st2, "e1", [128, 512], F32)
                    b16 = sb(st2, "b16", [128, 512], F32)
                    eb = sb(st2, "eb", [128, 512], F32)
                    enb = sb(st2, "enb", [128, 512], F32)
                    ed = sb(st2, "ed", [128, 512], F32)
                    ks = sb(st2, "ks", [128, 512], BF16)
                    dec = [sb(st2, "dec", [128, 8], F32) for _ in range(2)]
                    qt = [sb(st2, "qt", [128, 512], BF16) for _ in range(2)]
                    kt = [sb(st2, "kt", [128, 512], BF16) for _ in range(2)]
                    ksT = [sb(st2, "ksT", [128, 4, 128], BF16) for _ in range(2)]
                    vtm = [sb(st2, "vtm", [128, 4, 256], BF16) for _ in range(2)]
                    sr = [sb(st2, "sr", [128, 2, 512], BF16) for _ in range(2)]
                    aT = [sb(st2, "aT", [128, 128], BF16) for _ in range(2)]
                    ot = sb(st2, "ot", [128, 2, 512], F32)
                    osq = sb(st2, "osq", [128, 2, 512], BF16)
                    ort = sb(st2, "ort", [128, 512], F32)
                    stb = [sb(st2, "stb", [128, 256], BF16) for _ in range(2)]
                    P.dma("pool", wga[:], wv[:, :, 3072:3088], writes=["wga"])
                    P.dma("sp", wa2[:], od_w_a2[o], writes=["wa2"])
                    P.op("dve", lambda e: e.tensor_scalar(out=nba[:], in0=lpt[:, L_BA:L_BA + 4], scalar1=-1.0, scalar2=None,
                                                          op0=ALU.mult), reads=["lpt"], writes=["nba"])
                    if half == 0:
                        P.op("dve", lambda e: e.memset(state[:], 0.0), writes=["state"])
                    for tt in range(NT):
                        ts = slice(tt * 512, (tt + 1) * 512)
                        pi = next_ps()
                        mm(psb[pi][0:16, :], [(wga[:, kc, :], h[:, kc, ts]) for kc in range(8)],
                           reads=["wga", "h"], writes=[f"ps{pi}"])
                        P.op("act", lambda e, pi=pi, ts=ts: e.activation(out=gaT[:, ts], in_=psb[pi][0:16, :], func=AF.Copy),
                             reads=[f"ps{pi}"], writes=["gaT"])
                    seq = [(hd, tt) for hd in range(4) for tt in range(NT)]
                    pctr = [0]
                    cctr = [0]

                    def ps_p():
                        pctr[0] += 1
                        return pctr[0] % 3

                    def ps_o():
                        cctr[0] += 1
                        return 3 + cctr[0] % 2

                    def mm_split(out, pairs, reads, writes, n=4):
                        for k0 in range(0, len(pairs), n):
                            mm(out, pairs[k0:k0 + n], reads=reads, writes=writes, start=(k0 == 0))
                            yield

                    def pgen(i):
                        hd, tt = seq[i]
                        z = i % 2
                        ts = slice(tt * 512, (tt + 1) * 512)
                        if tt == 0:
                            P.dma("pool", wq[:], wv[:, :, hd * 128:(hd + 1) * 128], writes=["wq"])
                            P.dma("pool", wk[:], wv[:, :, 512 + hd * 128:512 + (hd + 1) * 128], writes=["wk"])
                            P.dma("pool", wvv[:], wv[:, :, 1024 + hd * 256:1024 + (hd + 1) * 256], writes=["wvv"])
                            P.dma("pool", wr[:], wv[:, :, 2048 + hd * 256:2048 + (hd + 1) * 256], writes=["wr"])
                        pi = ps_p()
                        mm(psb[pi][:], [(wa2[:, hd * 128:(hd + 1) * 128], gaT[:, ts])], reads=["wa2", "gaT"], writes=[f"ps{pi}"])
                        P.op("act", lambda e: e.activation(out=e1[:], in_=psb[pi][:], func=AF.Exp, bias=nba[:, hd:hd + 1], scale=-1.0),
                             reads=[f"ps{pi}", "nba"], writes=["e1"])
                        yield
                        P.op("act", lambda e: e.activation(out=e1[:], in_=e1[:], func=AF.Ln, bias=1.0, scale=1.0),
                             reads=["e1"], writes=["e1"])
                        yield
                        P.op("dve", lambda e: e.tensor_tensor_scan(out=b16[:], data0=cst[:, 256:768],
                                                                 data1=e1[:], initial=0.0, op0=ALU.mult, op1=ALU.add),
                             reads=["e1", "cst"], writes=["b16"])
                        yield
                        P.op("dve", lambda e: e.tensor_tensor(
                            out=e1[:].rearrange("p (c j) -> p c j", j=64),
                            in0=b16[:].rearrange("p (c j) -> p c j", j=64)[:, :, 63:64].to_broadcast([128, 8, 64]),
                            in1=b16[:].rearrange("p (c j) -> p c j", j=64), op=ALU.subtract),
                            reads=["b16", "e1"], writes=["e1"])
                        P.op("act", lambda e: e.activation(out=eb[:], in_=b16[:], func=AF.Exp, scale=-1.0 / 16), reads=["b16"], writes=["eb"])
                        yield
                        P.op("act", lambda e: e.activation(out=enb[:], in_=b16[:], func=AF.Exp, scale=1.0 / 16), reads=["b16"], writes=["enb"])
                        yield
                        P.op("act", lambda e: e.activation(out=ed[:], in_=e1[:], func=AF.Exp, scale=-1.0 / 16), reads=["e1"], writes=["ed"])
                        P.op("act", lambda e: e.activation(out=dec[z][:], in_=b16[:, 63:512:64], func=AF.Exp, scale=-1.0 / 16),
                             reads=["b16"], writes=[f"dec{z}"])
                        yield
                        pq = ps_p()
                        yield from mm_split(psb[pq][:], [(wq[:, kc, :], h[:, kc, ts]) for kc in range(8)], ["wq", "h"], [f"ps{pq}"])
                        P.op("dve", lambda e: e.scalar_tensor_tensor(out=qt[z][:], in0=psb[pq][:], scalar=128.0 ** -0.5,
                                                                   in1=eb[:], op0=ALU.mult, op1=ALU.mult),
                             reads=[f"ps{pq}", "eb"], writes=[f"qt{z}"])
                        yield
                        pk = ps_p()
                        yield from mm_split(psb[pk][:], [(wk[:, kc, :], h[:, kc, ts]) for kc in range(8)], ["wk", "h"], [f"ps{pk}"])
                        P.op("dve", lambda e: e.tensor_tensor(out=kt[z][:], in0=psb[pk][:], in1=enb[:], op=ALU.mult),
                             reads=[f"ps{pk}", "enb"], writes=[f"kt{z}"])
                        yield
                        P.op("dve", lambda e: e.tensor_tensor(out=ks[:], in0=psb[pk][:], in1=ed[:], op=ALU.mult),
                             reads=[f"ps{pk}", "ed"], writes=["ks"])
                        yield

                        def tr(e):
                            ins = None
                            for bl in range(4):
                                ins = e.transpose(pst[:, bl * 128:(bl + 1) * 128], ks[:, bl * 128:(bl + 1) * 128], ident[:])
                            return ins
                        P.op("pe", tr, reads=["ks", "ident"], writes=["pst"])
                        P.op("act", lambda e: e.activation(out=ksT[z][:].rearrange("p a b -> p (a b)"), in_=pst[:, 0:512], func=AF.Copy),
                             reads=["pst"], writes=[f"ksT{z}"])
                        yield
                        for bp in range(2):
                            pv = ps_p()
                            for b2 in range(2):
                                bl = bp * 2 + b2
                                t0 = tt * 512 + bl * 128
                                pairs = [(h[:, kc, t0:t0 + 128], wvv[:, kc, :]) for kc in range(8)]
                                for k0 in (0, 4):
                                    mm(psb[pv][:, b2 * 256:(b2 + 1) * 256], pairs[k0:k0 + 4], reads=["wvv", "h"], writes=[f"ps{pv}"],
                                       start=(k0 == 0 and b2 == 0))
                                    yield
                            P.op("act", lambda e, pv=pv, bp=bp: e.activation(
                                out=vtm[z][:, bp * 2:bp * 2 + 2, :].rearrange("p a b -> p (a b)"), in_=psb[pv][:], func=AF.Copy),
                                reads=[f"ps{pv}"], writes=[f"vtm{z}"])
                            yield
                        for dvc in range(2):
                            pr = ps_p()
                            yield from mm_split(psb[pr][:], [(wr[:, kc, dvc * 128:(dvc + 1) * 128], h[:, kc, ts]) for kc in range(8)],
                                                ["wr", "h"], [f"ps{pr}"])
                            P.op("act", lambda e, pr=pr, dvc=dvc: e.activation(out=sr[z][:, dvc, :], in_=psb[pr][:], func=AF.Silu),
                                 reads=[f"ps{pr}"], writes=[f"sr{z}"])
                            yield

                    def adv(g, n=1):
                        if g is None:
                            return
                        for _ in range(n):
                            try:
                                next(g)
                            except StopIteration:
                                return

                    def cstage(i, g):
                        hd, tt = seq[i]
                        z = i % 2
                        ts = slice(tt * 512, (tt + 1) * 512)
                        for bl in range(4):
                            bs_ = slice(bl * 128, (bl + 1) * 128)
                            az = bl % 2
                            pa = 5
                            mm(psb[pa][:, 0:128], [(kt[z][:, bs_], qt[z][:, bs_])], reads=[f"kt{z}", f"qt{z}"], writes=[f"ps{pa}"])
                            P.op("dve", lambda e, pa=pa, az=az: e.tensor_tensor(out=aT[az][:], in0=psb[pa][:, 0:128],
                                                                             in1=cst[:, 128:256], op=ALU.mult),
                                 reads=[f"ps{pa}", "cst"], writes=[f"aT{az}"])
                            adv(g)
                            po = ps_o()
                            for dvc in range(2):
                                mm(psb[po][:, dvc * 128:(dvc + 1) * 128], [(vtm[z][:, bl, dvc * 128:(dvc + 1) * 128], aT[az][:])],
                                   reads=[f"vtm{z}", f"aT{az}"], writes=[f"ps{po}"], start=(dvc == 0))
                            for c2 in range(2):
                                ci = bl * 2 + c2
                                cs = slice(bl * 128 + c2 * 64, bl * 128 + (c2 + 1) * 64)
                                pp = slice(c2 * 64, (c2 + 1) * 64)
                                P.op("dve", lambda e, c2=c2: e.tensor_copy(out=stb[c2][:], in_=state[:, hd, :]),
                                     reads=["state"], writes=[f"stb{c2}"])
                                mm(psb[6][:, 0:256], [(ksT[z][pp, bl, :], vtm[z][pp, bl, :])], reads=[f"ksT{z}", f"vtm{z}"], writes=["ps6"])
                                adv(g)
                                P.op("dve", lambda e, ci=ci: e.scalar_tensor_tensor(
                                    out=state[:, hd, :], in0=state[:, hd, :], scalar=dec[z][:, ci:ci + 1], in1=psb[6][:, 0:256],
                                    op0=ALU.mult, op1=ALU.add), reads=["state", f"dec{z}", "ps6", f"stb{c2}"], writes=["state"])
                                for dvc in range(2):
                                    mm(psb[po][:, dvc * 128 + c2 * 64:dvc * 128 + (c2 + 1) * 64],
                                       [(stb[c2][:, dvc * 128:(dvc + 1) * 128], qt[z][:, cs])],
                                       reads=[f"stb{c2}", f"qt{z}"], writes=[f"ps{po}"], start=False)
                                adv(g, 2)
                            P.op("act", lambda e, po=po, bs_=bs_: e.activation(
                                out=ot[:, :, bs_], in_=psb[po][:, 0:256].rearrange("p (a b) -> p a b", a=2), func=AF.Copy),
                                reads=[f"ps{po}"], writes=["ot"])
                        P.op("act", lambda e: e.activation(out=osq[:], in_=ot[:], func=AF.Square), reads=["ot"], writes=["osq"])
                        pn = 5
                        mm(psb[pn][:], [(ones[:], osq[:, dvc, :]) for dvc in range(2)], reads=["ones", "osq"], writes=[f"ps{pn}"])
                        P.op("act", lambda e: e.activation(out=ort[:], in_=psb[pn][:], func=AF.Ln, bias=EPS, scale=1.0 / 256),
                             reads=[f"ps{pn}"], writes=["ort"])
                        P.op("act", lambda e: e.activation(out=ort[:], in_=ort[:], func=AF.Exp, scale=-0.5), reads=["ort"], writes=["ort"])
                        adv(g, 2)
                        for dvc in range(2):
                            P.op("dve", lambda e, dvc=dvc: e.scalar_tensor_tensor(
                                out=ot[:, dvc, :], in0=ot[:, dvc, :], scalar=lpt[:, L_HG + dvc:L_HG + dvc + 1], in1=ort[:],
                                op0=ALU.mult, op1=ALU.mult), reads=["ot", "ort", "lpt"], writes=["ot"])
                            P.op("dve", lambda e, dvc=dvc: e.tensor_tensor(
                                out=cat[:, hd * 2 + dvc, ts], in0=ot[:, dvc, :], in1=sr[z][:, dvc, :], op=ALU.mult),
                                reads=["ot", f"sr{z}"], writes=["src"])
                        adv(g, 100)

                    g0 = pgen(0)
                    adv(g0, 100)
                    for i in range(len(seq)):
                        g = pgen(i + 1) if i + 1 < len(seq) else None
                        cstage(i, g)
                    barrier()
                out_proj(od_w_out[o], cat, norm_gcol=(L_GFFN if fuse_ffn_norm else None))


        def attn_branch(hp, bi, d, half, vT, qn, kn, knp, vp, vpp, acc, pT, vctr):
            nbl = 16 // d
            vb = vctr[0] % 2
            vctr[0] += 1
            vpc = vp[vb]
            vpn = f"vp{vb}"

            def cols(r, n):
                a = r + d * 128 * n
                return slice(a, a + d * 127 + 1, d)
            if half == 1:
                P.dma("sp", vpp[:].rearrange("p a b -> p (a b)"), vcar[bi, hp], reads=[f"vcar{bi}_{hp}"], writes=["vpp"])
            for q4 in range(4):
                ptile, pname = ((pst[:, 0:512], "pst") if q4 % 2 == 0 else (psb[6][:].bitcast(BF16)[:, 0:512], "ps6"))

                def trs(e, q4=q4, ptile=ptile):
                    ins = None
                    for b4 in range(4):
                        blk = q4 * 4 + b4
                        r, n = blk // nbl, blk % nbl
                        ins = e.transpose(ptile[:, b4 * 128:(b4 + 1) * 128], vT[:, cols(r, n)], ident[:])
                    return ins
                P.op("pe", trs, reads=["vT", "ident"], writes=[pname])
                pv3 = ptile.rearrange("p (a b) -> p a b", a=4)
                P.op("act", lambda e, pv3=pv3, q4=q4: e.activation(out=vpc[:, q4 * 4:q4 * 4 + 4, 0:64], in_=pv3[:, :, 0:64], func=AF.Copy),
                     reads=[pname], writes=[vpn])
                P.op("act", lambda e, pv3=pv3, q4=q4: e.activation(out=vpc[:, q4 * 4:q4 * 4 + 4, 128:192], in_=pv3[:, :, 64:128], func=AF.Copy),
                     reads=[pname], writes=[vpn])
            if half == 0:
                P.dma("sp", vcar[bi, hp], vpc[:].rearrange("p a b -> p (a b)"), reads=[vpn], writes=[f"vcar{bi}_{hp}"])
            info = {}

            def s1(blk):
                r, n = blk // nbl, blk % nbl
                cq = cols(r, n)
                has_prev = (n > 0) or (half == 1)
                s = blk % 3
                pi = next_ps()
                vprev = None
                if has_prev:
                    if n > 0:
                        kpt, kc_, vprev, kpn = kn, cols(r, n - 1), (vpc, blk - 1, vpn), "kn"
                    else:
                        kpt, kc_, vprev, kpn = knp, cols(r, nbl - 1), (vpp, r * nbl + nbl - 1, "vpp"), "knp"
                    mm(psb[pi][:, 0:256], [(kpt[:, kc_], qn[:, :, cq])], reads=[kpn, "qn"], writes=[f"ps{pi}"])
                    mm(psb[pi][:, 256:512], [(kn[:, cq], qn[:, :, cq])], reads=["kn", "qn"], writes=[f"ps{pi}"], start=False)
                    mm(psb[pi][:], [(ident[:], maskb4[:])], reads=["ident", "maskb4"], writes=[f"ps{pi}"], start=False)
                    P.op("act", lambda e: e.activation(out=pT[s][:], in_=psb[pi][:], func=AF.Exp, scale=0.125),
                         reads=[f"ps{pi}"], writes=[f"pT{s}"])
                else:
                    mm(psb[pi][:, 256:512], [(kn[:, cq], qn[:, :, cq]), (ident[:], maskb4[:, 256:512])],
                       reads=["kn", "qn", "ident", "maskb4"], writes=[f"ps{pi}"])
                    P.op("act", lambda e: e.activation(out=pT[s][:, 256:512], in_=psb[pi][:, 256:512], func=AF.Exp, scale=0.125),
                         reads=[f"ps{pi}"], writes=[f"pT{s}"])
                info[blk] = (cq, has_prev, s, vprev)

            def s2(blk):
                cq, has_prev, s, vprev = info[blk]
                pn = next_ps()
                first = True
                for hh in range(2):
                    vs = slice(hh * 64, hh * 64 + 128)
                    prs, rd = [], [vpn, f"pT{s}"]
                    if has_prev:
                        prs.append((vprev[0][:, vprev[1], vs], pT[s][:, hh * 128:hh * 128 + 128]))
                        rd.append(vprev[2])
                    prs.append((vpc[:, blk, vs], pT[s][:, 256 + hh * 128:256 + hh * 128 + 128]))
                    mm(psb[pn][:, hh * 128:(hh + 1) * 128], prs, reads=rd, writes=[f"ps{pn}"], start=first)
                    first = False
                av = acc[:, :, cq]
                sv = psb[pn][:, 0:256].rearrange("p (a b) -> p a b", a=2)
                if bi == 0:
                    P.op("act", lambda e: e.activation(out=av, in_=sv, func=AF.Copy), reads=[f"ps{pn}"], writes=["acc"])
                else:
                    P.op("dve", lambda e: e.tensor_tensor(out=av, in0=av, in1=sv, op=ALU.add), reads=[f"ps{pn}", "acc"], writes=["acc"])
            SK = 2
            for i in range(16 + SK):
                if i < 16:
                    s1(i)
                if i >= SK:
                    s2(i - SK)

        def even(ei, half):
            wv = ev_w_in[ei].rearrange("(kc p) n -> p kc n", p=128)
            with contextlib.ExitStack() as st:
                cat = sb(st, "cat", [128, 8, T], BF16)
                with contextlib.ExitStack() as st2:
                    wva = sb(st2, "wva", [128, 8, 512], BF16)
                    wu = [sb(st2, "wua", [128, 8, 128], BF16) for _ in range(2)]
                    wsT = sb(st2, "wsT", [128, 8, 128], BF16)
                    wsf = sb(st2, "wsf", [128, 8, 128], F32)
                    gv = [sb(st2, "gv", [128, 512], F32) for _ in range(2)]
                    vn = [sb(st2, "vn", [128, 512], BF16) for _ in range(2)]
                    s1 = [sb(st2, "s1", [128, 6], F32) for _ in range(2)]
                    s2 = [sb(st2, "s2", [128, 2], F32) for _ in range(2)]
                    s3 = [sb(st2, "s3", [128, 1], F32) for _ in range(2)]
                    tmp = [sb(st2, "tmp", [128, 128], F32) for _ in range(4)]
                    mhalf = sb(st2, "mhalf", [128, 1], F32)
                    lpa = sb(st2, "lpa", [128, 520], F32)
                    P.dma("sp", lpa[:, 0:512], lp[2 * ei][:, L_BIAS:L_BIAS + 512], writes=["lpa"])
                    P.dma("sp", lpa[:, 512:520], lp[2 * ei][:, L_LNGF:L_LNGF + 8], writes=["lpa"])
                    b2t = sb(st2, "b2t", [128, 4, 128], F32)
                    s4 = [sb(st2, "s4", [128, 1], F32) for _ in range(2)]
                    P.op("pool", lambda e: e.memset(mhalf[:], -0.5), writes=["mhalf"])
                    P.dma("sp", wsf[:], ev_wsT[ei], writes=["wsf"])
                    P.op("dve", lambda e: e.tensor_tensor(out=wsT[:], in0=wsf[:],
                                                        in1=cst[:, 0:128].unsqueeze(1).to_broadcast([128, 8, 128]),
                                                        op=ALU.mult), reads=["wsf", "cst"], writes=["wsT"])
                    for g in range(8):
                        cp_, gg_ = g // 2, g % 2
                        pp_ = slice(gg_ * 64, (gg_ + 1) * 64)
                        pi = next_ps()
                        mm(psb[pi][:, 0:128], [(ones[:], wsT[:, g, :])], reads=["ones", "wsT"], writes=[f"ps{pi}"])
                        P.op("dve", lambda e, pi=pi, pp_=pp_, cp_=cp_: e.scalar_tensor_tensor(
                            out=b2t[pp_, cp_, :], in0=psb[pi][pp_, 0:128], scalar=lpa[pp_, 516 + cp_:517 + cp_],
                            in1=lpa[pp_, cp_ * 128:(cp_ + 1) * 128], op0=ALU.mult, op1=ALU.add),
                            reads=[f"ps{pi}", "lpa"], writes=["b2t"])
                    P.dma("pool", wva[:], wv[:, :, 512:1024], writes=["wva"])
                    for uc in range(4):
                        b = uc % 2
                        P.dma("pool", wu[b][:], wv[:, :, uc * 128:(uc + 1) * 128], writes=[f"wua{b}"])
                        for tt in range(NT):
                            ts = slice(tt * 512, (tt + 1) * 512)
                            pi = next_ps()
                            mm(psb[pi][:], [(wu[b][:, kc, :], h[:, kc, ts]) for kc in range(8)], reads=[f"wua{b}", "h"], writes=[f"ps{pi}"])
                            P.op("act", lambda e, pi=pi, uc=uc, ts=ts: e.activation(out=cat[:, uc, ts], in_=psb[pi][:], func=AF.Gelu),
                                 reads=[f"ps{pi}"], writes=[f"srcu{uc}_{tt}"])

                    def a_s1(tc_):
                        b = tc_ % 2
                        tsl = slice(tc_ * 128, (tc_ + 1) * 128)
                        pi = next_ps()
                        mm(psb[pi][:], [(h[:, kc, tsl], wva[:, kc, :]) for kc in range(8)], reads=["wva", "h"], writes=[f"ps{pi}"])
                        P.op("act", lambda e: e.activation(out=gv[b][:], in_=psb[pi][:], func=AF.Gelu), reads=[f"ps{pi}"], writes=[f"gv{b}"])
                        yield
                        P.op("dve", lambda e: e.bn_stats(out=s1[b][:], in_=gv[b][:]), reads=[f"gv{b}"], writes=[f"s1{b}"])
                        yield
                        P.op("dve", lambda e: e.bn_aggr(out=s2[b][:], in_=s1[b][:]), reads=[f"s1{b}"], writes=[f"s2{b}"])
                        yield
                        P.op("pool", lambda e: e.tensor_scalar(out=s3[b][:], in0=s2[b][:, 1:2], scalar1=EPS, scalar2=None, op0=ALU.add),
                             reads=[f"s2{b}"], writes=[f"s3{b}"])
                        P.op("pool", lambda e: e.tensor_tensor(out=s3[b][:], in0=s3[b][:], in1=mhalf[:], op=ALU.pow),
                             reads=[f"s3{b}", "mhalf"], writes=[f"s3{b}"])
                        yield
                        yield
                        P.op("dve", lambda e: e.scalar_tensor_tensor(out=s4[b][:], in0=s2[b][:, 0:1], scalar=-1.0, in1=s3[b][:],
                                                                   op0=ALU.mult, op1=ALU.mult), reads=[f"s2{b}", f"s3{b}"], writes=[f"s4{b}"])
                        yield
                        P.op("act", lambda e: e.activation(out=vn[b][:], in_=gv[b][:], func=AF.Identity, scale=s3[b][:, 0:1], bias=s4[b][:, 0:1]),
                             reads=[f"gv{b}", f"s3{b}", f"s4{b}"], writes=[f"vn{b}"])
                        yield

                    def a_adv(g):
                        if g is not None:
                            try:
                                next(g)
                            except StopIteration:
                                pass

                    def a_s2(tc_, g):
                        b = tc_ % 2
                        tsl = slice(tc_ * 128, (tc_ + 1) * 128)
                        k = 0
                        for cp in range(4):
                            for gg in range(2):
                                gi = cp * 2 + gg
                                pp = slice(gg * 64, (gg + 1) * 64)
                                pi = next_ps()
                                tb = k % 4
                                k += 1
                                mm(psb[pi][:, 0:128], [(vn[b][:, cp * 128:(cp + 1) * 128], wsT[:, gi, :])], reads=[f"vn{b}", "wsT"], writes=[f"ps{pi}"])
                                P.op("dve", lambda e, pi=pi, pp=pp, cp=cp, tb=tb: e.scalar_tensor_tensor(
                                    out=tmp[tb][pp, :], in0=psb[pi][pp, 0:128], scalar=lpa[pp, 512 + cp:513 + cp], in1=b2t[pp, cp, :],
                                    op0=ALU.mult, op1=ALU.add), reads=[f"ps{pi}", "lpa", "b2t"], writes=[f"tmp{tb}"])
                                P.op("pool", lambda e, pp=pp, cp=cp, tsl=tsl, tb=tb: e.tensor_tensor(
                                    out=cat[pp, cp, tsl], in0=tmp[tb][pp, :], in1=cat[pp, cp, tsl], op=ALU.mult),
                                    reads=[f"tmp{tb}", f"srcu{cp}_{tc_ // 4}"], writes=[f"srca{cp}_{gg}_{tc_}"])
                                a_adv(g)
                        for _ in range(8):
                            a_adv(g)
                    g0 = a_s1(0)
                    for _ in range(8):
                        a_adv(g0)
                    for i in range(16):
                        gnext = a_s1(i + 1) if i + 1 < 16 else None
                        a_s2(i, gnext)
                    barrier()
                with contextlib.ExitStack() as st2:
                    wq = sb(st2, "wq", [128, 8, 128], BF16)
                    wk = sb(st2, "wk", [128, 8, 128], BF16)
                    wvb = sb(st2, "wvb", [128, 8, 128], BF16)
                    qn = sb(st2, "qz", [128, 2, T], BF16)
                    kn = sb(st2, "kn", [128, T], BF16)
                    P.op("pool", lambda e: e.memset(qn[:], 0.0), writes=["qn"])
                    knp = sb(st2, "knp", [128, T], BF16)
                    vT = sb(st2, "vT", [128, T], BF16)
                    vp = [sb(st2, "vp", [128, 16, 192], BF16) for _ in range(2)]
                    vpp = sb(st2, "vpp", [128, 16, 192], BF16)
                    acc = sb(st2, "acc", [128, 2, T], F32)
                    for vv in vp:
                        P.op("pool", lambda e, vv=vv: e.memset(vv[:, :, 64:128], 1.0), writes=["vp0", "vp1"])
                    sq = [sb(st2, "sqb", [128, 512], BF16) for _ in range(2)]
                    rt = [sb(st2, "rtb", [128, 512], F32) for _ in range(2)]
                    pT = [sb(st2, "pT", [128, 512], BF16) for _ in range(3)]
                    vctr = [0]
                    for hp in range(4):
                        P.dma("pool", wq[:], wv[:, :, 1024 + hp * 128:1024 + (hp + 1) * 128], writes=["wq"])
                        P.dma("pool", wk[:], wv[:, :, 1536 + hp * 128:1536 + (hp + 1) * 128], writes=["wk"])
                        P.dma("pool", wvb[:], wv[:, :, 2048 + hp * 128:2048 + (hp + 1) * 128], writes=["wvb"])
                        if half == 1:
                            P.dma("sp", knp[:], kcar[hp], reads=[f"kcar{hp}"], writes=["knp"])
                        jobs = [(wt_, wn, gcol, dst, dn, tt) for (wt_, wn, gcol, dst, dn) in
                                ((wq, "wq", L_QG, qn, "qn"), (wk, "wk", L_KG, kn, "kn")) for tt in range(NT)]
                        jps = {}

                        def n_s1(j):
                            wt_, wn, gcol, dst, dn, tt = jobs[j]
                            b = j % 2
                            ts = slice(tt * 512, (tt + 1) * 512)
                            pi = next_ps()
                            jps[j] = pi
                            mm(psb[pi][:], [(wt_[:, kc, :], h[:, kc, ts]) for kc in range(8)], reads=[wn, "h"], writes=[f"ps{pi}"])
                            P.op("act", lambda e: e.activation(out=sq[b][:], in_=psb[pi][:], func=AF.Square), reads=[f"ps{pi}"], writes=[f"sqb{b}"])

                        def n_s2(j):
                            wt_, wn, gcol, dst, dn, tt = jobs[j]
                            b = j % 2
                            ts = slice(tt * 512, (tt + 1) * 512)
                            pi = jps[j]
                            p2 = 6
                            mm(psb[p2][:], [(bones[:], sq[b][:])], reads=["bones", f"sqb{b}"], writes=[f"ps{p2}"])
                            P.op("act", lambda e: e.activation(out=rt[b][:], in_=psb[p2][:], func=AF.Ln, bias=EPS, scale=1.0 / 64),
                                 reads=[f"ps{p2}"], writes=[f"rtb{b}"])
                            P.op("act", lambda e: e.activation(out=rt[b][:], in_=rt[b][:], func=AF.Exp, scale=-0.5), reads=[f"rtb{b}"], writes=[f"rtb{b}"])
                            if dn == "qn":
                                for hh in range(2):
                                    hs_ = slice(hh * 64, (hh + 1) * 64)
                                    P.op("dve", lambda e, hh=hh, hs_=hs_: e.scalar_tensor_tensor(
                                        out=dst[hs_, hh, ts], in0=psb[pi][hs_, :], scalar=lpt[hs_, gcol:gcol + 1], in1=rt[b][hs_, :],
                                        op0=ALU.mult, op1=ALU.mult), reads=[f"ps{pi}", f"rtb{b}", "lpt"], writes=[dn])
                            else:
                                P.op("dve", lambda e: e.scalar_tensor_tensor(
                                    out=dst[:, ts], in0=psb[pi][:], scalar=lpt[:, gcol:gcol + 1], in1=rt[b][:], op0=ALU.mult, op1=ALU.mult),
                                    reads=[f"ps{pi}", f"rtb{b}", "lpt"], writes=[dn])
                        for j in range(len(jobs) + 1):
                            if j < len(jobs):
                                n_s1(j)
                            if j >= 1:
                                n_s2(j - 1)
                        if half == 0:
                            P.dma("sp", kcar[hp], kn[:], reads=["kn"], writes=[f"kcar{hp}"])
                        for tt in range(NT):
                            ts = slice(tt * 512, (tt + 1) * 512)
                            pi = next_ps()
                            mm(psb[pi][:], [(wvb[:, kc, :], h[:, kc, ts]) for kc in range(8)], reads=["wvb", "h"], writes=[f"ps{pi}"])
                            P.op("act", lambda e, pi=pi, ts=ts: e.activation(out=vT[:, ts], in_=psb[pi][:], func=AF.Copy),
                                 reads=[f"ps{pi}"], writes=["vT"])
                        for bi, d in enumerate((1, 4, 16)):
                            attn_branch(hp, bi, d, half, vT, qn, kn, knp, vp, vpp, acc, pT, vctr)
                        k_ = 0
                        for hh in range(2):
                            nr = slice(hh * 64, (hh + 1) * 64)
                            dr = slice((1 - hh) * 64, (2 - hh) * 64)
                            for tt in range(NT):
                                ts = slice(tt * 512, (tt + 1) * 512)
                                b = k_ % 2
                                k_ += 1
                                P.op("act", lambda e, b=b, hh=hh, nr=nr, dr=dr, ts=ts: e.activation(out=rt[b][nr, :], in_=acc[dr, hh, ts], func=AF.Ln),
                                     reads=["acc"], writes=[f"rtb{b}"])
                                P.op("act", lambda e, b=b, nr=nr: e.activation(out=rt[b][nr, :], in_=rt[b][nr, :], func=AF.Exp, scale=-1.0),
                                     reads=[f"rtb{b}"], writes=[f"rtb{b}"])
                                P.op("dve", lambda e, b=b, hh=hh, nr=nr, ts=ts, hp=hp: e.tensor_tensor(
                                    out=cat[nr, 4 + hp, ts], in0=acc[nr, hh, ts], in1=rt[b][nr, :], op=ALU.mult),
                                    reads=["acc", f"rtb{b}"], writes=["src"])
                    barrier()

                out_proj(ev_w_out[ei], cat, norm_gcol=(L_GFFN if fuse_ffn_norm else None))

        steps = [(li, layer, half) for li, layer in enumerate(layers) for half in halves]
        xall = lambda c: [xr(c, t_) for t_ in range(NT)]

        def xsrc(si):
            li, layer, half = steps[si]
            return (xT if li == 0 else xs)[half], half

        src0, h0 = xsrc(0)
        for c in range(8):
            P.dma("sp", x[:, c, :], src0[c * 128:(c + 1) * 128, :], reads=[f"xs{h0}_{c}"], writes=xall(c))
        for si, (li, layer, half) in enumerate(steps):
            if half == halves[0]:
                P.dma("sp", lpt[:], lp[layer][:, 0:128], writes=["lpt"])
            last = li == len(layers) - 1
            dst = (outT if last else xs)[half]

            def fin(c, si=si, dst=dst, last=last, half=half):
                P.dma("sp", dst[c * 128:(c + 1) * 128, :], x[:, c, :], reads=xall(c), writes=[f"xs{half}_{c}"], final=last)
                if si + 1 < len(steps):
                    nsrc, nh = xsrc(si + 1)
                    P.dma("sp", x[:, c, :], nsrc[c * 128:(c + 1) * 128, :], reads=[f"xs{nh}_{c}"], writes=xall(c))
            if "mix" in phases:
                rms_norm(L_GMIX)
                if layer % 2 == 0:
                    even(layer // 2, half)
                else:
                    gla(layer // 2, half)
            if "ffn" in phases:
                ffn(layer, half, fin, normed=fuse_ffn_norm)
            else:
                for c in range(8):
                    fin(c)
                barrier()
        P.emit()
    return nc


def _consts():
    c = np.zeros((128, C_N), np.float32)
    p = np.arange(128)[:, None]
    i = np.arange(128)[None, :]
    c[:, C_ID:C_ID + 128] = (p == i)
    c[:, C_BONES:C_BONES + 128] = (p // 64 == i // 64)
    c[:, C_MPREV:C_MPREV + 128] = np.where(i <= p, 0.0, NEGM)
    c[:, C_MCUR:C_MCUR + 128] = np.where(p <= i, 0.0, NEGM)
    c[:, C_TRIL:C_TRIL + 128] = (p <= i)
    c[:, C_M2:C_M2 + 128] = (p <= i) & (p // 64 == i // 64)
    c[:, C_RESET:C_RESET + 512] = (np.arange(512)[None, :] % 64 != 0)
    return c


def _fm(v, n):
    return np.ascontiguousarray(v.reshape(n, 128).T)


def _layer_pack(inp):
    lp = np.zeros((4, 128, L_N), np.float32)
    for l in range(4):
        lp[l, :, L_GMIX:L_GMIX + 8] = _fm(inp["norm_mix_g"][l], 8)
        lp[l, :, L_GFFN:L_GFFN + 8] = _fm(inp["norm_ffn_g"][l], 8)
        for k, off in enumerate((L_CW0, L_CW1, L_CW2)):
            lp[l, :, off:off + NFC] = _fm(inp["ffn_conv_w"][l, k], NFC)
        lp[l, :, L_CB:L_CB + NFC] = _fm(inp["ffn_conv_b"][l], NFC)
        if l % 2 == 0:
            e = l // 2
            lp[l, :, L_QG] = np.tile(inp["ev_q_g"][e], 2)
            lp[l, :, L_KG] = np.tile(inp["ev_k_g"][e], 2)
            bs = inp["ev_a_bs"][e]
            for cp in range(4):
                lp[l, 0:64, L_BIAS + cp * 128:L_BIAS + (cp + 1) * 128] = bs[2 * cp][None, :]
                lp[l, 64:128, L_BIAS + cp * 128:L_BIAS + (cp + 1) * 128] = bs[2 * cp + 1][None, :]
            lp[l, :, L_LNG:L_LNG + 512] = inp["ev_a_ln_g"][e][None, :]
            lp[l, :, L_LNB:L_LNB + 512] = inp["ev_a_ln_b"][e][None, :]
            lp[l, :, L_LNGF:L_LNGF + 4] = _fm(inp["ev_a_ln_g"][e], 4)
            lp[l, :, L_LNBF:L_LNBF + 4] = _fm(inp["ev_a_ln_b"][e], 4)
        else:
            o = l // 2
            lp[l, :, L_BA:L_BA + 4] = _fm(inp["od_b_a"][o], 4)
            lp[l, :, L_HG:L_HG + 2] = _fm(inp["od_head_g"][o], 2)
    return lp


def make_in_maps(inp, seqs):
    consts = _consts()
    lp = _layer_pack(inp)
    wsT = np.ascontiguousarray(np.transpose(inp["ev_a_ws"], (0, 3, 1, 2)))
    shared = {
        "consts": consts, "lp": lp,
        "ev_w_in": np.ascontiguousarray(inp["ev_w_in"]), "ev_wsT": wsT,
        "ev_w_out": np.ascontiguousarray(inp["ev_w_out"]),
        "od_w_in": np.ascontiguousarray(inp["od_w_in"]), "od_w_a2": np.ascontiguousarray(inp["od_w_a2"]),
        "od_w_out": np.ascontiguousarray(inp["od_w_out"]),
        "ffn_w_gate": np.ascontiguousarray(inp["ffn_w_gate"]), "ffn_w_up": np.ascontiguousarray(inp["ffn_w_up"]),
        "ffn_w_down": np.ascontiguousarray(inp["ffn_w_down"]),
    }
    maps = []
    for b in seqs:
        xb = np.asarray(inp["x"][b], np.float32)
        xTt = np.ascontiguousarray(xb.reshape(2, T, 1024).transpose(0, 2, 1))
        m = dict(shared)
        m["xT"] = xTt
        maps.append(m)
    return maps


_NC = {}


def kernel(**inputs):
    inp = {k: np.asarray(v) for k, v in inputs.items()}
    if "full" not in _NC:
        _NC["full"] = build()
    nc = _NC["full"]
    maps = make_in_maps(inp, range(4))
    res = run_bass_kernel_spmd(nc, maps, core_ids=list(range(4)))
    out = np.empty((4, 2 * T, 1024), np.float32)
    for b in range(4):
        o = res.results[b]["outT"]
        out[b] = o.transpose(0, 2, 1).reshape(2 * T, 1024)
    return out
```

```python
import contextlib
import numpy as np
import concourse.bass as bass
import concourse.mybir as mybir
from concourse.bass_utils import run_bass_kernel_spmd

F32 = mybir.dt.float32
BF16 = mybir.dt.bfloat16
AF = mybir.ActivationFunctionType
ALU = mybir.AluOpType

ENGS = ("pe", "act", "dve", "pool", "sp")
SEM_CHUNK = 12000
DMA_RING = 6
EPS = 1e-6
T = 2048
NT = 4
DFF = 2816
NFC = 22
NEGM = -30000.0


class Op:
    __slots__ = ("eng", "fn", "reads", "writes", "is_dma", "deps", "signal",
                 "sem", "val", "ring_prev", "idx")


class Prog:
    def __init__(self, nc):
        self.nc = nc
        self.ops = []
        self.final_waits = []

    def op(self, eng, fn, reads=(), writes=(), is_dma=False, barrier=False):
        o = Op()
        o.eng, o.fn, o.is_dma = eng, fn, is_dma
        psr_ = tuple(r for r in reads if r.startswith("ps") and r[2:].isdigit() or r == "pst")
        o.reads = tuple(reads) + (() if barrier else ("PHASE",))
        o.writes = tuple(writes) + psr_ + (("PHASE",) if barrier else ())
        o.deps = []
        o.signal = is_dma
        o.sem = None
        o.val = 0
        o.ring_prev = None
        o.idx = len(self.ops)
        self.ops.append(o)
        return o

    def dma(self, eng, out, in_, reads=(), writes=(), final=False):
        o = self.op(eng, lambda e, out=out, in_=in_: e.dma_start(out=out, in_=in_),
                    reads, writes, is_dma=True)
        if final:
            self.final_waits.append(o)
        return o

    def _analyze(self):
        last_w = {}
        readers = {}
        ops = self.ops
        for o in ops:
            deps = {}
            for r in o.reads:
                lw = last_w.get(r)
                if lw is not None:
                    deps[lw] = "raw"
            for w in o.writes:
                lw = last_w.get(w)
                if lw is not None and lw not in deps:
                    deps[lw] = "waw"
                for rd in readers.get(w, ()):
                    if rd not in deps:
                        deps[rd] = "war"
            deps.pop(o.idx, None)
            for r in o.reads:
                readers.setdefault(r, []).append(o.idx)
            for w in o.writes:
                last_w[w] = o.idx
                readers[w] = []
            best = {}
            dma_deps = []
            for d, kind in deps.items():
                od = ops[d]
                if od.is_dma:
                    dma_deps.append(d)
                    continue
                if od.eng == o.eng and not o.is_dma:
                    if od.eng in ("pe", "sp"):
                        continue
                if od.eng not in best or best[od.eng] < d:
                    best[od.eng] = d
            o.deps = sorted(list(best.values()) + dma_deps)
        waited = {e: {} for e in ENGS}
        waited_dma = {e: set() for e in ENGS}
        for o in ops:
            nd = []
            for d in o.deps:
                od = ops[d]
                if od.is_dma:
                    if d in waited_dma[o.eng]:
                        continue
                    waited_dma[o.eng].add(d)
                    nd.append(d)
                else:
                    if waited[o.eng].get(od.eng, -1) >= d:
                        continue
                    waited[o.eng][od.eng] = d
                    nd.append(d)
            o.deps = nd
            for d in nd:
                ops[d].signal = True

    def emit(self):
        nc = self.nc
        self._analyze()
        with contextlib.ExitStack() as stack:
            cnt = {e: 0 for e in ENGS}
            sems = {e: [] for e in ENGS}
            rings = {e: [] for e in ENGS}
            ring_cnt = {e: 0 for e in ENGS}
            ring_ord = {}
            ring_last = {}
            for o in self.ops:
                if o.is_dma:
                    j = ring_cnt[o.eng] % DMA_RING
                    ring_cnt[o.eng] += 1
                    if len(rings[o.eng]) <= j:
                        rings[o.eng].append(stack.enter_context(nc.semaphore(f"dr_{o.eng}_{j}")))
                    key = (o.eng, j)
                    ring_ord[key] = ring_ord.get(key, 0) + 1
                    o.sem = rings[o.eng][j]
                    o.val = 16 * ring_ord[key]
                    o.ring_prev = ring_last.get(key)
                    ring_last[key] = o
                elif o.signal:
                    c = cnt[o.eng]
                    k = c // SEM_CHUNK
                    if len(sems[o.eng]) <= k:
                        sems[o.eng].append(stack.enter_context(nc.semaphore(f"cs_{o.eng}_{k}")))
                    o.sem = sems[o.eng][k]
                    o.val = (c % SEM_CHUNK) + 1
                    cnt[o.eng] += 1
            block = stack.enter_context(nc.Block())
            ops = self.ops
            finals = self.final_waits

            def run(eng_name):
                def body(e):
                    for o in ops:
                        if o.eng != eng_name:
                            continue
                        if o.ring_prev is not None:
                            e.wait_ge(o.ring_prev.sem, o.ring_prev.val)
                        for d in o.deps:
                            od = ops[d]
                            e.wait_ge(od.sem, od.val)
                        ins = o.fn(e)
                        if o.signal:
                            ins.then_inc(o.sem, 16 if o.is_dma else 1)
                    for o in finals:
                        if o.eng == eng_name:
                            e.wait_ge(o.sem, o.val)
                return body

            block.tensor(run("pe"))
            block.scalar(run("act"))
            block.vector(run("dve"))
            block.gpsimd(run("pool"))
            block.sync(run("sp"))


C_ID, C_BONES, C_MPREV, C_MCUR, C_TRIL, C_M2, C_RESET, C_N = 0, 128, 256, 384, 512, 640, 768, 1280
L_GMIX, L_GFFN, L_CW0, L_CW1, L_CW2, L_CB, L_X = 0, 8, 16, 38, 60, 82, 104
L_QG, L_KG, L_BIAS, L_LNG, L_LNB = 104, 105, 106, 106 + 512, 106 + 1024
L_BA, L_HG = 104, 108
L_LNGF, L_LNBF = 106 + 1536, 106 + 1536 + 4
L_N = 106 + 1536 + 8


def build(layers=(0, 1, 2, 3), halves=(0, 1), phases=("mix", "ffn")):
    nc = bass.Bass("TRN2", target_bir_lowering=False)

    def D(name, shape, kind="ExternalInput", dt=F32):
        return nc.dram_tensor(name, shape, dt, kind=kind).ap()

    xT = D("xT", [2, 1024, T])
    outT = D("outT", [2, 1024, T], "ExternalOutput")
    xs = D("xs", [2, 1024, T], "Internal")
    consts = D("consts", [128, C_N])
    lp = D("lp", [4, 128, L_N])
    has_even = any(l % 2 == 0 for l in layers) and "mix" in phases
    has_odd = any(l % 2 == 1 for l in layers) and "mix" in phases
    has_ffn = "ffn" in phases
    ev_w_in = D("ev_w_in", [2, 1024, 2560]) if has_even else None
    ev_wsT = D("ev_wsT", [2, 128, 8, 128]) if has_even else None
    ev_w_out = D("ev_w_out", [2, 1024, 1024]) if has_even else None
    od_w_in = D("od_w_in", [2, 1024, 3088]) if has_odd else None
    od_w_a2 = D("od_w_a2", [2, 16, 512]) if has_odd else None
    od_w_out = D("od_w_out", [2, 1024, 1024]) if has_odd else None
    w_gate = D("ffn_w_gate", [4, 1024, DFF]) if has_ffn else None
    w_up = D("ffn_w_up", [4, 1024, DFF]) if has_ffn else None
    w_down = D("ffn_w_down", [4, DFF, 1024]) if has_ffn else None
    kcar = D("kcar", [4, 128, T], "Internal", BF16)
    vcar = D("vcar", [3, 4, 128, 16 * 192], "Internal", BF16)

    P = Prog(nc)
    fuse_ffn_norm = ("mix" in phases) and ("ffn" in phases)
    top = contextlib.ExitStack()
    uid = [0]

    def sb(stack, name, shape, dt):
        uid[0] += 1
        return stack.enter_context(nc.sbuf_tensor(f"{name}_{uid[0]}", shape, dt))

    with top:
        x = sb(top, "x", [128, 8, T], F32)
        h = sb(top, "h", [128, 8, T], BF16)
        cst = sb(top, "cst", [128, C_N - C_TRIL], F32)
        lpt = sb(top, "lpt", [128, 128], F32)
        ident = sb(top, "ident", [128, 128], BF16)
        ones = sb(top, "ones", [128, 128], BF16)
        bones = sb(top, "bones", [128, 128], BF16)
        maskb4 = sb(top, "maskb4", [128, 512], BF16)
        state = sb(top, "state", [128, 4, 256], F32)
        gtail = sb(top, "gtail", [128, NFC, 2], F32)
        dummy = sb(top, "dummy", [128, 8], F32)
        psb = [top.enter_context(nc.psum_tensor(f"ps{i}", [128, 512], F32)) for i in range(7)]
        pst = top.enter_context(nc.psum_tensor("pst", [128, 1024], BF16))

        def barrier():
            P.op("pool", lambda e: e.memset(dummy[:], 0.0), barrier=True)

        def mm(out, pairs, reads, writes, start=True):
            def fn(e):
                n = len(pairs)
                ins = None
                for i, (l, r) in enumerate(pairs):
                    ins = e.matmul(out, l, r, start=(start and i == 0), stop=(i == n - 1),
                                   skip_group_check=True)
                return ins
            P.op("pe", fn, reads, writes)

        psr = [0]

        def next_ps(n=1):
            i = psr[0] % 6
            psr[0] += 1
            return i

        P.dma("sp", cst[:], consts[:, C_TRIL:C_N], writes=["cst"])
        P.dma("pool", ident[:], consts[:, C_ID:C_ID + 128], writes=["ident"])
        P.dma("pool", bones[:], consts[:, C_BONES:C_BONES + 128], writes=["bones"])
        for q_, c_ in enumerate((C_MPREV, C_MPREV, C_MCUR, C_MCUR)):
            P.dma("pool", maskb4[:, q_ * 128:(q_ + 1) * 128], consts[:, c_:c_ + 128], writes=["maskb4"])
        P.op("dve", lambda e: e.memset(ones[:], 1.0), writes=["ones"])

        def xr(c, tt):
            return f"x{c}_{tt}"

        def norm_tile(tt, gcol, sq, rt, rs):
            b = tt % 2
            ts = slice(tt * 512, (tt + 1) * 512)
            pi = next_ps()
            P.op("act", lambda e: e.activation(out=sq[b][:], in_=x[:, :, ts], func=AF.Square),
                 reads=[xr(k, tt) for k in range(8)], writes=[f"sq{b}"])
            mm(psb[pi][:], [(ones[:], sq[b][:, kc, :]) for kc in range(8)],
               reads=[f"sq{b}", "ones"], writes=[f"ps{pi}"])
            P.op("act", lambda e: e.activation(out=rt[b][:], in_=psb[pi][:], func=AF.Ln, bias=EPS, scale=1.0 / 1024),
                 reads=[f"ps{pi}"], writes=[f"rt{b}"])
            P.op("act", lambda e: e.activation(out=rs[b][:], in_=rt[b][:], func=AF.Exp, scale=-0.5),
                 reads=[f"rt{b}"], writes=[f"rs{b}"])
            for kc in range(8):
                P.op("dve", lambda e, kc=kc: e.scalar_tensor_tensor(
                    out=h[:, kc, ts], in0=x[:, kc, ts], scalar=lpt[:, gcol + kc:gcol + kc + 1],
                    in1=rs[b][:], op0=ALU.mult, op1=ALU.mult),
                    reads=[xr(kc, tt), f"rs{b}", "lpt"], writes=[f"h{kc}_{tt}"])

        def rms_norm(gcol):
            with contextlib.ExitStack() as st:
                sq = [sb(st, "sq", [128, 8, 512], BF16) for _ in range(2)]
                rt = [sb(st, "rt", [128, 512], F32) for _ in range(2)]
                rs = [sb(st, "rs", [128, 512], F32) for _ in range(2)]
                for tt in range(NT):
                    norm_tile(tt, gcol, sq, rt, rs)
                barrier()

        def out_proj(w_dram, src, nkc=8, norm_gcol=None):
            with contextlib.ExitStack() as st:
                wb = [sb(st, "wo", [128, nkc, 256], BF16) for _ in range(4)]
                wv = w_dram.rearrange("(kc p) n -> p kc n", p=128)
                if norm_gcol is not None:
                    sq = [sb(st, "sq", [128, 8, 512], BF16) for _ in range(2)]
                    rt = [sb(st, "rt", [128, 512], F32) for _ in range(2)]
                    rs = [sb(st, "rs", [128, 512], F32) for _ in range(2)]
                for blk in range(4):
                    P.dma("pool", wb[blk][:], wv[:, :, blk * 256:(blk + 1) * 256], writes=[f"wo{blk}"])

                def op_tile(tt):
                    ts = slice(tt * 512, (tt + 1) * 512)
                    for blk in range(4):
                        for dc in range(2):
                            dch = blk * 2 + dc
                            pi = next_ps()
                            mm(psb[pi][:], [(wb[blk][:, kc, dc * 128:(dc + 1) * 128], src[:, kc, ts]) for kc in range(nkc)],
                               reads=[f"wo{blk}", "src"], writes=[f"ps{pi}"])
                            P.op("dve", lambda e, pi=pi, dch=dch: e.tensor_tensor(
                                out=x[:, dch, ts], in0=x[:, dch, ts], in1=psb[pi][:], op=ALU.add),
                                reads=[f"ps{pi}", xr(dch, tt)], writes=[xr(dch, tt)])
                for i in range(NT + 1):
                    if i < NT:
                        op_tile(i)
                    if i >= 1 and norm_gcol is not None:
                        norm_tile(i - 1, norm_gcol, sq, rt, rs)
                barrier()

        def ffn(layer, half, fin=None, normed=False):
            if not normed:
                rms_norm(L_GFFN)
            wgv = w_gate[layer].rearrange("(kc p) n -> p kc n", p=128)
            wuv = w_up[layer].rearrange("(kc p) n -> p kc n", p=128)
            wdv = w_down[layer].rearrange("(kc p) n -> p kc n", p=128)
            if half == 0:
                P.op("dve", lambda e: e.memset(gtail[:], 0.0), writes=["gtail"])
            with contextlib.ExitStack() as st:
                hid = sb(st, "hid", [128, NFC, 1024], BF16)
                wg = [sb(st, "wg", [128, 8, 256], BF16) for _ in range(2)]
                wu = [sb(st, "wu", [128, 8, 256], BF16) for _ in range(2)]
                gs = [sb(st, "gs", [128, 514], F32) for _ in range(2)]
                ac = [sb(st, "ac", [128, 512], F32) for _ in range(2)]
                wd = [sb(st, "wd", [128, NFC, 256], BF16) for _ in range(2)]
                it = 0
                for sh in range(2):
                    for blk in range(11):
                        b = blk % 2
                        P.dma("pool", wg[b][:], wgv[:, :, blk * 256:(blk + 1) * 256], writes=[f"wg{b}"])
                        P.dma("pool", wu[b][:], wuv[:, :, blk * 256:(blk + 1) * 256], writes=[f"wu{b}"])
                        for fc2 in range(2):
                            fc = blk * 2 + fc2
                            cs = slice(fc2 * 128, (fc2 + 1) * 128)
                            for t2 in range(2):
                                tt = sh * 2 + t2
                                ts = slice(tt * 512, (tt + 1) * 512)
                                hs = slice(t2 * 512, (t2 + 1) * 512)
                                s = it % 2
                                it += 1
                                pg = next_ps()
                                pu = next_ps()
                                mm(psb[pg][:], [(wg[b][:, kc, cs], h[:, kc, ts]) for kc in range(8)],
                                   reads=[f"wg{b}", "h"], writes=[f"ps{pg}"])
                                mm(psb[pu][:], [(wu[b][:, kc, cs], h[:, kc, ts]) for kc in range(8)],
                                   reads=[f"wu{b}", "h"], writes=[f"ps{pu}"])
                                P.op("dve", lambda e, s=s, fc=fc: e.tensor_copy(out=gs[s][:, 0:2], in_=gtail[:, fc, :]),
                                     reads=[f"gtail{fc}", "gtail"], writes=[f"gs{s}"])
                                P.op("act", lambda e, s=s, pg=pg: e.activation(out=gs[s][:, 2:514], in_=psb[pg][:], func=AF.Copy),
                                     reads=[f"ps{pg}"], writes=[f"gs{s}b"])
                                P.op("dve", lambda e, s=s, fc=fc: e.tensor_copy(out=gtail[:, fc, :], in_=gs[s][:, 512:514]),
                                     reads=[f"gs{s}b"], writes=[f"gtail{fc}"])
                                P.op("dve", lambda e, s=s, fc=fc: e.tensor_scalar(
                                    out=ac[s][:], in0=gs[s][:, 2:514], scalar1=lpt[:, L_CW2 + fc:L_CW2 + fc + 1],
                                    scalar2=lpt[:, L_CB + fc:L_CB + fc + 1], op0=ALU.mult, op1=ALU.add),
                                    reads=[f"gs{s}b", "lpt"], writes=[f"ac{s}"])
                                P.op("dve", lambda e, s=s, fc=fc: e.scalar_tensor_tensor(
                                    out=ac[s][:], in0=gs[s][:, 1:513], scalar=lpt[:, L_CW1 + fc:L_CW1 + fc + 1],
                                    in1=ac[s][:], op0=ALU.mult, op1=ALU.add),
                                    reads=[f"gs{s}b", f"gs{s}", f"ac{s}", "lpt"], writes=[f"ac{s}"])
                                P.op("dve", lambda e, s=s, fc=fc: e.scalar_tensor_tensor(
                                    out=ac[s][:], in0=gs[s][:, 0:512], scalar=lpt[:, L_CW0 + fc:L_CW0 + fc + 1],
                                    in1=ac[s][:], op0=ALU.mult, op1=ALU.add),
                                    reads=[f"gs{s}b", f"gs{s}", f"ac{s}", "lpt"], writes=[f"ac{s}"])
                                P.op("act", lambda e, s=s: e.activation(out=ac[s][:], in_=ac[s][:], func=AF.Silu),
                                     reads=[f"ac{s}"], writes=[f"ac{s}"])
                                P.op("dve", lambda e, s=s, pu=pu, fc=fc, hs=hs: e.tensor_tensor(
                                    out=hid[:, fc, hs], in0=ac[s][:], in1=psb[pu][:], op=ALU.mult),
                                    reads=[f"ac{s}", f"ps{pu}"], writes=["hid"])
                    for blk in range(4):
                        b = blk % 2
                        P.dma("pool", wd[b][:], wdv[:, :, blk * 256:(blk + 1) * 256], writes=[f"wd{b}"])
                        for dc in range(2):
                            dch = blk * 2 + dc
                            for t2 in range(2):
                                tt = sh * 2 + t2
                                ts = slice(tt * 512, (tt + 1) * 512)
                                hs = slice(t2 * 512, (t2 + 1) * 512)
                                pi = next_ps()
                                mm(psb[pi][:], [(wd[b][:, kc, dc * 128:(dc + 1) * 128], hid[:, kc, hs]) for kc in range(NFC)],
                                   reads=[f"wd{b}", "hid"], writes=[f"ps{pi}"])
                                P.op("dve", lambda e, pi=pi, dch=dch, ts=ts: e.tensor_tensor(
                                    out=x[:, dch, ts], in0=x[:, dch, ts], in1=psb[pi][:], op=ALU.add),
                                    reads=[f"ps{pi}", xr(dch, tt)], writes=[xr(dch, tt)])
                            if sh == 1 and fin is not None:
                                fin(dch)
                barrier()

        def gla(o, half):
            wv = od_w_in[o].rearrange("(kc p) n -> p kc n", p=128)
            with contextlib.ExitStack() as st:
                cat = sb(st, "cat", [128, 8, T], BF16)
                with contextlib.ExitStack() as st2:
                    wq = sb(st2, "wq", [128, 8, 128], BF16)
                    wk = sb(st2, "wk", [128, 8, 128], BF16)
                    wvv = sb(st2, "wvv", [128, 8, 256], BF16)
                    wr = sb(st2, "wr", [128, 8, 256], BF16)
                    wga = sb(st2, "wga", [128, 8, 16], BF16)
                    wa2 = sb(st2, "wa2", [16, 512], F32)
                    nba = sb(st2, "nba", [128, 4], F32)
                    gaT = sb(st2, "gaT", [16, T], F32)
                    e1 = sb(st2, "e1", [128, 512], F32)
                    b16 = sb(st2, "b16", [128, 512], F32)
                    eb = sb(st2, "eb", [128, 512], F32)
                    enb = sb(st2, "enb", [128, 512], F32)
                    ed = sb(st2, "ed", [128, 512], F32)
                    ks = sb(st2, "ks", [128, 512], BF16)
                    dec = [sb(st2, "dec", [128, 8], F32) for _ in range(2)]
                    qt = [sb(st2, "qt", [128, 512], BF16) for _ in range(2)]
                    kt = [sb(st2, "kt", [128, 512], BF16) for _ in range(2)]
                    ksT = [sb(st2, "ksT", [128, 4, 128], BF16) for _ in range(2)]
                    vtm = [sb(st2, "vtm", [128, 4, 256], BF16) for _ in range(2)]
                    sr = [sb(st2, "sr", [128, 2, 512], BF16) for _ in range(2)]
                    aT = [sb(st2, "aT", [128, 128], BF16) for _ in range(2)]
                    ot = sb(st2, "ot", [128, 2, 512], F32)
                    osq = sb(st2, "osq", [128, 2, 512], BF16)
                    ort = sb(st2, "ort", [128, 512], F32)
                    stb = [sb(st2, "stb", [128, 256], BF16) for _ in range(2)]
                    P.dma("pool", wga[:], wv[:, :, 3072:3088], writes=["wga"])
                    P.dma("sp", wa2[:], od_w_a2[o], writes=["wa2"])
                    P.op("dve", lambda e: e.tensor_scalar(out=nba[:], in0=lpt[:, L_BA:L_BA + 4], scalar1=-1.0, scalar2=None,
                                                          op0=ALU.mult), reads=["lpt"], writes=["nba"])
                    if half == 0:
                        P.op("dve", lambda e: e.memset(state[:], 0.0), writes=["state"])
                    for tt in range(NT):
                        ts = slice(tt * 512, (tt + 1) * 512)
                        pi = next_ps()
                        mm(psb[pi][0:16, :], [(wga[:, kc, :], h[:, kc, ts]) for kc in range(8)],
                           reads=["wga", "h"], writes=[f"ps{pi}"])
                        P.op("act", lambda e, pi=pi, ts=ts: e.activation(out=gaT[:, ts], in_=psb[pi][0:16, :], func=AF.Copy),
                             reads=[f"ps{pi}"], writes=["gaT"])
                    seq = [(hd, tt) for hd in range(4) for tt in range(NT)]
                    pctr = [0]
                    cctr = [0]

                    def ps_p():
                        pctr[0] += 1
                        return pctr[0] % 3

                    def ps_o():
                        cctr[0] += 1
                        return 3 + cctr[0] % 2

                    def mm_split(out, pairs, reads, writes, n=4):
                        for k0 in range(0, len(pairs), n):
                            mm(out, pairs[k0:k0 + n], reads=reads, writes=writes, start=(k0 == 0))
                            yield

                    def pgen(i):
                        hd, tt = seq[i]
                        z = i % 2
                        ts = slice(tt * 512, (tt + 1) * 512)
                        if tt == 0:
                            P.dma("pool", wq[:], wv[:, :, hd * 128:(hd + 1) * 128], writes=["wq"])
                            P.dma("pool", wk[:], wv[:, :, 512 + hd * 128:512 + (hd + 1) * 128], writes=["wk"])
                            P.dma("pool", wvv[:], wv[:, :, 1024 + hd * 256:1024 + (hd + 1) * 256], writes=["wvv"])
                            P.dma("pool", wr[:], wv[:, :, 2048 + hd * 256:2048 + (hd + 1) * 256], writes=["wr"])
                        pi = ps_p()
                        mm(psb[pi][:], [(wa2[:, hd * 128:(hd + 1) * 128], gaT[:, ts])], reads=["wa2", "gaT"], writes=[f"ps{pi}"])
                        P.op("act", lambda e: e.activation(out=e1[:], in_=psb[pi][:], func=AF.Exp, bias=nba[:, hd:hd + 1], scale=-1.0),
                             reads=[f"ps{pi}", "nba"], writes=["e1"])
                        yield
                        P.op("act", lambda e: e.activation(out=e1[:], in_=e1[:], func=AF.Ln, bias=1.0, scale=1.0),
                             reads=["e1"], writes=["e1"])
                        yield
                        P.op("dve", lambda e: e.tensor_tensor_scan(out=b16[:], data0=cst[:, 256:768],
                                                                 data1=e1[:], initial=0.0, op0=ALU.mult, op1=ALU.add),
                             reads=["e1", "cst"], writes=["b16"])
                        yield
                        P.op("dve", lambda e: e.tensor_tensor(
                            out=e1[:].rearrange("p (c j) -> p c j", j=64),
                            in0=b16[:].rearrange("p (c j) -> p c j", j=64)[:, :, 63:64].to_broadcast([128, 8, 64]),
                            in1=b16[:].rearrange("p (c j) -> p c j", j=64), op=ALU.subtract),
                            reads=["b16", "e1"], writes=["e1"])
                        P.op("act", lambda e: e.activation(out=eb[:], in_=b16[:], func=AF.Exp, scale=-1.0 / 16), reads=["b16"], writes=["eb"])
                        yield
                        P.op("act", lambda e: e.activation(out=enb[:], in_=b16[:], func=AF.Exp, scale=1.0 / 16), reads=["b16"], writes=["enb"])
                        yield
                        P.op("act", lambda e: e.activation(out=ed[:], in_=e1[:], func=AF.Exp, scale=-1.0 / 16), reads=["e1"], writes=["ed"])
                        P.op("act", lambda e: e.activation(out=dec[z][:], in_=b16[:, 63:512:64], func=AF.Exp, scale=-1.0 / 16),
                             reads=["b16"], writes=[f"dec{z}"])
                        yield
                        pq = ps_p()
                        yield from mm_split(psb[pq][:], [(wq[:, kc, :], h[:, kc, ts]) for kc in range(8)], ["wq", "h"], [f"ps{pq}"])
                        P.op("dve", lambda e: e.scalar_tensor_tensor(out=qt[z][:], in0=psb[pq][:], scalar=128.0 ** -0.5,
                                                                   in1=eb[:], op0=ALU.mult, op1=ALU.mult),
                             reads=[f"ps{pq}", "eb"], writes=[f"qt{z}"])
                        yield
                        pk = ps_p()
                        yield from mm_split(psb[pk][:], [(wk[:, kc, :], h[:, kc, ts]) for kc in range(8)], ["wk", "h"], [f"ps{pk}"])
                        P.op("dve", lambda e: e.tensor_tensor(out=kt[z][:], in0=psb[pk][:], in1=enb[:], op=ALU.mult),
                             reads=[f"ps{pk}", "enb"], writes=[f"kt{z}"])
                        yield
                        P.op("dve", lambda e: e.tensor_tensor(out=ks[:], in0=psb[pk][:], in1=ed[:], op=ALU.mult),
                             reads=[f"ps{pk}", "ed"], writes=["ks"])
                        yield

                        def tr(e):
                            ins = None
                            for bl in range(4):
                                ins = e.transpose(pst[:, bl * 128:(bl + 1) * 128], ks[:, bl * 128:(bl + 1) * 128], ident[:])
                            return ins
                        P.op("pe", tr, reads=["ks", "ident"], writes=["pst"])
                        P.op("act", lambda e: e.activation(out=ksT[z][:].rearrange("p a b -> p (a b)"), in_=pst[:, 0:512], func=AF.Copy),
                             reads=["pst"], writes=[f"ksT{z}"])
                        yield
                        for bp in range(2):
                            pv = ps_p()
                            for b2 in range(2):
                                bl = bp * 2 + b2
                                t0 = tt * 512 + bl * 128
                                pairs = [(h[:, kc, t0:t0 + 128], wvv[:, kc, :]) for kc in range(8)]
                                for k0 in (0, 4):
                                    mm(psb[pv][:, b2 * 256:(b2 + 1) * 256], pairs[k0:k0 + 4], reads=["wvv", "h"], writes=[f"ps{pv}"],
                                       start=(k0 == 0 and b2 == 0))
                                    yield
                            P.op("act", lambda e, pv=pv, bp=bp: e.activation(
                                out=vtm[z][:, bp * 2:bp * 2 + 2, :].rearrange("p a b -> p (a b)"), in_=psb[pv][:], func=AF.Copy),
                                reads=[f"ps{pv}"], writes=[f"vtm{z}"])
                            yield
                        for dvc in range(2):
                            pr = ps_p()
                            yield from mm_split(psb[pr][:], [(wr[:, kc, dvc * 128:(dvc + 1) * 128], h[:, kc, ts]) for kc in range(8)],
                                                ["wr", "h"], [f"ps{pr}"])
                            P.op("act", lambda e, pr=pr, dvc=dvc: e.activation(out=sr[z][:, dvc, :], in_=psb[pr][:], func=AF.Silu),
                                 reads=[f"ps{pr}"], writes=[f"sr{z}"])
                            yield

                    def adv(g, n=1):
                        if g is None:
                            return
                        for _ in range(n):
                            try:
                                next(g)
                            except StopIteration:
                                return

                    def cstage(i, g):
                        hd, tt = seq[i]
                        z = i % 2
                        ts = slice(tt * 512, (tt + 1) * 512)
                        for bl in range(4):
                            bs_ = slice(bl * 128, (bl + 1) * 128)
                            az = bl % 2
                            pa = 5
                            mm(psb[pa][:, 0:128], [(kt[z][:, bs_], qt[z][:, bs_])], reads=[f"kt{z}", f"qt{z}"], writes=[f"ps{pa}"])
                            P.op("dve", lambda e, pa=pa, az=az: e.tensor_tensor(out=aT[az][:], in0=psb[pa][:, 0:128],
                                                                             in1=cst[:, 128:256], op=ALU.mult),
                                 reads=[f"ps{pa}", "cst"], writes=[f"aT{az}"])
                            adv(g)
                            po = ps_o()
                            for dvc in range(2):
                                mm(psb[po][:, dvc * 128:(dvc + 1) * 128], [(vtm[z][:, bl, dvc * 128:(dvc + 1) * 128], aT[az][:])],
                                   reads=[f"vtm{z}", f"aT{az}"], writes=[f"ps{po}"], start=(dvc == 0))
                            for c2 in range(2):
                                ci = bl * 2 + c2
                                cs = slice(bl * 128 + c2 * 64, bl * 128 + (c2 + 1) * 64)
                                pp = slice(c2 * 64, (c2 + 1) * 64)
                                P.op("dve", lambda e, c2=c2: e.tensor_copy(out=stb[c2][:], in_=state[:, hd, :]),
                                     reads=["state"], writes=[f"stb{c2}"])
                                mm(psb[6][:, 0:256], [(ksT[z][pp, bl, :], vtm[z][pp, bl, :])], reads=[f"ksT{z}", f"vtm{z}"], writes=["ps6"])
                                adv(g)
                                P.op("dve", lambda e, ci=ci: e.scalar_tensor_tensor(
                                    out=state[:, hd, :], in0=state[:, hd, :], scalar=dec[z][:, ci:ci + 1], in1=psb[6][:, 0:256],
                                    op0=ALU.mult, op1=ALU.add), reads=["state", f"dec{z}", "ps6", f"stb{c2}"], writes=["state"])
                                for dvc in range(2):
                                    mm(psb[po][:, dvc * 128 + c2 * 64:dvc * 128 + (c2 + 1) * 64],
                                       [(stb[c2][:, dvc * 128:(dvc + 1) * 128], qt[z][:, cs])],
                                       reads=[f"stb{c2}", f"qt{z}"], writes=[f"ps{po}"], start=False)
                                adv(g, 2)
                            P.op("act", lambda e, po=po, bs_=bs_: e.activation(
                                out=ot[:, :, bs_], in_=psb[po][:, 0:256].rearrange("p (a b) -> p a b", a=2), func=AF.Copy),
                                reads=[f"ps{po}"], writes=["ot"])
                        P.op("act", lambda e: e.activation(out=osq[:], in_=ot[:], func=AF.Square), reads=["ot"], writes=["osq"])
                        pn = 5
                        mm(psb[pn][:], [(ones[:], osq[:, dvc, :]) for dvc in range(2)], reads=["ones", "osq"], writes=[f"ps{pn}"])
                        P.op("act", lambda e: e.activation(out=ort[:], in_=psb[pn][:], func=AF.Ln, bias=EPS, scale=1.0 / 256),
                             reads=[f"ps{pn}"], writes=["ort"])
                        P.op("act", lambda e: e.activation(out=ort[:], in_=ort[:], func=AF.Exp, scale=-0.5), reads=["ort"], writes=["ort"])
                        adv(g, 2)
                        for dvc in range(2):
                            P.op("dve", lambda e, dvc=dvc: e.scalar_tensor_tensor(
                                out=ot[:, dvc, :], in0=ot[:, dvc, :], scalar=lpt[:, L_HG + dvc:L_HG + dvc + 1], in1=ort[:],
                                op0=ALU.mult, op1=ALU.mult), reads=["ot", "ort", "lpt"], writes=["ot"])
                            P.op("dve", lambda e, dvc=dvc: e.tensor_tensor(
                                out=cat[:, hd * 2 + dvc, ts], in0=ot[:, dvc, :], in1=sr[z][:, dvc, :], op=ALU.mult),
                                reads=["ot", f"sr{z}"], writes=["src"])
                        adv(g, 100)

                    g0 = pgen(0)
                    adv(g0, 100)
                    for i in range(len(seq)):
                        g = pgen(i + 1) if i + 1 < len(seq) else None
                        cstage(i, g)
                    barrier()
                out_proj(od_w_out[o], cat, norm_gcol=(L_GFFN if fuse_ffn_norm else None))


        def attn_branch(hp, bi, d, half, vT, qn, kn, knp, vp, vpp, acc, pT, vctr):
            nbl = 16 // d
            vb = vctr[0] % 2
            vctr[0] += 1
            vpc = vp[vb]
            vpn = f"vp{vb}"

            def cols(r, n):
                a = r + d * 128 * n
                return slice(a, a + d * 127 + 1, d)
            if half == 1:
                P.dma("sp", vpp[:].rearrange("p a b -> p (a b)"), vcar[bi, hp], reads=[f"vcar{bi}_{hp}"], writes=["vpp"])
            for q4 in range(4):
                ptile, pname = ((pst[:, 0:512], "pst") if q4 % 2 == 0 else (psb[6][:].bitcast(BF16)[:, 0:512], "ps6"))

                def trs(e, q4=q4, ptile=ptile):
                    ins = None
                    for b4 in range(4):
                        blk = q4 * 4 + b4
                        r, n = blk // nbl, blk % nbl
                        ins = e.transpose(ptile[:, b4 * 128:(b4 + 1) * 128], vT[:, cols(r, n)], ident[:])
                    return ins
                P.op("pe", trs, reads=["vT", "ident"], writes=[pname])
                pv3 = ptile.rearrange("p (a b) -> p a b", a=4)
                P.op("act", lambda e, pv3=pv3, q4=q4: e.activation(out=vpc[:, q4 * 4:q4 * 4 + 4, 0:64], in_=pv3[:, :, 0:64], func=AF.Copy),
                     reads=[pname], writes=[vpn])
                P.op("act", lambda e, pv3=pv3, q4=q4: e.activation(out=vpc[:, q4 * 4:q4 * 4 + 4, 128:192], in_=pv3[:, :, 64:128], func=AF.Copy),
                     reads=[pname], writes=[vpn])
            if half == 0:
                P.dma("sp", vcar[bi, hp], vpc[:].rearrange("p a b -> p (a b)"), reads=[vpn], writes=[f"vcar{bi}_{hp}"])
            info = {}

            def s1(blk):
                r, n = blk // nbl, blk % nbl
                cq = cols(r, n)
                has_prev = (n > 0) or (half == 1)
                s = blk % 3
                pi = next_ps()
                vprev = None
                if has_prev:
                    if n > 0:
                        kpt, kc_, vprev, kpn = kn, cols(r, n - 1), (vpc, blk - 1, vpn), "kn"
                    else:
                        kpt, kc_, vprev, kpn = knp, cols(r, nbl - 1), (vpp, r * nbl + nbl - 1, "vpp"), "knp"
                    mm(psb[pi][:, 0:256], [(kpt[:, kc_], qn[:, :, cq])], reads=[kpn, "qn"], writes=[f"ps{pi}"])
                    mm(psb[pi][:, 256:512], [(kn[:, cq], qn[:, :, cq])], reads=["kn", "qn"], writes=[f"ps{pi}"], start=False)
                    mm(psb[pi][:], [(ident[:], maskb4[:])], reads=["ident", "maskb4"], writes=[f"ps{pi}"], start=False)
                    P.op("act", lambda e: e.activation(out=pT[s][:], in_=psb[pi][:], func=AF.Exp, scale=0.125),
                         reads=[f"ps{pi}"], writes=[f"pT{s}"])
                else:
                    mm(psb[pi][:, 256:512], [(kn[:, cq], qn[:, :, cq]), (ident[:], maskb4[:, 256:512])],
                       reads=["kn", "qn", "ident", "maskb4"], writes=[f"ps{pi}"])
                    P.op("act", lambda e: e.activation(out=pT[s][:, 256:512], in_=psb[pi][:, 256:512], func=AF.Exp, scale=0.125),
                         reads=[f"ps{pi}"], writes=[f"pT{s}"])
                info[blk] = (cq, has_prev, s, vprev)

            def s2(blk):
                cq, has_prev, s, vprev = info[blk]
                pn = next_ps()
                first = True
                for hh in range(2):
                    vs = slice(hh * 64, hh * 64 + 128)
                    prs, rd = [], [vpn, f"pT{s}"]
                    if has_prev:
                        prs.append((vprev[0][:, vprev[1], vs], pT[s][:, hh * 128:hh * 128 + 128]))
                        rd.append(vprev[2])
                    prs.append((vpc[:, blk, vs], pT[s][:, 256 + hh * 128:256 + hh * 128 + 128]))
                    mm(psb[pn][:, hh * 128:(hh + 1) * 128], prs, reads=rd, writes=[f"ps{pn}"], start=first)
                    first = False
                av = acc[:, :, cq]
                sv = psb[pn][:, 0:256].rearrange("p (a b) -> p a b", a=2)
                if bi == 0:
                    P.op("act", lambda e: e.activation(out=av, in_=sv, func=AF.Copy), reads=[f"ps{pn}"], writes=["acc"])
                else:
                    P.op("dve", lambda e: e.tensor_tensor(out=av, in0=av, in1=sv, op=ALU.add), reads=[f"ps{pn}", "acc"], writes=["acc"])
            SK = 2
            for i in range(16 + SK):
                if i < 16:
                    s1(i)
                if i >= SK:
                    s2(i - SK)

        def even(ei, half):
            wv = ev_w_in[ei].rearrange("(kc p) n -> p kc n", p=128)
            with contextlib.ExitStack() as st:
                cat = sb(st, "cat", [128, 8, T], BF16)
                with contextlib.ExitStack() as st2:
                    wva = sb(st2, "wva", [128, 8, 512], BF16)
                    wu = [sb(st2, "wua", [128, 8, 128], BF16) for _ in range(2)]
                    wsT = sb(st2, "wsT", [128, 8, 128], BF16)
                    wsf = sb(st2, "wsf", [128, 8, 128], F32)
                    gv = [sb(st2, "gv", [128, 512], F32) for _ in range(3)]
                    vn = [sb(st2, "vn", [128, 512], BF16) for _ in range(3)]
                    s1 = [sb(st2, "s1", [128, 6], F32) for _ in range(3)]
                    s2 = [sb(st2, "s2", [128, 2], F32) for _ in range(3)]
                    s3 = [sb(st2, "s3", [128, 1], F32) for _ in range(3)]
                    tmp = [sb(st2, "tmp", [128, 128], F32) for _ in range(4)]
                    mhalf = sb(st2, "mhalf", [128, 1], F32)
                    lpa = sb(st2, "lpa", [128, 520], F32)
                    P.dma("sp", lpa[:, 0:512], lp[2 * ei][:, L_BIAS:L_BIAS + 512], writes=["lpa"])
                    P.dma("sp", lpa[:, 512:520], lp[2 * ei][:, L_LNGF:L_LNGF + 8], writes=["lpa"])
                    b2t = sb(st2, "b2t", [128, 4, 128], F32)
                    s4 = [sb(st2, "s4", [128, 1], F32) for _ in range(3)]
                    P.op("pool", lambda e: e.memset(mhalf[:], -0.5), writes=["mhalf"])
                    P.dma("sp", wsf[:], ev_wsT[ei], writes=["wsf"])
                    P.op("dve", lambda e: e.tensor_tensor(out=wsT[:], in0=wsf[:],
                                                        in1=cst[:, 0:128].unsqueeze(1).to_broadcast([128, 8, 128]),
                                                        op=ALU.mult), reads=["wsf", "cst"], writes=["wsT"])
                    for g in range(8):
                        cp_, gg_ = g // 2, g % 2
                        pp_ = slice(gg_ * 64, (gg_ + 1) * 64)
                        pi = next_ps()
                        mm(psb[pi][:, 0:128], [(ones[:], wsT[:, g, :])], reads=["ones", "wsT"], writes=[f"ps{pi}"])
                        P.op("dve", lambda e, pi=pi, pp_=pp_, cp_=cp_: e.scalar_tensor_tensor(
                            out=b2t[pp_, cp_, :], in0=psb[pi][pp_, 0:128], scalar=lpa[pp_, 516 + cp_:517 + cp_],
                            in1=lpa[pp_, cp_ * 128:(cp_ + 1) * 128], op0=ALU.mult, op1=ALU.add),
                            reads=[f"ps{pi}", "lpa"], writes=["b2t"])
                    P.dma("pool", wva[:], wv[:, :, 512:1024], writes=["wva"])
                    for uc in range(4):
                        b = uc % 2
                        P.dma("pool", wu[b][:], wv[:, :, uc * 128:(uc + 1) * 128], writes=[f"wua{b}"])
                        for tt in range(NT):
                            ts = slice(tt * 512, (tt + 1) * 512)
                            pi = next_ps()
                            mm(psb[pi][:], [(wu[b][:, kc, :], h[:, kc, ts]) for kc in range(8)], reads=[f"wua{b}", "h"], writes=[f"ps{pi}"])
                            P.op("act", lambda e, pi=pi, uc=uc, ts=ts: e.activation(out=cat[:, uc, ts], in_=psb[pi][:], func=AF.Gelu),
                                 reads=[f"ps{pi}"], writes=[f"srcu{uc}_{tt}"])

                    def a_s1(tc_):
                        b = tc_ % 3
                        tsl = slice(tc_ * 128, (tc_ + 1) * 128)
                        pi = next_ps()
                        mm(psb[pi][:], [(h[:, kc, tsl], wva[:, kc, :]) for kc in range(8)], reads=["wva", "h"], writes=[f"ps{pi}"])
                        P.op("act", lambda e: e.activation(out=gv[b][:], in_=psb[pi][:], func=AF.Gelu), reads=[f"ps{pi}"], writes=[f"gv{b}"])
                        yield
                        P.op("dve", lambda e: e.bn_stats(out=s1[b][:], in_=gv[b][:]), reads=[f"gv{b}"], writes=[f"s1{b}"])
                        yield
                        P.op("dve", lambda e: e.bn_aggr(out=s2[b][:], in_=s1[b][:]), reads=[f"s1{b}"], writes=[f"s2{b}"])
                        yield
                        P.op("pool", lambda e: e.tensor_scalar(out=s3[b][:], in0=s2[b][:, 1:2], scalar1=EPS, scalar2=None, op0=ALU.add),
                             reads=[f"s2{b}"], writes=[f"s3{b}"])
                        P.op("pool", lambda e: e.tensor_tensor(out=s3[b][:], in0=s3[b][:], in1=mhalf[:], op=ALU.pow),
                             reads=[f"s3{b}", "mhalf"], writes=[f"s3{b}"])
                        yield
                        yield
                        P.op("dve", lambda e: e.scalar_tensor_tensor(out=s4[b][:], in0=s2[b][:, 0:1], scalar=-1.0, in1=s3[b][:],
                                                                   op0=ALU.mult, op1=ALU.mult), reads=[f"s2{b}", f"s3{b}"], writes=[f"s4{b}"])
                        yield
                        P.op("act", lambda e: e.activation(out=vn[b][:], in_=gv[b][:], func=AF.Identity, scale=s3[b][:, 0:1], bias=s4[b][:, 0:1]),
                             reads=[f"gv{b}", f"s3{b}", f"s4{b}"], writes=[f"vn{b}"])
                        yield

                    def a_adv(g):
                        if g is not None:
                            try:
                                next(g)
                            except StopIteration:
                                pass

                    def a_s2(tc_, g, g2):
                        b = tc_ % 3
                        tsl = slice(tc_ * 128, (tc_ + 1) * 128)
                        k = 0
                        for cp in range(4):
                            for gg in range(2):
                                gi = cp * 2 + gg
                                pp = slice(gg * 64, (gg + 1) * 64)
                                pi = next_ps()
                                tb = k % 4
                                k += 1
                                mm(psb[pi][:, 0:128], [(vn[b][:, cp * 128:(cp + 1) * 128], wsT[:, gi, :])], reads=[f"vn{b}", "wsT"], writes=[f"ps{pi}"])
                                P.op("dve", lambda e, pi=pi, pp=pp, cp=cp, tb=tb: e.scalar_tensor_tensor(
                                    out=tmp[tb][pp, :], in0=psb[pi][pp, 0:128], scalar=lpa[pp, 512 + cp:513 + cp], in1=b2t[pp, cp, :],
                                    op0=ALU.mult, op1=ALU.add), reads=[f"ps{pi}", "lpa", "b2t"], writes=[f"tmp{tb}"])
                                P.op("pool", lambda e, pp=pp, cp=cp, tsl=tsl, tb=tb: e.tensor_tensor(
                                    out=cat[pp, cp, tsl], in0=tmp[tb][pp, :], in1=cat[pp, cp, tsl], op=ALU.mult),
                                    reads=[f"tmp{tb}", f"srcu{cp}_{tc_ // 4}"], writes=[f"srca{cp}_{gg}_{tc_}"])
                                a_adv(g)
                                a_adv(g2)
                        for _ in range(8):
                            a_adv(g)
                    gens = [a_s1(i) for i in range(16)] + [None, None]
                    for _ in range(8):
                        a_adv(gens[0])
                    for i in range(16):
                        a_s2(i, gens[i + 1], gens[i + 2])
                    barrier()
                with contextlib.ExitStack() as st2:
                    wq = sb(st2, "wq", [128, 8, 128], BF16)
                    wk = sb(st2, "wk", [128, 8, 128], BF16)
                    wvb = sb(st2, "wvb", [128, 8, 128], BF16)
                    qn = sb(st2, "qz", [128, 2, T], BF16)
                    kn = sb(st2, "kn", [128, T], BF16)
                    P.op("pool", lambda e: e.memset(qn[:], 0.0), writes=["qn"])
                    knp = sb(st2, "knp", [128, T], BF16)
                    vT = sb(st2, "vT", [128, T], BF16)
                    vp = [sb(st2, "vp", [128, 16, 192], BF16) for _ in range(2)]
                    vpp = sb(st2, "vpp", [128, 16, 192], BF16)
                    acc = sb(st2, "acc", [128, 2, T], F32)
                    for vv in vp:
                        P.op("pool", lambda e, vv=vv: e.memset(vv[:, :, 64:128], 1.0), writes=["vp0", "vp1"])
                    sq = [sb(st2, "sqb", [128, 512], BF16) for _ in range(2)]
                    rt = [sb(st2, "rtb", [128, 512], F32) for _ in range(2)]
                    pT = [sb(st2, "pT", [128, 512], BF16) for _ in range(3)]
                    vctr = [0]
                    for hp in range(4):
                        P.dma("pool", wq[:], wv[:, :, 1024 + hp * 128:1024 + (hp + 1) * 128], writes=["wq"])
                        P.dma("pool", wk[:], wv[:, :, 1536 + hp * 128:1536 + (hp + 1) * 128], writes=["wk"])
                        P.dma("pool", wvb[:], wv[:, :, 2048 + hp * 128:2048 + (hp + 1) * 128], writes=["wvb"])
                        if half == 1:
                            P.dma("sp", knp[:], kcar[hp], reads=[f"kcar{hp}"], writes=["knp"])
                        jobs = [(wt_, wn, gcol, dst, dn, tt) for (wt_, wn, gcol, dst, dn) in
                                ((wq, "wq", L_QG, qn, "qn"), (wk, "wk", L_KG, kn, "kn")) for tt in range(NT)]
                        jps = {}

                        def n_s1(j):
                            wt_, wn, gcol, dst, dn, tt = jobs[j]
                            b = j % 2
                            ts = slice(tt * 512, (tt + 1) * 512)
                            pi = next_ps()
                            jps[j] = pi
                            mm(psb[pi][:], [(wt_[:, kc, :], h[:, kc, ts]) for kc in range(8)], reads=[wn, "h"], writes=[f"ps{pi}"])
                            P.op("act", lambda e: e.activation(out=sq[b][:], in_=psb[pi][:], func=AF.Square), reads=[f"ps{pi}"], writes=[f"sqb{b}"])

                        def n_s2(j):
                            wt_, wn, gcol, dst, dn, tt = jobs[j]
                            b = j % 2
                            ts = slice(tt * 512, (tt + 1) * 512)
                            pi = jps[j]
                            p2 = 6
                            mm(psb[p2][:], [(bones[:], sq[b][:])], reads=["bones", f"sqb{b}"], writes=[f"ps{p2}"])
                            P.op("act", lambda e: e.activation(out=rt[b][:], in_=psb[p2][:], func=AF.Ln, bias=EPS, scale=1.0 / 64),
                                 reads=[f"ps{p2}"], writes=[f"rtb{b}"])
                            P.op("act", lambda e: e.activation(out=rt[b][:], in_=rt[b][:], func=AF.Exp, scale=-0.5), reads=[f"rtb{b}"], writes=[f"rtb{b}"])
                            if dn == "qn":
                                for hh in range(2):
                                    hs_ = slice(hh * 64, (hh + 1) * 64)
                                    P.op("dve", lambda e, hh=hh, hs_=hs_: e.scalar_tensor_tensor(
                                        out=dst[hs_, hh, ts], in0=psb[pi][hs_, :], scalar=lpt[hs_, gcol:gcol + 1], in1=rt[b][hs_, :],
                                        op0=ALU.mult, op1=ALU.mult), reads=[f"ps{pi}", f"rtb{b}", "lpt"], writes=[dn])
                            else:
                                P.op("dve", lambda e: e.scalar_tensor_tensor(
                                    out=dst[:, ts], in0=psb[pi][:], scalar=lpt[:, gcol:gcol + 1], in1=rt[b][:], op0=ALU.mult, op1=ALU.mult),
                                    reads=[f"ps{pi}", f"rtb{b}", "lpt"], writes=[dn])
                        for j in range(len(jobs) + 1):
                            if j < len(jobs):
                                n_s1(j)
                            if j >= 1:
                                n_s2(j - 1)
                        if half == 0:
                            P.dma("sp", kcar[hp], kn[:], reads=["kn"], writes=[f"kcar{hp}"])
                        for tt in range(NT):
                            ts = slice(tt * 512, (tt + 1) * 512)
                            pi = next_ps()
                            mm(psb[pi][:], [(wvb[:, kc, :], h[:, kc, ts]) for kc in range(8)], reads=["wvb", "h"], writes=[f"ps{pi}"])
                            P.op("act", lambda e, pi=pi, ts=ts: e.activation(out=vT[:, ts], in_=psb[pi][:], func=AF.Copy),
                                 reads=[f"ps{pi}"], writes=["vT"])
                        for bi, d in enumerate((1, 4, 16)):
                            attn_branch(hp, bi, d, half, vT, qn, kn, knp, vp, vpp, acc, pT, vctr)
                        k_ = 0
                        for hh in range(2):
                            nr = slice(hh * 64, (hh + 1) * 64)
                            dr = slice((1 - hh) * 64, (2 - hh) * 64)
                            for tt in range(NT):
                                ts = slice(tt * 512, (tt + 1) * 512)
                                b = k_ % 2
                                k_ += 1
                                P.op("act", lambda e, b=b, hh=hh, nr=nr, dr=dr, ts=ts: e.activation(out=rt[b][nr, :], in_=acc[dr, hh, ts], func=AF.Ln),
                                     reads=["acc"], writes=[f"rtb{b}"])
                                P.op("act", lambda e, b=b, nr=nr: e.activation(out=rt[b][nr, :], in_=rt[b][nr, :], func=AF.Exp, scale=-1.0),
                                     reads=[f"rtb{b}"], writes=[f"rtb{b}"])
                                P.op("dve", lambda e, b=b, hh=hh, nr=nr, ts=ts, hp=hp: e.tensor_tensor(
                                    out=cat[nr, 4 + hp, ts], in0=acc[nr, hh, ts], in1=rt[b][nr, :], op=ALU.mult),
                                    reads=["acc", f"rtb{b}"], writes=["src"])
                    barrier()

                out_proj(ev_w_out[ei], cat, norm_gcol=(L_GFFN if fuse_ffn_norm else None))

        steps = [(li, layer, half) for li, layer in enumerate(layers) for half in halves]
        xall = lambda c: [xr(c, t_) for t_ in range(NT)]

        def xsrc(si):
            li, layer, half = steps[si]
            return (xT if li == 0 else xs)[half], half

        src0, h0 = xsrc(0)
        for c in range(8):
            P.dma("sp", x[:, c, :], src0[c * 128:(c + 1) * 128, :], reads=[f"xs{h0}_{c}"], writes=xall(c))
        for si, (li, layer, half) in enumerate(steps):
            if half == halves[0]:
                P.dma("sp", lpt[:], lp[layer][:, 0:128], writes=["lpt"])
            last = li == len(layers) - 1
            dst = (outT if last else xs)[half]

            def fin(c, si=si, dst=dst, last=last, half=half):
                P.dma("sp", dst[c * 128:(c + 1) * 128, :], x[:, c, :], reads=xall(c), writes=[f"xs{half}_{c}"], final=last)
                if si + 1 < len(steps):
                    nsrc, nh = xsrc(si + 1)
                    P.dma("sp", x[:, c, :], nsrc[c * 128:(c + 1) * 128, :], reads=[f"xs{nh}_{c}"], writes=xall(c))
            if "mix" in phases:
                rms_norm(L_GMIX)
                if layer % 2 == 0:
                    even(layer // 2, half)
                else:
                    gla(layer // 2, half)
            if "ffn" in phases:
                ffn(layer, half, fin, normed=fuse_ffn_norm)
            else:
                for c in range(8):
                    fin(c)
                barrier()
        P.emit()
    return nc


def _consts():
    c = np.zeros((128, C_N), np.float32)
    p = np.arange(128)[:, None]
    i = np.arange(128)[None, :]
    c[:, C_ID:C_ID + 128] = (p == i)
    c[:, C_BONES:C_BONES + 128] = (p // 64 == i // 64)
    c[:, C_MPREV:C_MPREV + 128] = np.where(i <= p, 0.0, NEGM)
    c[:, C_MCUR:C_MCUR + 128] = np.where(p <= i, 0.0, NEGM)
    c[:, C_TRIL:C_TRIL + 128] = (p <= i)
    c[:, C_M2:C_M2 + 128] = (p <= i) & (p // 64 == i // 64)
    c[:, C_RESET:C_RESET + 512] = (np.arange(512)[None, :] % 64 != 0)
    return c


def _fm(v, n):
    return np.ascontiguousarray(v.reshape(n, 128).T)


def _layer_pack(inp):
    lp = np.zeros((4, 128, L_N), np.float32)
    for l in range(4):
        lp[l, :, L_GMIX:L_GMIX + 8] = _fm(inp["norm_mix_g"][l], 8)
        lp[l, :, L_GFFN:L_GFFN + 8] = _fm(inp["norm_ffn_g"][l], 8)
        for k, off in enumerate((L_CW0, L_CW1, L_CW2)):
            lp[l, :, off:off + NFC] = _fm(inp["ffn_conv_w"][l, k], NFC)
        lp[l, :, L_CB:L_CB + NFC] = _fm(inp["ffn_conv_b"][l], NFC)
        if l % 2 == 0:
            e = l // 2
            lp[l, :, L_QG] = np.tile(inp["ev_q_g"][e], 2)
            lp[l, :, L_KG] = np.tile(inp["ev_k_g"][e], 2)
            bs = inp["ev_a_bs"][e]
            for cp in range(4):
                lp[l, 0:64, L_BIAS + cp * 128:L_BIAS + (cp + 1) * 128] = bs[2 * cp][None, :]
                lp[l, 64:128, L_BIAS + cp * 128:L_BIAS + (cp + 1) * 128] = bs[2 * cp + 1][None, :]
            lp[l, :, L_LNG:L_LNG + 512] = inp["ev_a_ln_g"][e][None, :]
            lp[l, :, L_LNB:L_LNB + 512] = inp["ev_a_ln_b"][e][None, :]
            lp[l, :, L_LNGF:L_LNGF + 4] = _fm(inp["ev_a_ln_g"][e], 4)
            lp[l, :, L_LNBF:L_LNBF + 4] = _fm(inp["ev_a_ln_b"][e], 4)
        else:
            o = l // 2
            lp[l, :, L_BA:L_BA + 4] = _fm(inp["od_b_a"][o], 4)
            lp[l, :, L_HG:L_HG + 2] = _fm(inp["od_head_g"][o], 2)
    return lp


def make_in_maps(inp, seqs):
    consts = _consts()
    lp = _layer_pack(inp)
    wsT = np.ascontiguousarray(np.transpose(inp["ev_a_ws"], (0, 3, 1, 2)))
    shared = {
        "consts": consts, "lp": lp,
        "ev_w_in": np.ascontiguousarray(inp["ev_w_in"]), "ev_wsT": wsT,
        "ev_w_out": np.ascontiguousarray(inp["ev_w_out"]),
        "od_w_in": np.ascontiguousarray(inp["od_w_in"]), "od_w_a2": np.ascontiguousarray(inp["od_w_a2"]),
        "od_w_out": np.ascontiguousarray(inp["od_w_out"]),
        "ffn_w_gate": np.ascontiguousarray(inp["ffn_w_gate"]), "ffn_w_up": np.ascontiguousarray(inp["ffn_w_up"]),
        "ffn_w_down": np.ascontiguousarray(inp["ffn_w_down"]),
    }
    maps = []
    for b in seqs:
        xb = np.asarray(inp["x"][b], np.float32)
        xTt = np.ascontiguousarray(xb.reshape(2, T, 1024).transpose(0, 2, 1))
        m = dict(shared)
        m["xT"] = xTt
        maps.append(m)
    return maps


_NC = {}


def kernel(**inputs):
    inp = {k: np.asarray(v) for k, v in inputs.items()}
    if "full" not in _NC:
        _NC["full"] = build()
    nc = _NC["full"]
    maps = make_in_maps(inp, range(4))
    res = run_bass_kernel_spmd(nc, maps, core_ids=list(range(4)))
    out = np.empty((4, 2 * T, 1024), np.float32)
    for b in range(4):
        o = res.results[b]["outT"]
        out[b] = o.transpose(0, 2, 1).reshape(2 * T, 1024)
    return out
```

```python
import contextlib
import numpy as np
import concourse.bass as bass
import concourse.mybir as mybir
from concourse.bass_utils import run_bass_kernel_spmd

F32 = mybir.dt.float32
BF16 = mybir.dt.bfloat16
AF = mybir.ActivationFunctionType
ALU = mybir.AluOpType

ENGS = ("pe", "act", "dve", "pool", "sp")
SEM_CHUNK = 12000
DMA_RING = 6
EPS = 1e-6
T = 2048
NT = 4
DFF = 2816
NFC = 22
NEGM = -30000.0


class Op:
    __slots__ = ("eng", "fn", "reads", "writes", "is_dma", "deps", "signal",
                 "sem", "val", "ring_prev", "idx")


class Prog:
    def __init__(self, nc):
        self.nc = nc
        self.ops = []
        self.final_waits = []

    def op(self, eng, fn, reads=(), writes=(), is_dma=False, barrier=False):
        o = Op()
        o.eng, o.fn, o.is_dma = eng, fn, is_dma
        psr_ = tuple(r for r in reads if r.startswith("ps") and r[2:].isdigit() or r == "pst")
        o.reads = tuple(reads) + (() if barrier else ("PHASE",))
        o.writes = tuple(writes) + psr_ + (("PHASE",) if barrier else ())
        o.deps = []
        o.signal = is_dma
        o.sem = None
        o.val = 0
        o.ring_prev = None
        o.idx = len(self.ops)
        self.ops.append(o)
        return o

    def dma(self, eng, out, in_, reads=(), writes=(), final=False):
        o = self.op(eng, lambda e, out=out, in_=in_: e.dma_start(out=out, in_=in_),
                    reads, writes, is_dma=True)
        if final:
            self.final_waits.append(o)
        return o

    def _analyze(self):
        last_w = {}
        readers = {}
        ops = self.ops
        for o in ops:
            deps = {}
            for r in o.reads:
                lw = last_w.get(r)
                if lw is not None:
                    deps[lw] = "raw"
            for w in o.writes:
                lw = last_w.get(w)
                if lw is not None and lw not in deps:
                    deps[lw] = "waw"
                for rd in readers.get(w, ()):
                    if rd not in deps:
                        deps[rd] = "war"
            deps.pop(o.idx, None)
            for r in o.reads:
                readers.setdefault(r, []).append(o.idx)
            for w in o.writes:
                last_w[w] = o.idx
                readers[w] = []
            best = {}
            dma_deps = []
            for d, kind in deps.items():
                od = ops[d]
                if od.is_dma:
                    dma_deps.append(d)
                    continue
                if od.eng == o.eng and not o.is_dma:
                    if od.eng in ("pe", "sp"):
                        continue
                if od.eng not in best or best[od.eng] < d:
                    best[od.eng] = d
            o.deps = sorted(list(best.values()) + dma_deps)
        waited = {e: {} for e in ENGS}
        waited_dma = {e: set() for e in ENGS}
        for o in ops:
            nd = []
            for d in o.deps:
                od = ops[d]
                if od.is_dma:
                    if d in waited_dma[o.eng]:
                        continue
                    waited_dma[o.eng].add(d)
                    nd.append(d)
                else:
                    if waited[o.eng].get(od.eng, -1) >= d:
                        continue
                    waited[o.eng][od.eng] = d
                    nd.append(d)
            o.deps = nd
            for d in nd:
                ops[d].signal = True

    def emit(self):
        nc = self.nc
        self._analyze()
        with contextlib.ExitStack() as stack:
            cnt = {e: 0 for e in ENGS}
            sems = {e: [] for e in ENGS}
            rings = {e: [] for e in ENGS}
            ring_cnt = {e: 0 for e in ENGS}
            ring_ord = {}
            ring_last = {}
            for o in self.ops:
                if o.is_dma:
                    j = ring_cnt[o.eng] % DMA_RING
                    ring_cnt[o.eng] += 1
                    if len(rings[o.eng]) <= j:
                        rings[o.eng].append(stack.enter_context(nc.semaphore(f"dr_{o.eng}_{j}")))
                    key = (o.eng, j)
                    ring_ord[key] = ring_ord.get(key, 0) + 1
                    o.sem = rings[o.eng][j]
                    o.val = 16 * ring_ord[key]
                    o.ring_prev = ring_last.get(key)
                    ring_last[key] = o
                elif o.signal:
                    c = cnt[o.eng]
                    k = c // SEM_CHUNK
                    if len(sems[o.eng]) <= k:
                        sems[o.eng].append(stack.enter_context(nc.semaphore(f"cs_{o.eng}_{k}")))
                    o.sem = sems[o.eng][k]
                    o.val = (c % SEM_CHUNK) + 1
                    cnt[o.eng] += 1
            block = stack.enter_context(nc.Block())
            ops = self.ops
            finals = self.final_waits

            def run(eng_name):
                def body(e):
                    for o in ops:
                        if o.eng != eng_name:
                            continue
                        if o.ring_prev is not None:
                            e.wait_ge(o.ring_prev.sem, o.ring_prev.val)
                        for d in o.deps:
                            od = ops[d]
                            e.wait_ge(od.sem, od.val)
                        ins = o.fn(e)
                        if o.signal:
                            ins.then_inc(o.sem, 16 if o.is_dma else 1)
                    for o in finals:
                        if o.eng == eng_name:
                            e.wait_ge(o.sem, o.val)
                return body

            block.tensor(run("pe"))
            block.scalar(run("act"))
            block.vector(run("dve"))
            block.gpsimd(run("pool"))
            block.sync(run("sp"))


C_ID, C_BONES, C_MPREV, C_MCUR, C_TRIL, C_M2, C_RESET, C_N = 0, 128, 256, 384, 512, 640, 768, 1280
L_GMIX, L_GFFN, L_CW0, L_CW1, L_CW2, L_CB, L_X = 0, 8, 16, 38, 60, 82, 104
L_QG, L_KG, L_BIAS, L_LNG, L_LNB = 104, 105, 106, 106 + 512, 106 + 1024
L_BA, L_HG = 104, 108
L_LNGF, L_LNBF = 106 + 1536, 106 + 1536 + 4
L_N = 106 + 1536 + 8


def build(layers=(0, 1, 2, 3), halves=(0, 1), phases=("mix", "ffn")):
    nc = bass.Bass("TRN2", target_bir_lowering=False)

    def D(name, shape, kind="ExternalInput", dt=F32):
        return nc.dram_tensor(name, shape, dt, kind=kind).ap()

    xT = D("xT", [2, 1024, T])
    outT = D("outT", [2, 1024, T], "ExternalOutput")
    xs = D("xs", [2, 1024, T], "Internal")
    consts = D("consts", [128, C_N])
    lp = D("lp", [4, 128, L_N])
    has_even = any(l % 2 == 0 for l in layers) and "mix" in phases
    has_odd = any(l % 2 == 1 for l in layers) and "mix" in phases
    has_ffn = "ffn" in phases
    ev_w_in = D("ev_w_in", [2, 1024, 2560]) if has_even else None
    ev_wsT = D("ev_wsT", [2, 128, 8, 128]) if has_even else None
    ev_w_out = D("ev_w_out", [2, 1024, 1024]) if has_even else None
    od_w_in = D("od_w_in", [2, 1024, 3088]) if has_odd else None
    od_w_a2 = D("od_w_a2", [2, 16, 512]) if has_odd else None
    od_w_out = D("od_w_out", [2, 1024, 1024]) if has_odd else None
    w_gate = D("ffn_w_gate", [4, 1024, DFF]) if has_ffn else None
    w_up = D("ffn_w_up", [4, 1024, DFF]) if has_ffn else None
    w_down = D("ffn_w_down", [4, DFF, 1024]) if has_ffn else None
    kcar = D("kcar", [4, 128, T], "Internal", BF16)
    vcar = D("vcar", [3, 4, 128, 16 * 192], "Internal", BF16)

    P = Prog(nc)
    fuse_ffn_norm = ("mix" in phases) and ("ffn" in phases)
    top = contextlib.ExitStack()
    uid = [0]

    def sb(stack, name, shape, dt):
        uid[0] += 1
        return stack.enter_context(nc.sbuf_tensor(f"{name}_{uid[0]}", shape, dt))

    with top:
        x = sb(top, "x", [128, 8, T], F32)
        h = sb(top, "h", [128, 8, T], BF16)
        cst = sb(top, "cst", [128, C_N - C_TRIL], F32)
        lpt = sb(top, "lpt", [128, 128], F32)
        ident = sb(top, "ident", [128, 128], BF16)
        ones = sb(top, "ones", [128, 128], BF16)
        bones = sb(top, "bones", [128, 128], BF16)
        maskb4 = sb(top, "maskb4", [128, 512], BF16)
        state = sb(top, "state", [128, 4, 256], F32)
        gtail = sb(top, "gtail", [128, NFC, 2], F32)
        dummy = sb(top, "dummy", [128, 8], F32)
        psb = [top.enter_context(nc.psum_tensor(f"ps{i}", [128, 512], F32)) for i in range(7)]
        pst = top.enter_context(nc.psum_tensor("pst", [128, 1024], BF16))

        def barrier():
            P.op("pool", lambda e: e.memset(dummy[:], 0.0), barrier=True)

        def mm(out, pairs, reads, writes, start=True):
            def fn(e):
                n = len(pairs)
                ins = None
                for i, (l, r) in enumerate(pairs):
                    ins = e.matmul(out, l, r, start=(start and i == 0), stop=(i == n - 1),
                                   skip_group_check=True)
                return ins
            P.op("pe", fn, reads, writes)

        psr = [0]

        def next_ps(n=1):
            i = psr[0] % 6
            psr[0] += 1
            return i

        P.dma("sp", cst[:], consts[:, C_TRIL:C_N], writes=["cst"])
        P.dma("pool", ident[:], consts[:, C_ID:C_ID + 128], writes=["ident"])
        P.dma("pool", bones[:], consts[:, C_BONES:C_BONES + 128], writes=["bones"])
        for q_, c_ in enumerate((C_MPREV, C_MPREV, C_MCUR, C_MCUR)):
            P.dma("pool", maskb4[:, q_ * 128:(q_ + 1) * 128], consts[:, c_:c_ + 128], writes=["maskb4"])
        P.op("dve", lambda e: e.memset(ones[:], 1.0), writes=["ones"])

        def xr(c, tt):
            return f"x{c}_{tt}"

        def norm_tile(tt, gcol, sq, rt, rs):
            b = tt % 2
            ts = slice(tt * 512, (tt + 1) * 512)
            pi = next_ps()
            P.op("act", lambda e: e.activation(out=sq[b][:], in_=x[:, :, ts], func=AF.Square),
                 reads=[xr(k, tt) for k in range(8)], writes=[f"sq{b}"])
            mm(psb[pi][:], [(ones[:], sq[b][:, kc, :]) for kc in range(8)],
               reads=[f"sq{b}", "ones"], writes=[f"ps{pi}"])
            P.op("act", lambda e: e.activation(out=rt[b][:], in_=psb[pi][:], func=AF.Ln, bias=EPS, scale=1.0 / 1024),
                 reads=[f"ps{pi}"], writes=[f"rt{b}"])
            P.op("act", lambda e: e.activation(out=rs[b][:], in_=rt[b][:], func=AF.Exp, scale=-0.5),
                 reads=[f"rt{b}"], writes=[f"rs{b}"])
            for kc in range(8):
                P.op("dve", lambda e, kc=kc: e.scalar_tensor_tensor(
                    out=h[:, kc, ts], in0=x[:, kc, ts], scalar=lpt[:, gcol + kc:gcol + kc + 1],
                    in1=rs[b][:], op0=ALU.mult, op1=ALU.mult),
                    reads=[xr(kc, tt), f"rs{b}", "lpt"], writes=[f"h{kc}_{tt}"])

        def rms_norm(gcol):
            with contextlib.ExitStack() as st:
                sq = [sb(st, "sq", [128, 8, 512], BF16) for _ in range(2)]
                rt = [sb(st, "rt", [128, 512], F32) for _ in range(2)]
                rs = [sb(st, "rs", [128, 512], F32) for _ in range(2)]
                for tt in range(NT):
                    norm_tile(tt, gcol, sq, rt, rs)
                barrier()

        def out_proj(w_dram, src, nkc=8, norm_gcol=None):
            with contextlib.ExitStack() as st:
                wb = [sb(st, "wo", [128, nkc, 256], BF16) for _ in range(4)]
                wv = w_dram.rearrange("(kc p) n -> p kc n", p=128)
                if norm_gcol is not None:
                    sq = [sb(st, "sq", [128, 8, 512], BF16) for _ in range(2)]
                    rt = [sb(st, "rt", [128, 512], F32) for _ in range(2)]
                    rs = [sb(st, "rs", [128, 512], F32) for _ in range(2)]
                for blk in range(4):
                    P.dma("pool", wb[blk][:], wv[:, :, blk * 256:(blk + 1) * 256], writes=[f"wo{blk}"])

                def op_tile(tt):
                    ts = slice(tt * 512, (tt + 1) * 512)
                    for blk in range(4):
                        for dc in range(2):
                            dch = blk * 2 + dc
                            pi = next_ps()
                            mm(psb[pi][:], [(wb[blk][:, kc, dc * 128:(dc + 1) * 128], src[:, kc, ts]) for kc in range(nkc)],
                               reads=[f"wo{blk}", "src"], writes=[f"ps{pi}"])
                            P.op("dve", lambda e, pi=pi, dch=dch: e.tensor_tensor(
                                out=x[:, dch, ts], in0=x[:, dch, ts], in1=psb[pi][:], op=ALU.add),
                                reads=[f"ps{pi}", xr(dch, tt)], writes=[xr(dch, tt)])
                for i in range(NT + 1):
                    if i < NT:
                        op_tile(i)
                    if i >= 1 and norm_gcol is not None:
                        norm_tile(i - 1, norm_gcol, sq, rt, rs)
                barrier()

        def ffn(layer, half, fin=None, normed=False):
            if not normed:
                rms_norm(L_GFFN)
            wgv = w_gate[layer].rearrange("(kc p) n -> p kc n", p=128)
            wuv = w_up[layer].rearrange("(kc p) n -> p kc n", p=128)
            wdv = w_down[layer].rearrange("(kc p) n -> p kc n", p=128)
            if half == 0:
                P.op("dve", lambda e: e.memset(gtail[:], 0.0), writes=["gtail"])
            with contextlib.ExitStack() as st:
                hid = sb(st, "hid", [128, NFC, 1024], BF16)
                wg = [sb(st, "wg", [128, 8, 256], BF16) for _ in range(2)]
                wu = [sb(st, "wu", [128, 8, 256], BF16) for _ in range(2)]
                gs = [sb(st, "gs", [128, 514], F32) for _ in range(2)]
                ac = [sb(st, "ac", [128, 512], F32) for _ in range(2)]
                wd = [sb(st, "wd", [128, NFC, 256], BF16) for _ in range(2)]
                it = 0
                for sh in range(2):
                    for blk in range(11):
                        b = blk % 2
                        P.dma("pool", wg[b][:], wgv[:, :, blk * 256:(blk + 1) * 256], writes=[f"wg{b}"])
                        P.dma("pool", wu[b][:], wuv[:, :, blk * 256:(blk + 1) * 256], writes=[f"wu{b}"])
                        for fc2 in range(2):
                            fc = blk * 2 + fc2
                            cs = slice(fc2 * 128, (fc2 + 1) * 128)
                            for t2 in range(2):
                                tt = sh * 2 + t2
                                ts = slice(tt * 512, (tt + 1) * 512)
                                hs = slice(t2 * 512, (t2 + 1) * 512)
                                s = it % 2
                                it += 1
                                pg = next_ps()
                                pu = next_ps()
                                mm(psb[pg][:], [(wg[b][:, kc, cs], h[:, kc, ts]) for kc in range(8)],
                                   reads=[f"wg{b}", "h"], writes=[f"ps{pg}"])
                                mm(psb[pu][:], [(wu[b][:, kc, cs], h[:, kc, ts]) for kc in range(8)],
                                   reads=[f"wu{b}", "h"], writes=[f"ps{pu}"])
                                P.op("dve", lambda e, s=s, fc=fc: e.tensor_copy(out=gs[s][:, 0:2], in_=gtail[:, fc, :]),
                                     reads=[f"gtail{fc}", "gtail"], writes=[f"gs{s}"])
                                P.op("act", lambda e, s=s, pg=pg: e.activation(out=gs[s][:, 2:514], in_=psb[pg][:], func=AF.Copy),
                                     reads=[f"ps{pg}"], writes=[f"gs{s}b"])
                                P.op("dve", lambda e, s=s, fc=fc: e.tensor_copy(out=gtail[:, fc, :], in_=gs[s][:, 512:514]),
                                     reads=[f"gs{s}b"], writes=[f"gtail{fc}"])
                                P.op("dve", lambda e, s=s, fc=fc: e.tensor_scalar(
                                    out=ac[s][:], in0=gs[s][:, 2:514], scalar1=lpt[:, L_CW2 + fc:L_CW2 + fc + 1],
                                    scalar2=lpt[:, L_CB + fc:L_CB + fc + 1], op0=ALU.mult, op1=ALU.add),
                                    reads=[f"gs{s}b", "lpt"], writes=[f"ac{s}"])
                                P.op("dve", lambda e, s=s, fc=fc: e.scalar_tensor_tensor(
                                    out=ac[s][:], in0=gs[s][:, 1:513], scalar=lpt[:, L_CW1 + fc:L_CW1 + fc + 1],
                                    in1=ac[s][:], op0=ALU.mult, op1=ALU.add),
                                    reads=[f"gs{s}b", f"gs{s}", f"ac{s}", "lpt"], writes=[f"ac{s}"])
                                P.op("dve", lambda e, s=s, fc=fc: e.scalar_tensor_tensor(
                                    out=ac[s][:], in0=gs[s][:, 0:512], scalar=lpt[:, L_CW0 + fc:L_CW0 + fc + 1],
                                    in1=ac[s][:], op0=ALU.mult, op1=ALU.add),
                                    reads=[f"gs{s}b", f"gs{s}", f"ac{s}", "lpt"], writes=[f"ac{s}"])
                                P.op("act", lambda e, s=s: e.activation(out=ac[s][:], in_=ac[s][:], func=AF.Silu),
                                     reads=[f"ac{s}"], writes=[f"ac{s}"])
                                P.op("dve", lambda e, s=s, pu=pu, fc=fc, hs=hs: e.tensor_tensor(
                                    out=hid[:, fc, hs], in0=ac[s][:], in1=psb[pu][:], op=ALU.mult),
                                    reads=[f"ac{s}", f"ps{pu}"], writes=["hid"])
                    for blk in range(4):
                        b = blk % 2
                        P.dma("pool", wd[b][:], wdv[:, :, blk * 256:(blk + 1) * 256], writes=[f"wd{b}"])
                        for dc in range(2):
                            dch = blk * 2 + dc
                            for t2 in range(2):
                                tt = sh * 2 + t2
                                ts = slice(tt * 512, (tt + 1) * 512)
                                hs = slice(t2 * 512, (t2 + 1) * 512)
                                pi = next_ps()
                                mm(psb[pi][:], [(wd[b][:, kc, dc * 128:(dc + 1) * 128], hid[:, kc, hs]) for kc in range(NFC)],
                                   reads=[f"wd{b}", "hid"], writes=[f"ps{pi}"])
                                P.op("dve", lambda e, pi=pi, dch=dch, ts=ts: e.tensor_tensor(
                                    out=x[:, dch, ts], in0=x[:, dch, ts], in1=psb[pi][:], op=ALU.add),
                                    reads=[f"ps{pi}", xr(dch, tt)], writes=[xr(dch, tt)])
                            if sh == 1 and fin is not None:
                                fin(dch)
                barrier()

        def gla(o, half):
            wv = od_w_in[o].rearrange("(kc p) n -> p kc n", p=128)
            with contextlib.ExitStack() as st:
                cat = sb(st, "cat", [128, 8, T], BF16)
                with contextlib.ExitStack() as st2:
                    wq = sb(st2, "wq", [128, 8, 128], BF16)
                    wk = sb(st2, "wk", [128, 8, 128], BF16)
                    wvv = sb(st2, "wvv", [128, 8, 256], BF16)
                    wr = sb(st2, "wr", [128, 8, 256], BF16)
                    wga = sb(st2, "wga", [128, 8, 16], BF16)
                    wa2 = sb(st2, "wa2", [16, 512], F32)
                    nba = sb(st2, "nba", [128, 4], F32)
                    gaT = sb(st2, "gaT", [16, T], F32)
                    e1 = sb(st2, "e1", [128, 512], F32)
                    b16 = sb(st2, "b16", [128, 512], F32)
                    eb = sb(st2, "eb", [128, 512], F32)
                    enb = sb(st2, "enb", [128, 512], F32)
                    ed = sb(st2, "ed", [128, 512], F32)
                    ks = sb(st2, "ks", [128, 512], BF16)
                    dec = [sb(st2, "dec", [128, 8], F32) for _ in range(2)]
                    qt = [sb(st2, "qt", [128, 512], BF16) for _ in range(2)]
                    kt = [sb(st2, "kt", [128, 512], BF16) for _ in range(2)]
                    ksT = [sb(st2, "ksT", [128, 4, 128], BF16) for _ in range(2)]
                    vtm = [sb(st2, "vtm", [128, 4, 256], BF16) for _ in range(2)]
                    sr = [sb(st2, "sr", [128, 2, 512], BF16) for _ in range(2)]
                    aT = [sb(st2, "aT", [128, 128], BF16) for _ in range(2)]
                    ot = sb(st2, "ot", [128, 2, 512], F32)
                    osq = sb(st2, "osq", [128, 2, 512], BF16)
                    ort = sb(st2, "ort", [128, 512], F32)
                    stb = [sb(st2, "stb", [128, 256], BF16) for _ in range(2)]
                    P.dma("pool", wga[:], wv[:, :, 3072:3088], writes=["wga"])
                    P.dma("sp", wa2[:], od_w_a2[o], writes=["wa2"])
                    P.op("dve", lambda e: e.tensor_scalar(out=nba[:], in0=lpt[:, L_BA:L_BA + 4], scalar1=-1.0, scalar2=None,
                                                          op0=ALU.mult), reads=["lpt"], writes=["nba"])
                    if half == 0:
                        P.op("dve", lambda e: e.memset(state[:], 0.0), writes=["state"])
                    for tt in range(NT):
                        ts = slice(tt * 512, (tt + 1) * 512)
                        pi = next_ps()
                        mm(psb[pi][0:16, :], [(wga[:, kc, :], h[:, kc, ts]) for kc in range(8)],
                           reads=["wga", "h"], writes=[f"ps{pi}"])
                        P.op("act", lambda e, pi=pi, ts=ts: e.activation(out=gaT[:, ts], in_=psb[pi][0:16, :], func=AF.Copy),
                             reads=[f"ps{pi}"], writes=["gaT"])
                    seq = [(hd, tt) for hd in range(4) for tt in range(NT)]
                    pctr = [0]
                    cctr = [0]

                    def ps_p():
                        pctr[0] += 1
                        return pctr[0] % 3

                    def ps_o():
                        cctr[0] += 1
                        return 3 + cctr[0] % 2

                    def mm_split(out, pairs, reads, writes, n=4):
                        for k0 in range(0, len(pairs), n):
                            mm(out, pairs[k0:k0 + n], reads=reads, writes=writes, start=(k0 == 0))
                            yield

                    def pgen(i):
                        hd, tt = seq[i]
                        z = i % 2
                        ts = slice(tt * 512, (tt + 1) * 512)
                        if tt == 0:
                            P.dma("pool", wq[:], wv[:, :, hd * 128:(hd + 1) * 128], writes=["wq"])
                            P.dma("pool", wk[:], wv[:, :, 512 + hd * 128:512 + (hd + 1) * 128], writes=["wk"])
                            P.dma("pool", wvv[:], wv[:, :, 1024 + hd * 256:1024 + (hd + 1) * 256], writes=["wvv"])
                            P.dma("pool", wr[:], wv[:, :, 2048 + hd * 256:2048 + (hd + 1) * 256], writes=["wr"])
                        pi = ps_p()
                        mm(psb[pi][:], [(wa2[:, hd * 128:(hd + 1) * 128], gaT[:, ts])], reads=["wa2", "gaT"], writes=[f"ps{pi}"])
                        P.op("act", lambda e: e.activation(out=e1[:], in_=psb[pi][:], func=AF.Exp, bias=nba[:, hd:hd + 1], scale=-1.0),
                             reads=[f"ps{pi}", "nba"], writes=["e1"])
                        yield
                        P.op("act", lambda e: e.activation(out=e1[:], in_=e1[:], func=AF.Ln, bias=1.0, scale=1.0),
                             reads=["e1"], writes=["e1"])
                        yield
                        P.op("dve", lambda e: e.tensor_tensor_scan(out=b16[:], data0=cst[:, 256:768],
                                                                 data1=e1[:], initial=0.0, op0=ALU.mult, op1=ALU.add),
                             reads=["e1", "cst"], writes=["b16"])
                        yield
                        P.op("dve", lambda e: e.tensor_tensor(
                            out=e1[:].rearrange("p (c j) -> p c j", j=64),
                            in0=b16[:].rearrange("p (c j) -> p c j", j=64)[:, :, 63:64].to_broadcast([128, 8, 64]),
                            in1=b16[:].rearrange("p (c j) -> p c j", j=64), op=ALU.subtract),
                            reads=["b16", "e1"], writes=["e1"])
                        P.op("act", lambda e: e.activation(out=eb[:], in_=b16[:], func=AF.Exp, scale=-1.0 / 16), reads=["b16"], writes=["eb"])
                        yield
                        P.op("act", lambda e: e.activation(out=enb[:], in_=b16[:], func=AF.Exp, scale=1.0 / 16), reads=["b16"], writes=["enb"])
                        yield
                        P.op("act", lambda e: e.activation(out=ed[:], in_=e1[:], func=AF.Exp, scale=-1.0 / 16), reads=["e1"], writes=["ed"])
                        P.op("act", lambda e: e.activation(out=dec[z][:], in_=b16[:, 63:512:64], func=AF.Exp, scale=-1.0 / 16),
                             reads=["b16"], writes=[f"dec{z}"])
                        yield
                        pq = ps_p()
                        yield from mm_split(psb[pq][:], [(wq[:, kc, :], h[:, kc, ts]) for kc in range(8)], ["wq", "h"], [f"ps{pq}"])
                        P.op("dve", lambda e: e.scalar_tensor_tensor(out=qt[z][:], in0=psb[pq][:], scalar=128.0 ** -0.5,
                                                                   in1=eb[:], op0=ALU.mult, op1=ALU.mult),
                             reads=[f"ps{pq}", "eb"], writes=[f"qt{z}"])
                        yield
                        pk = ps_p()
                        yield from mm_split(psb[pk][:], [(wk[:, kc, :], h[:, kc, ts]) for kc in range(8)], ["wk", "h"], [f"ps{pk}"])
                        P.op("dve", lambda e: e.tensor_tensor(out=kt[z][:], in0=psb[pk][:], in1=enb[:], op=ALU.mult),
                             reads=[f"ps{pk}", "enb"], writes=[f"kt{z}"])
                        yield
                        P.op("dve", lambda e: e.tensor_tensor(out=ks[:], in0=psb[pk][:], in1=ed[:], op=ALU.mult),
                             reads=[f"ps{pk}", "ed"], writes=["ks"])
                        yield

                        def tr(e):
                            ins = None
                            for bl in range(4):
                                ins = e.transpose(pst[:, bl * 128:(bl + 1) * 128], ks[:, bl * 128:(bl + 1) * 128], ident[:])
                            return ins
                        P.op("pe", tr, reads=["ks", "ident"], writes=["pst"])
                        P.op("act", lambda e: e.activation(out=ksT[z][:].rearrange("p a b -> p (a b)"), in_=pst[:, 0:512], func=AF.Copy),
                             reads=["pst"], writes=[f"ksT{z}"])
                        yield
                        for bp in range(2):
                            pv = ps_p()
                            for b2 in range(2):
                                bl = bp * 2 + b2
                                t0 = tt * 512 + bl * 128
                                pairs = [(h[:, kc, t0:t0 + 128], wvv[:, kc, :]) for kc in range(8)]
                                for k0 in (0, 4):
                                    mm(psb[pv][:, b2 * 256:(b2 + 1) * 256], pairs[k0:k0 + 4], reads=["wvv", "h"], writes=[f"ps{pv}"],
                                       start=(k0 == 0 and b2 == 0))
                                    yield
                            P.op("act", lambda e, pv=pv, bp=bp: e.activation(
                                out=vtm[z][:, bp * 2:bp * 2 + 2, :].rearrange("p a b -> p (a b)"), in_=psb[pv][:], func=AF.Copy),
                                reads=[f"ps{pv}"], writes=[f"vtm{z}"])
                            yield
                        for dvc in range(2):
                            pr = ps_p()
                            yield from mm_split(psb[pr][:], [(wr[:, kc, dvc * 128:(dvc + 1) * 128], h[:, kc, ts]) for kc in range(8)],
                                                ["wr", "h"], [f"ps{pr}"])
                            P.op("act", lambda e, pr=pr, dvc=dvc: e.activation(out=sr[z][:, dvc, :], in_=psb[pr][:], func=AF.Silu),
                                 reads=[f"ps{pr}"], writes=[f"sr{z}"])
                            yield

                    def adv(g, n=1):
                        if g is None:
                            return
                        for _ in range(n):
                            try:
                                next(g)
                            except StopIteration:
                                return

                    def cstage(i, g):
                        hd, tt = seq[i]
                        z = i % 2
                        ts = slice(tt * 512, (tt + 1) * 512)
                        for bl in range(4):
                            bs_ = slice(bl * 128, (bl + 1) * 128)
                            az = bl % 2
                            pa = 5
                            mm(psb[pa][:, 0:128], [(kt[z][:, bs_], qt[z][:, bs_])], reads=[f"kt{z}", f"qt{z}"], writes=[f"ps{pa}"])
                            P.op("dve", lambda e, pa=pa, az=az: e.tensor_tensor(out=aT[az][:], in0=psb[pa][:, 0:128],
                                                                             in1=cst[:, 128:256], op=ALU.mult),
                                 reads=[f"ps{pa}", "cst"], writes=[f"aT{az}"])
                            adv(g)
                            po = ps_o()
                            for dvc in range(2):
                                mm(psb[po][:, dvc * 128:(dvc + 1) * 128], [(vtm[z][:, bl, dvc * 128:(dvc + 1) * 128], aT[az][:])],
                                   reads=[f"vtm{z}", f"aT{az}"], writes=[f"ps{po}"], start=(dvc == 0))
                            for c2 in range(2):
                                ci = bl * 2 + c2
                                cs = slice(bl * 128 + c2 * 64, bl * 128 + (c2 + 1) * 64)
                                pp = slice(c2 * 64, (c2 + 1) * 64)
                                P.op("dve", lambda e, c2=c2: e.tensor_copy(out=stb[c2][:], in_=state[:, hd, :]),
                                     reads=["state"], writes=[f"stb{c2}"])
                                mm(psb[6][:, 0:256], [(ksT[z][pp, bl, :], vtm[z][pp, bl, :])], reads=[f"ksT{z}", f"vtm{z}"], writes=["ps6"])
                                adv(g)
                                P.op("dve", lambda e, ci=ci: e.scalar_tensor_tensor(
                                    out=state[:, hd, :], in0=state[:, hd, :], scalar=dec[z][:, ci:ci + 1], in1=psb[6][:, 0:256],
                                    op0=ALU.mult, op1=ALU.add), reads=["state", f"dec{z}", "ps6", f"stb{c2}"], writes=["state"])
                                for dvc in range(2):
                                    mm(psb[po][:, dvc * 128 + c2 * 64:dvc * 128 + (c2 + 1) * 64],
                                       [(stb[c2][:, dvc * 128:(dvc + 1) * 128], qt[z][:, cs])],
                                       reads=[f"stb{c2}", f"qt{z}"], writes=[f"ps{po}"], start=False)
                                adv(g, 2)
                            P.op("act", lambda e, po=po, bs_=bs_: e.activation(
                                out=ot[:, :, bs_], in_=psb[po][:, 0:256].rearrange("p (a b) -> p a b", a=2), func=AF.Copy),
                                reads=[f"ps{po}"], writes=["ot"])
                        P.op("act", lambda e: e.activation(out=osq[:], in_=ot[:], func=AF.Square), reads=["ot"], writes=["osq"])
                        pn = 5
                        mm(psb[pn][:], [(ones[:], osq[:, dvc, :]) for dvc in range(2)], reads=["ones", "osq"], writes=[f"ps{pn}"])
                        P.op("act", lambda e: e.activation(out=ort[:], in_=psb[pn][:], func=AF.Ln, bias=EPS, scale=1.0 / 256),
                             reads=[f"ps{pn}"], writes=["ort"])
                        P.op("act", lambda e: e.activation(out=ort[:], in_=ort[:], func=AF.Exp, scale=-0.5), reads=["ort"], writes=["ort"])
                        adv(g, 2)
                        for dvc in range(2):
                            P.op("dve", lambda e, dvc=dvc: e.scalar_tensor_tensor(
                                out=ot[:, dvc, :], in0=ot[:, dvc, :], scalar=lpt[:, L_HG + dvc:L_HG + dvc + 1], in1=ort[:],
                                op0=ALU.mult, op1=ALU.mult), reads=["ot", "ort", "lpt"], writes=["ot"])
                            P.op("dve", lambda e, dvc=dvc: e.tensor_tensor(
                                out=cat[:, hd * 2 + dvc, ts], in0=ot[:, dvc, :], in1=sr[z][:, dvc, :], op=ALU.mult),
                                reads=["ot", f"sr{z}"], writes=["src"])
                        adv(g, 100)

                    g0 = pgen(0)
                    adv(g0, 100)
                    for i in range(len(seq)):
                        g = pgen(i + 1) if i + 1 < len(seq) else None
                        cstage(i, g)
                    barrier()
                out_proj(od_w_out[o], cat, norm_gcol=(L_GFFN if fuse_ffn_norm else None))


        def attn_branch(hp, bi, d, half, vT, qn, kn, knp, vp, vpp, acc, pT, vctr):
            nbl = 16 // d
            vb = vctr[0] % 2
            vctr[0] += 1
            vpc = vp[vb]
            vpn = f"vp{vb}"

            def cols(r, n):
                a = r + d * 128 * n
                return slice(a, a + d * 127 + 1, d)
            if half == 1:
                P.dma("sp", vpp[:].rearrange("p a b -> p (a b)"), vcar[bi, hp], reads=[f"vcar{bi}_{hp}"], writes=["vpp"])
            for q4 in range(4):
                ptile, pname = ((pst[:, 0:512], "pst") if q4 % 2 == 0 else (psb[6][:].bitcast(BF16)[:, 0:512], "ps6"))

                def trs(e, q4=q4, ptile=ptile):
                    ins = None
                    for b4 in range(4):
                        blk = q4 * 4 + b4
                        r, n = blk // nbl, blk % nbl
                        ins = e.transpose(ptile[:, b4 * 128:(b4 + 1) * 128], vT[:, cols(r, n)], ident[:])
                    return ins
                P.op("pe", trs, reads=["vT", "ident"], writes=[pname])
                pv3 = ptile.rearrange("p (a b) -> p a b", a=4)
                P.op("act", lambda e, pv3=pv3, q4=q4: e.activation(out=vpc[:, q4 * 4:q4 * 4 + 4, 0:64], in_=pv3[:, :, 0:64], func=AF.Copy),
                     reads=[pname], writes=[vpn])
                P.op("act", lambda e, pv3=pv3, q4=q4: e.activation(out=vpc[:, q4 * 4:q4 * 4 + 4, 128:192], in_=pv3[:, :, 64:128], func=AF.Copy),
                     reads=[pname], writes=[vpn])
            if half == 0:
                P.dma("sp", vcar[bi, hp], vpc[:].rearrange("p a b -> p (a b)"), reads=[vpn], writes=[f"vcar{bi}_{hp}"])
            info = {}

            def s1(blk):
                r, n = blk // nbl, blk % nbl
                cq = cols(r, n)
                has_prev = (n > 0) or (half == 1)
                s = blk % 3
                pi = next_ps()
                vprev = None
                if has_prev:
                    if n > 0:
                        kpt, kc_, vprev, kpn = kn, cols(r, n - 1), (vpc, blk - 1, vpn), "kn"
                    else:
                        kpt, kc_, vprev, kpn = knp, cols(r, nbl - 1), (vpp, r * nbl + nbl - 1, "vpp"), "knp"
                    mm(psb[pi][:, 0:256], [(kpt[:, kc_], qn[:, :, cq])], reads=[kpn, "qn"], writes=[f"ps{pi}"])
                    mm(psb[pi][:, 256:512], [(kn[:, cq], qn[:, :, cq])], reads=["kn", "qn"], writes=[f"ps{pi}"], start=False)
                    mm(psb[pi][:], [(ident[:], maskb4[:])], reads=["ident", "maskb4"], writes=[f"ps{pi}"], start=False)
                    P.op("act", lambda e: e.activation(out=pT[s][:], in_=psb[pi][:], func=AF.Exp, scale=0.125),
                         reads=[f"ps{pi}"], writes=[f"pT{s}"])
                else:
                    mm(psb[pi][:, 256:512], [(kn[:, cq], qn[:, :, cq]), (ident[:], maskb4[:, 256:512])],
                       reads=["kn", "qn", "ident", "maskb4"], writes=[f"ps{pi}"])
                    P.op("act", lambda e: e.activation(out=pT[s][:, 256:512], in_=psb[pi][:, 256:512], func=AF.Exp, scale=0.125),
                         reads=[f"ps{pi}"], writes=[f"pT{s}"])
                info[blk] = (cq, has_prev, s, vprev)

            def s2(blk):
                cq, has_prev, s, vprev = info[blk]
                pn = next_ps()
                first = True
                for hh in range(2):
                    vs = slice(hh * 64, hh * 64 + 128)
                    prs, rd = [], [vpn, f"pT{s}"]
                    if has_prev:
                        prs.append((vprev[0][:, vprev[1], vs], pT[s][:, hh * 128:hh * 128 + 128]))
                        rd.append(vprev[2])
                    prs.append((vpc[:, blk, vs], pT[s][:, 256 + hh * 128:256 + hh * 128 + 128]))
                    mm(psb[pn][:, hh * 128:(hh + 1) * 128], prs, reads=rd, writes=[f"ps{pn}"], start=first)
                    first = False
                av = acc[:, :, cq]
                sv = psb[pn][:, 0:256].rearrange("p (a b) -> p a b", a=2)
                wr = [f"acc{bi}_{blk}"] + (["accdone"] if blk == 15 else [])
                if bi == 0:
                    P.op("act", lambda e: e.activation(out=av, in_=sv, func=AF.Copy), reads=[f"ps{pn}"], writes=wr)
                else:
                    P.op("dve", lambda e: e.tensor_tensor(out=av, in0=av, in1=sv, op=ALU.add), reads=[f"ps{pn}", "accdone"], writes=wr)
            SK = 2
            for i in range(16 + SK):
                if i < 16:
                    s1(i)
                if i >= SK:
                    s2(i - SK)

        def even(ei, half):
            wv = ev_w_in[ei].rearrange("(kc p) n -> p kc n", p=128)
            with contextlib.ExitStack() as st:
                cat = sb(st, "cat", [128, 8, T], BF16)
                with contextlib.ExitStack() as st2:
                    wva = sb(st2, "wva", [128, 8, 512], BF16)
                    wu = [sb(st2, "wua", [128, 8, 128], BF16) for _ in range(2)]
                    wsT = sb(st2, "wsT", [128, 8, 128], BF16)
                    wsf = sb(st2, "wsf", [128, 8, 128], F32)
                    gv = [sb(st2, "gv", [128, 512], F32) for _ in range(3)]
                    vn = [sb(st2, "vn", [128, 512], BF16) for _ in range(3)]
                    s1 = [sb(st2, "s1", [128, 6], F32) for _ in range(3)]
                    s2 = [sb(st2, "s2", [128, 2], F32) for _ in range(3)]
                    s3 = [sb(st2, "s3", [128, 1], F32) for _ in range(3)]
                    tmp = [sb(st2, "tmp", [128, 128], F32) for _ in range(4)]
                    mhalf = sb(st2, "mhalf", [128, 1], F32)
                    lpa = sb(st2, "lpa", [128, 520], F32)
                    P.dma("sp", lpa[:, 0:512], lp[2 * ei][:, L_BIAS:L_BIAS + 512], writes=["lpa"])
                    P.dma("sp", lpa[:, 512:520], lp[2 * ei][:, L_LNGF:L_LNGF + 8], writes=["lpa"])
                    b2t = sb(st2, "b2t", [128, 4, 128], F32)
                    s4 = [sb(st2, "s4", [128, 1], F32) for _ in range(3)]
                    P.op("pool", lambda e: e.memset(mhalf[:], -0.5), writes=["mhalf"])
                    P.dma("sp", wsf[:], ev_wsT[ei], writes=["wsf"])
                    P.op("dve", lambda e: e.tensor_tensor(out=wsT[:], in0=wsf[:],
                                                        in1=cst[:, 0:128].unsqueeze(1).to_broadcast([128, 8, 128]),
                                                        op=ALU.mult), reads=["wsf", "cst"], writes=["wsT"])
                    for g in range(8):
                        cp_, gg_ = g // 2, g % 2
                        pp_ = slice(gg_ * 64, (gg_ + 1) * 64)
                        pi = next_ps()
                        mm(psb[pi][:, 0:128], [(ones[:], wsT[:, g, :])], reads=["ones", "wsT"], writes=[f"ps{pi}"])
                        P.op("dve", lambda e, pi=pi, pp_=pp_, cp_=cp_: e.scalar_tensor_tensor(
                            out=b2t[pp_, cp_, :], in0=psb[pi][pp_, 0:128], scalar=lpa[pp_, 516 + cp_:517 + cp_],
                            in1=lpa[pp_, cp_ * 128:(cp_ + 1) * 128], op0=ALU.mult, op1=ALU.add),
                            reads=[f"ps{pi}", "lpa"], writes=["b2t"])
                    P.dma("pool", wva[:], wv[:, :, 512:1024], writes=["wva"])
                    for uc in range(4):
                        b = uc % 2
                        P.dma("pool", wu[b][:], wv[:, :, uc * 128:(uc + 1) * 128], writes=[f"wua{b}"])
                        for tt in range(NT):
                            ts = slice(tt * 512, (tt + 1) * 512)
                            pi = next_ps()
                            mm(psb[pi][:], [(wu[b][:, kc, :], h[:, kc, ts]) for kc in range(8)], reads=[f"wua{b}", "h"], writes=[f"ps{pi}"])
                            P.op("act", lambda e, pi=pi, uc=uc, ts=ts: e.activation(out=cat[:, uc, ts], in_=psb[pi][:], func=AF.Gelu),
                                 reads=[f"ps{pi}"], writes=[f"srcu{uc}_{tt}"])

                    def a_s1(tc_):
                        b = tc_ % 3
                        tsl = slice(tc_ * 128, (tc_ + 1) * 128)
                        pi = next_ps()
                        mm(psb[pi][:], [(h[:, kc, tsl], wva[:, kc, :]) for kc in range(8)], reads=["wva", "h"], writes=[f"ps{pi}"])
                        P.op("act", lambda e: e.activation(out=gv[b][:], in_=psb[pi][:], func=AF.Gelu), reads=[f"ps{pi}"], writes=[f"gv{b}"])
                        yield
                        P.op("dve", lambda e: e.bn_stats(out=s1[b][:], in_=gv[b][:]), reads=[f"gv{b}"], writes=[f"s1{b}"])
                        yield
                        P.op("dve", lambda e: e.bn_aggr(out=s2[b][:], in_=s1[b][:]), reads=[f"s1{b}"], writes=[f"s2{b}"])
                        yield
                        P.op("pool", lambda e: e.tensor_scalar(out=s3[b][:], in0=s2[b][:, 1:2], scalar1=EPS, scalar2=None, op0=ALU.add),
                             reads=[f"s2{b}"], writes=[f"s3{b}"])
                        P.op("pool", lambda e: e.tensor_tensor(out=s3[b][:], in0=s3[b][:], in1=mhalf[:], op=ALU.pow),
                             reads=[f"s3{b}", "mhalf"], writes=[f"s3{b}"])
                        yield
                        yield
                        P.op("dve", lambda e: e.scalar_tensor_tensor(out=s4[b][:], in0=s2[b][:, 0:1], scalar=-1.0, in1=s3[b][:],
                                                                   op0=ALU.mult, op1=ALU.mult), reads=[f"s2{b}", f"s3{b}"], writes=[f"s4{b}"])
                        yield
                        P.op("act", lambda e: e.activation(out=vn[b][:], in_=gv[b][:], func=AF.Identity, scale=s3[b][:, 0:1], bias=s4[b][:, 0:1]),
                             reads=[f"gv{b}", f"s3{b}", f"s4{b}"], writes=[f"vn{b}"])
                        yield

                    def a_adv(g):
                        if g is not None:
                            try:
                                next(g)
                            except StopIteration:
                                pass

                    def a_s2(tc_, g, g2):
                        b = tc_ % 3
                        tsl = slice(tc_ * 128, (tc_ + 1) * 128)
                        k = 0
                        for cp in range(4):
                            for gg in range(2):
                                gi = cp * 2 + gg
                                pp = slice(gg * 64, (gg + 1) * 64)
                                pi = next_ps()
                                tb = k % 4
                                k += 1
                                mm(psb[pi][:, 0:128], [(vn[b][:, cp * 128:(cp + 1) * 128], wsT[:, gi, :])], reads=[f"vn{b}", "wsT"], writes=[f"ps{pi}"])
                                P.op("dve", lambda e, pi=pi, pp=pp, cp=cp, tb=tb: e.scalar_tensor_tensor(
                                    out=tmp[tb][pp, :], in0=psb[pi][pp, 0:128], scalar=lpa[pp, 512 + cp:513 + cp], in1=b2t[pp, cp, :],
                                    op0=ALU.mult, op1=ALU.add), reads=[f"ps{pi}", "lpa", "b2t"], writes=[f"tmp{tb}"])
                                P.op("pool", lambda e, pp=pp, cp=cp, tsl=tsl, tb=tb: e.tensor_tensor(
                                    out=cat[pp, cp, tsl], in0=tmp[tb][pp, :], in1=cat[pp, cp, tsl], op=ALU.mult),
                                    reads=[f"tmp{tb}", f"srcu{cp}_{tc_ // 4}"], writes=[f"srca{cp}_{gg}_{tc_}"])
                                a_adv(g)
                                a_adv(g2)
                        for _ in range(8):
                            a_adv(g)
                    gens = [a_s1(i) for i in range(16)] + [None, None]
                    for _ in range(8):
                        a_adv(gens[0])
                    for i in range(16):
                        a_s2(i, gens[i + 1], gens[i + 2])
                    barrier()
                with contextlib.ExitStack() as st2:
                    wq = sb(st2, "wq", [128, 8, 128], BF16)
                    wk = sb(st2, "wk", [128, 8, 128], BF16)
                    wvb = sb(st2, "wvb", [128, 8, 128], BF16)
                    qn = sb(st2, "qz", [128, 2, T], BF16)
                    kn = sb(st2, "kn", [128, T], BF16)
                    P.op("pool", lambda e: e.memset(qn[:], 0.0), writes=["qn"])
                    knp = sb(st2, "knp", [128, T], BF16)
                    vT = sb(st2, "vT", [128, T], BF16)
                    vp = [sb(st2, "vp", [128, 16, 192], BF16) for _ in range(2)]
                    vpp = sb(st2, "vpp", [128, 16, 192], BF16)
                    acc = sb(st2, "acc", [128, 2, T], F32)
                    for vv in vp:
                        P.op("pool", lambda e, vv=vv: e.memset(vv[:, :, 64:128], 1.0), writes=["vp0", "vp1"])
                    sq = [sb(st2, "sqb", [128, 512], BF16) for _ in range(2)]
                    rt = [sb(st2, "rtb", [128, 512], F32) for _ in range(2)]
                    pT = [sb(st2, "pT", [128, 512], BF16) for _ in range(3)]
                    vctr = [0]
                    for hp in range(4):
                        P.dma("pool", wq[:], wv[:, :, 1024 + hp * 128:1024 + (hp + 1) * 128], writes=["wq"])
                        P.dma("pool", wk[:], wv[:, :, 1536 + hp * 128:1536 + (hp + 1) * 128], writes=["wk"])
                        P.dma("pool", wvb[:], wv[:, :, 2048 + hp * 128:2048 + (hp + 1) * 128], writes=["wvb"])
                        if half == 1:
                            P.dma("sp", knp[:], kcar[hp], reads=[f"kcar{hp}"], writes=["knp"])
                        jobs = [(wt_, wn, gcol, dst, dn, tt) for (wt_, wn, gcol, dst, dn) in
                                ((wq, "wq", L_QG, qn, "qn"), (wk, "wk", L_KG, kn, "kn")) for tt in range(NT)]
                        jps = {}

                        def n_s1(j):
                            wt_, wn, gcol, dst, dn, tt = jobs[j]
                            b = j % 2
                            ts = slice(tt * 512, (tt + 1) * 512)
                            pi = next_ps()
                            jps[j] = pi
                            mm(psb[pi][:], [(wt_[:, kc, :], h[:, kc, ts]) for kc in range(8)], reads=[wn, "h"], writes=[f"ps{pi}"])
                            P.op("act", lambda e: e.activation(out=sq[b][:], in_=psb[pi][:], func=AF.Square), reads=[f"ps{pi}"], writes=[f"sqb{b}"])

                        def n_s2(j):
                            wt_, wn, gcol, dst, dn, tt = jobs[j]
                            b = j % 2
                            ts = slice(tt * 512, (tt + 1) * 512)
                            pi = jps[j]
                            p2 = 6
                            mm(psb[p2][:], [(bones[:], sq[b][:])], reads=["bones", f"sqb{b}"], writes=[f"ps{p2}"])
                            P.op("act", lambda e: e.activation(out=rt[b][:], in_=psb[p2][:], func=AF.Ln, bias=EPS, scale=1.0 / 64),
                                 reads=[f"ps{p2}"], writes=[f"rtb{b}"])
                            P.op("act", lambda e: e.activation(out=rt[b][:], in_=rt[b][:], func=AF.Exp, scale=-0.5), reads=[f"rtb{b}"], writes=[f"rtb{b}"])
                            if dn == "qn":
                                for hh in range(2):
                                    hs_ = slice(hh * 64, (hh + 1) * 64)
                                    P.op("dve", lambda e, hh=hh, hs_=hs_: e.scalar_tensor_tensor(
                                        out=dst[hs_, hh, ts], in0=psb[pi][hs_, :], scalar=lpt[hs_, gcol:gcol + 1], in1=rt[b][hs_, :],
                                        op0=ALU.mult, op1=ALU.mult), reads=[f"ps{pi}", f"rtb{b}", "lpt"], writes=[dn])
                            else:
                                P.op("dve", lambda e: e.scalar_tensor_tensor(
                                    out=dst[:, ts], in0=psb[pi][:], scalar=lpt[:, gcol:gcol + 1], in1=rt[b][:], op0=ALU.mult, op1=ALU.mult),
                                    reads=[f"ps{pi}", f"rtb{b}", "lpt"], writes=[dn])
                        for j in range(len(jobs) + 1):
                            if j < len(jobs):
                                n_s1(j)
                            if j >= 1:
                                n_s2(j - 1)
                        if half == 0:
                            P.dma("sp", kcar[hp], kn[:], reads=["kn"], writes=[f"kcar{hp}"])
                        for tt in range(NT):
                            ts = slice(tt * 512, (tt + 1) * 512)
                            pi = next_ps()
                            mm(psb[pi][:], [(wvb[:, kc, :], h[:, kc, ts]) for kc in range(8)], reads=["wvb", "h"], writes=[f"ps{pi}"])
                            P.op("act", lambda e, pi=pi, ts=ts: e.activation(out=vT[:, ts], in_=psb[pi][:], func=AF.Copy),
                                 reads=[f"ps{pi}"], writes=["vT"])
                        for bi, d in enumerate((1, 4, 16)):
                            attn_branch(hp, bi, d, half, vT, qn, kn, knp, vp, vpp, acc, pT, vctr)
                        k_ = 0
                        for hh in range(2):
                            nr = slice(hh * 64, (hh + 1) * 64)
                            dr = slice((1 - hh) * 64, (2 - hh) * 64)
                            for tt in range(NT):
                                ts = slice(tt * 512, (tt + 1) * 512)
                                b = k_ % 2
                                k_ += 1
                                P.op("act", lambda e, b=b, hh=hh, nr=nr, dr=dr, ts=ts: e.activation(out=rt[b][nr, :], in_=acc[dr, hh, ts], func=AF.Ln),
                                     reads=["accdone"], writes=[f"rtb{b}"])
                                P.op("act", lambda e, b=b, nr=nr: e.activation(out=rt[b][nr, :], in_=rt[b][nr, :], func=AF.Exp, scale=-1.0),
                                     reads=[f"rtb{b}"], writes=[f"rtb{b}"])
                                P.op("dve", lambda e, b=b, hh=hh, nr=nr, ts=ts, hp=hp: e.tensor_tensor(
                                    out=cat[nr, 4 + hp, ts], in0=acc[nr, hh, ts], in1=rt[b][nr, :], op=ALU.mult),
                                    reads=["accdone", f"rtb{b}"], writes=["src"])
                    barrier()

                out_proj(ev_w_out[ei], cat, norm_gcol=(L_GFFN if fuse_ffn_norm else None))

        steps = [(li, layer, half) for li, layer in enumerate(layers) for half in halves]
        xall = lambda c: [xr(c, t_) for t_ in range(NT)]

        def xsrc(si):
            li, layer, half = steps[si]
            return (xT if li == 0 else xs)[half], half

        src0, h0 = xsrc(0)
        for c in range(8):
            P.dma("sp", x[:, c, :], src0[c * 128:(c + 1) * 128, :], reads=[f"xs{h0}_{c}"], writes=xall(c))
        for si, (li, layer, half) in enumerate(steps):
            if half == halves[0]:
                P.dma("sp", lpt[:], lp[layer][:, 0:128], writes=["lpt"])
            last = li == len(layers) - 1
            dst = (outT if last else xs)[half]

            def fin(c, si=si, dst=dst, last=last, half=half):
                P.dma("sp", dst[c * 128:(c + 1) * 128, :], x[:, c, :], reads=xall(c), writes=[f"xs{half}_{c}"], final=last)
                if si + 1 < len(steps):
                    nsrc, nh = xsrc(si + 1)
                    P.dma("sp", x[:, c, :], nsrc[c * 128:(c + 1) * 128, :], reads=[f"xs{nh}_{c}"], writes=xall(c))
            if "mix" in phases:
                rms_norm(L_GMIX)
                if layer % 2 == 0:
                    even(layer // 2, half)
                else:
                    gla(layer // 2, half)
            if "ffn" in phases:
                ffn(layer, half, fin, normed=fuse_ffn_norm)
            else:
                for c in range(8):
                    fin(c)
                barrier()
        P.emit()
    return nc


def _consts():
    c = np.zeros((128, C_N), np.float32)
    p = np.arange(128)[:, None]
    i = np.arange(128)[None, :]
    c[:, C_ID:C_ID + 128] = (p == i)
    c[:, C_BONES:C_BONES + 128] = (p // 64 == i // 64)
    c[:, C_MPREV:C_MPREV + 128] = np.where(i <= p, 0.0, NEGM)
    c[:, C_MCUR:C_MCUR + 128] = np.where(p <= i, 0.0, NEGM)
    c[:, C_TRIL:C_TRIL + 128] = (p <= i)
    c[:, C_M2:C_M2 + 128] = (p <= i) & (p // 64 == i // 64)
    c[:, C_RESET:C_RESET + 512] = (np.arange(512)[None, :] % 64 != 0)
    return c


def _fm(v, n):
    return np.ascontiguousarray(v.reshape(n, 128).T)


def _layer_pack(inp):
    lp = np.zeros((4, 128, L_N), np.float32)
    for l in range(4):
        lp[l, :, L_GMIX:L_GMIX + 8] = _fm(inp["norm_mix_g"][l], 8)
        lp[l, :, L_GFFN:L_GFFN + 8] = _fm(inp["norm_ffn_g"][l], 8)
        for k, off in enumerate((L_CW0, L_CW1, L_CW2)):
            lp[l, :, off:off + NFC] = _fm(inp["ffn_conv_w"][l, k], NFC)
        lp[l, :, L_CB:L_CB + NFC] = _fm(inp["ffn_conv_b"][l], NFC)
        if l % 2 == 0:
            e = l // 2
            lp[l, :, L_QG] = np.tile(inp["ev_q_g"][e], 2)
            lp[l, :, L_KG] = np.tile(inp["ev_k_g"][e], 2)
            bs = inp["ev_a_bs"][e]
            for cp in range(4):
                lp[l, 0:64, L_BIAS + cp * 128:L_BIAS + (cp + 1) * 128] = bs[2 * cp][None, :]
                lp[l, 64:128, L_BIAS + cp * 128:L_BIAS + (cp + 1) * 128] = bs[2 * cp + 1][None, :]
            lp[l, :, L_LNG:L_LNG + 512] = inp["ev_a_ln_g"][e][None, :]
            lp[l, :, L_LNB:L_LNB + 512] = inp["ev_a_ln_b"][e][None, :]
            lp[l, :, L_LNGF:L_LNGF + 4] = _fm(inp["ev_a_ln_g"][e], 4)
            lp[l, :, L_LNBF:L_LNBF + 4] = _fm(inp["ev_a_ln_b"][e], 4)
        else:
            o = l // 2
            lp[l, :, L_BA:L_BA + 4] = _fm(inp["od_b_a"][o], 4)
            lp[l, :, L_HG:L_HG + 2] = _fm(inp["od_head_g"][o], 2)
    return lp


def make_in_maps(inp, seqs):
    consts = _consts()
    lp = _layer_pack(inp)
    wsT = np.ascontiguousarray(np.transpose(inp["ev_a_ws"], (0, 3, 1, 2)))
    shared = {
        "consts": consts, "lp": lp,
        "ev_w_in": np.ascontiguousarray(inp["ev_w_in"]), "ev_wsT": wsT,
        "ev_w_out": np.ascontiguousarray(inp["ev_w_out"]),
        "od_w_in": np.ascontiguousarray(inp["od_w_in"]), "od_w_a2": np.ascontiguousarray(inp["od_w_a2"]),
        "od_w_out": np.ascontiguousarray(inp["od_w_out"]),
        "ffn_w_gate": np.ascontiguousarray(inp["ffn_w_gate"]), "ffn_w_up": np.ascontiguousarray(inp["ffn_w_up"]),
        "ffn_w_down": np.ascontiguousarray(inp["ffn_w_down"]),
    }
    maps = []
    for b in seqs:
        xb = np.asarray(inp["x"][b], np.float32)
        xTt = np.ascontiguousarray(xb.reshape(2, T, 1024).transpose(0, 2, 1))
        m = dict(shared)
        m["xT"] = xTt
        maps.append(m)
    return maps


_NC = {}


def kernel(**inputs):
    inp = {k: np.asarray(v) for k, v in inputs.items()}
    if "full" not in _NC:
        _NC["full"] = build()
    nc = _NC["full"]
    maps = make_in_maps(inp, range(4))
    res = run_bass_kernel_spmd(nc, maps, core_ids=list(range(4)))
    out = np.empty((4, 2 * T, 1024), np.float32)
    for b in range(4):
        o = res.results[b]["outT"]
        out[b] = o.transpose(0, 2, 1).reshape(2 * T, 1024)
    return out
```
